# Optimizing a Trainium2 kernel written in Bass

```python
import math
import jax, jax.numpy as jnp
from jax import lax
import numpy as np

D_MODEL = 2048
BATCH = 2
SEQ = 16384
DEPTH = 1

HEAD_DIM = 128
A_Q_HEADS = 8
A_KV_HEADS = 2
A_REP = A_Q_HEADS // A_KV_HEADS
WINDOW = 128
BLOCK = 128
B_HEADS = 4
D_FF = 5632
ROPE_THETA = 10000.0
EPS = 1e-6
NEG_INF = -1e30

A_Q = A_Q_HEADS * HEAD_DIM
A_KV = A_KV_HEADS * HEAD_DIM
A_WIDTH = A_Q
B_QK = B_HEADS * 2 * HEAD_DIM
B_V = B_HEADS * 2 * HEAD_DIM
B_WIDTH = B_V
GATE_W = 2 * D_MODEL
W_IN_COLS = A_Q + 2 * A_KV + 2 * B_QK + B_V + GATE_W
SPLITS = tuple(np.cumsum([A_Q, A_KV, A_KV, B_QK, B_QK, B_V, D_MODEL])[:].tolist())

kernel_name = "hybrid_gated_window_gqa_diff_attn_macaron"


def rmsnorm(x, g):
    xf = x.astype(jnp.float32)
    y = xf * lax.rsqrt(jnp.mean(xf * xf, axis=-1, keepdims=True) + EPS)
    return (y * g.astype(jnp.float32)).astype(x.dtype)


def swiglu(h, w_gate, w_up, w_down):
    return (jax.nn.silu(h @ w_gate) * (h @ w_up)) @ w_down


def rope_tables(seq, dim):
    pos = jnp.arange(seq, dtype=jnp.float32)
    inv = ROPE_THETA ** (-jnp.arange(0, dim, 2, dtype=jnp.float32) / dim)
    ang = pos[:, None] * inv[None, :]
    return jnp.cos(ang), jnp.sin(ang)


def apply_rope(x, cos, sin):
    shape = (1, x.shape[1]) + (1,) * (x.ndim - 3) + (x.shape[-1] // 2,)
    c, s = cos.reshape(shape), sin.reshape(shape)
    xf = x.astype(jnp.float32)
    x1, x2 = jnp.split(xf, 2, axis=-1)
    return jnp.concatenate([x1 * c - x2 * s, x2 * c + x1 * s], axis=-1).astype(x.dtype)


def windowed_gqa_sink(q, k, v, sink):
    b, s = q.shape[0], q.shape[1]
    nb = s // BLOCK
    qb = q.reshape(b, nb, BLOCK, A_KV_HEADS, A_REP, HEAD_DIM)

    def band(t):
        tp = jnp.pad(t, ((0, 0), (BLOCK, BLOCK), (0, 0), (0, 0)))
        tb = tp.reshape(b, nb + 2, BLOCK, A_KV_HEADS, HEAD_DIM)
        return jnp.concatenate([tb[:, :-2], tb[:, 1:-1], tb[:, 2:]], axis=2)

    kband, vband = band(k), band(v)
    scores = jnp.einsum('bnqgrd,bnkgd->bngrqk', qb, kband,
                        preferred_element_type=jnp.float32) * (HEAD_DIM ** -0.5)
    blk = jnp.arange(nb)[:, None, None] * BLOCK
    qpos = blk + jnp.arange(BLOCK)[None, :, None]
    kpos = blk + jnp.arange(3 * BLOCK)[None, None, :] - BLOCK
    valid = (jnp.abs(kpos - qpos) <= WINDOW) & (kpos >= 0) & (kpos < s)
    scores = jnp.where(valid[None, :, None, None], scores, NEG_INF)
    sk = sink.astype(jnp.float32).reshape(1, 1, A_KV_HEADS, A_REP, 1, 1)
    m = jnp.maximum(jnp.max(scores, axis=-1, keepdims=True), sk)
    e = jnp.exp(scores - m)
    p = e / (jnp.sum(e, axis=-1, keepdims=True) + jnp.exp(sk - m))
    out = jnp.einsum('bngrqk,bnkgd->bnqgrd', p.astype(v.dtype), vband)
    return out.reshape(b, s, A_WIDTH)


def differential_attention(q, k, v, lam, subln_g, lam_init):
    b, s = q.shape[0], q.shape[1]
    nb = s // BLOCK
    qblocks = jnp.moveaxis(q.reshape(b, nb, BLOCK, B_HEADS, 2, HEAD_DIM), 1, 0)

    def one_block(qblk):
        sc = jnp.einsum('bqhcd,bkhcd->bhcqk', qblk, k,
                        preferred_element_type=jnp.float32) * (HEAD_DIM ** -0.5)
        p = jax.nn.softmax(sc, axis=-1)
        attn = p[:, :, 0] - lam * p[:, :, 1]
        return jnp.einsum('bhqk,bkhe->bqhe', attn.astype(v.dtype), v)

    out = lax.map(one_block, qblocks)
    out = jnp.moveaxis(out, 0, 1).reshape(b, s, B_HEADS, 2 * HEAD_DIM)
    out = rmsnorm(out, subln_g) * (1.0 - lam_init)
    return out.reshape(b, s, B_WIDTH)


def setup_inputs(seed: int = 0) -> dict:
    key = jax.random.key(seed)
    ks = jax.random.split(key, 24)
    L = DEPTH

    def nrm(k, shape, fan_in):
        return jax.random.normal(k, shape, jnp.float32) * (fan_in ** -0.5)

    def gain(k, dim):
        return 1.0 + 0.05 * jax.random.normal(k, (L, dim), jnp.float32)

    return {
        "x": jax.random.normal(ks[0], (BATCH, SEQ, D_MODEL), jnp.float32),
        "ffn1_pre_g": gain(ks[1], D_MODEL),
        "ffn1_w_gate": nrm(ks[2], (L, D_MODEL, D_FF), D_MODEL),
        "ffn1_w_up": nrm(ks[3], (L, D_MODEL, D_FF), D_MODEL),
        "ffn1_w_down": nrm(ks[4], (L, D_FF, D_MODEL), D_FF),
        "ffn1_post_g": gain(ks[5], D_MODEL),
        "mix_pre_g": gain(ks[6], D_MODEL),
        "w_in": nrm(ks[7], (L, D_MODEL, W_IN_COLS), D_MODEL),
        "gate_bias": 0.01 * jax.random.normal(ks[8], (L, GATE_W), jnp.float32),
        "sink_logit": 0.5 * jax.random.normal(ks[9], (L, A_Q_HEADS), jnp.float32),
        "lambda_q1": 0.1 * jax.random.normal(ks[10], (L, HEAD_DIM), jnp.float32),
        "lambda_k1": 0.1 * jax.random.normal(ks[11], (L, HEAD_DIM), jnp.float32),
        "lambda_q2": 0.1 * jax.random.normal(ks[12], (L, HEAD_DIM), jnp.float32),
        "lambda_k2": 0.1 * jax.random.normal(ks[13], (L, HEAD_DIM), jnp.float32),
        "subln_g": gain(ks[14], 2 * HEAD_DIM),
        "w_proj_a": nrm(ks[15], (L, A_WIDTH, D_MODEL), A_WIDTH),
        "w_proj_b": nrm(ks[16], (L, B_WIDTH, D_MODEL), B_WIDTH),
        "w_out": nrm(ks[17], (L, D_MODEL, D_MODEL), D_MODEL),
        "mix_post_g": gain(ks[18], D_MODEL),
        "ffn2_pre_g": gain(ks[19], D_MODEL),
        "ffn2_w_gate": nrm(ks[20], (L, D_MODEL, D_FF), D_MODEL),
        "ffn2_w_up": nrm(ks[21], (L, D_MODEL, D_FF), D_MODEL),
        "ffn2_w_down": nrm(ks[22], (L, D_FF, D_MODEL), D_FF),
        "ffn2_post_g": gain(ks[23], D_MODEL),
    }


def reference(x, ffn1_pre_g, ffn1_w_gate, ffn1_w_up, ffn1_w_down, ffn1_post_g,
              mix_pre_g, w_in, gate_bias, sink_logit, lambda_q1, lambda_k1, lambda_q2,
              lambda_k2, subln_g, w_proj_a, w_proj_b, w_out, mix_post_g,
              ffn2_pre_g, ffn2_w_gate, ffn2_w_up, ffn2_w_down, ffn2_post_g):
    b, s = x.shape[0], x.shape[1]
    cos, sin = rope_tables(s, HEAD_DIM)
    for l in range(DEPTH):
        lam_init = 0.8 - 0.6 * math.exp(-0.3 * l)
        f = swiglu(rmsnorm(x, ffn1_pre_g[l]), ffn1_w_gate[l], ffn1_w_up[l], ffn1_w_down[l])
        x = x + 0.5 * rmsnorm(f, ffn1_post_g[l])

        h = rmsnorm(x, mix_pre_g[l])
        proj = h @ w_in[l]
        qa, ka, va, qb, kb, vb, ga, gb = jnp.split(proj, SPLITS, axis=-1)
        ga = jax.nn.sigmoid(ga + gate_bias[l, :D_MODEL])
        gb = jax.nn.sigmoid(gb + gate_bias[l, D_MODEL:])

        qa = apply_rope(qa.reshape(b, s, A_Q_HEADS, HEAD_DIM), cos, sin)
        ka = apply_rope(ka.reshape(b, s, A_KV_HEADS, HEAD_DIM), cos, sin)
        va = va.reshape(b, s, A_KV_HEADS, HEAD_DIM)
        out_a = windowed_gqa_sink(qa, ka, va, sink_logit[l])

        qb = apply_rope(qb.reshape(b, s, B_HEADS, 2, HEAD_DIM), cos, sin)
        kb = apply_rope(kb.reshape(b, s, B_HEADS, 2, HEAD_DIM), cos, sin)
        vb = vb.reshape(b, s, B_HEADS, 2 * HEAD_DIM)
        lam = (jnp.exp(jnp.sum(lambda_q1[l].astype(jnp.float32) * lambda_k1[l].astype(jnp.float32)))
               - jnp.exp(jnp.sum(lambda_q2[l].astype(jnp.float32) * lambda_k2[l].astype(jnp.float32)))
               + lam_init)
        out_b = differential_attention(qb, kb, vb, lam, subln_g[l], lam_init)

        merged = ga * (out_a @ w_proj_a[l]) + gb * (out_b @ w_proj_b[l])
        x = x + rmsnorm(merged @ w_out[l], mix_post_g[l])

        f = swiglu(rmsnorm(x, ffn2_pre_g[l]), ffn2_w_gate[l], ffn2_w_up[l], ffn2_w_down[l])
        x = x + 0.5 * rmsnorm(f, ffn2_post_g[l])
    return x
```

```python
import math
from contextlib import ExitStack

import numpy as np
import ml_dtypes
import concourse.bass as bass
import concourse.mybir as mybir
from concourse.bass_utils import run_bass_kernel_spmd

F32 = mybir.dt.float32
BF16 = mybir.dt.bfloat16
AF = mybir.ActivationFunctionType
ALU = mybir.AluOpType

NCORES = 8
NR = 4
D = 2048
DFF = 5632
HD = 128
WIN_COLS = 8704
EPS = 1e-6
LAM_INIT = 0.8 - 0.6 * math.exp(-0.3 * 0)
T = 512
KC = D // 128
JF = DFF // 128
SCALE = HD ** -0.5

ENGINES = ("tensor", "vector", "scalar", "gpsimd", "sync")
N_DMA_SEMS = 12


class _Op:
    __slots__ = ("fn", "waits", "semkey", "incval")

    def __init__(self, fn, waits, semkey, incval):
        self.fn = fn
        self.waits = waits
        self.semkey = semkey
        self.incval = incval


class Prog:
    def __init__(self, nc):
        self.nc = nc
        self.ops = {e: [] for e in ENGINES}
        self.cnt = {e: 0 for e in ENGINES}
        self.dma_rr = {"sync": 0, "gpsimd": 0}
        self.dma_cnt = {}
        self.last_w = {}
        self.readers = {}
        self.known = {e: {} for e in ENGINES}
        self.semkeys = [e for e in ENGINES if e != "sync"] + ["cc"]
        for q in ("sync", "gpsimd"):
            for i in range(N_DMA_SEMS):
                self.semkeys.append(("dma", q, i))
                self.dma_cnt[("dma", q, i)] = 0

    def _deps(self, eng, reads, writes):
        need = {}

        def add(tok):
            if tok is None:
                return
            k, v = tok
            if eng == "tensor" and k == "tensor":
                return
            if need.get(k, 0) < v:
                need[k] = v

        for r in reads:
            add(self.last_w.get(r))
        for w in writes:
            add(self.last_w.get(w))
            for t in self.readers.get(w, ()):
                add(t)
        return need

    def _commit(self, tok, reads, writes):
        for w in writes:
            self.last_w[w] = tok
            self.readers[w] = []
        for r in reads:
            self.readers.setdefault(r, []).append(tok)

    def _filter(self, eng, need):
        kn = self.known[eng]
        out = []
        for k, v in need.items():
            if kn.get(k, 0) < v:
                kn[k] = v
                out.append((k, v))
        return out

    def op(self, eng, fn, reads=(), writes=()):
        reads = tuple(reads)
        writes = tuple(writes)
        waits = self._filter(eng, self._deps(eng, reads, writes))
        self.cnt[eng] += 1
        tok = (eng, self.cnt[eng])
        self.ops[eng].append(_Op(fn, waits, eng, 1))
        self._commit(tok, reads, writes)
        return tok

    def dma(self, q, fn, reads=(), writes=()):
        reads = tuple(reads)
        writes = tuple(writes)
        need = self._deps(q, reads, writes)
        i = self.dma_rr[q]
        self.dma_rr[q] = (i + 1) % N_DMA_SEMS
        sk = ("dma", q, i)
        prev = self.dma_cnt[sk]
        if prev and need.get(sk, 0) < 16 * prev:
            need[sk] = 16 * prev
        waits = self._filter(q, need)
        self.dma_cnt[sk] = prev + 1
        tok = (sk, 16 * (prev + 1))
        self.ops[q].append(_Op(fn, waits, sk, 16))
        self._commit(tok, reads, writes)
        return tok

    def cc(self, fn, reads, writes, total):
        reads = tuple(reads)
        writes = tuple(writes)
        waits = self._filter("gpsimd", self._deps("gpsimd", reads, writes))
        self.ops["gpsimd"].append(_Op(fn, waits, "cc", 1))
        self._commit(("cc", total), reads, writes)

    def alias(self, src_keys, dst_keys):
        toks = []
        for s in src_keys:
            if self.last_w.get(s) is not None:
                toks.append(self.last_w[s])
            toks.extend(self.readers.get(s, ()))
        for d in dst_keys:
            self.readers.setdefault(d, []).extend(toks)

    def wait_all(self, eng, keys):
        waits = self._filter(eng, self._deps(eng, keys, keys))
        self.ops[eng].append(_Op(None, waits, None, 0))

    def emit(self):
        nc = self.nc
        with ExitStack() as es:
            sems = {}
            for k in self.semkeys:
                nm = k if isinstance(k, str) else "d_%s_%d" % (k[1], k[2])
                sems[k] = es.enter_context(nc.semaphore("s_" + nm))
            block = es.enter_context(nc.Block())

            def run(eng_name):
                def body(e):
                    for o in self.ops[eng_name]:
                        for (k, v) in o.waits:
                            e.wait_ge(sems[k], v)
                        if o.fn is not None:
                            o.fn(e).then_inc(sems[o.semkey], o.incval)
                return body

            block.tensor(run("tensor"))
            block.vector(run("vector"))
            block.scalar(run("scalar"))
            block.gpsimd(run("gpsimd"))
            block.sync(run("sync"))


class _Stop(Exception):
    pass


import os as _os
_KSTOP = int(_os.environ.get("KSTOP", "99"))


def _ck(n):
    if _KSTOP <= n:
        raise _Stop()


def build_nc(TPC):
    NT = TPC // T
    NB = TPC // 128
    S = NR * TPC
    KG = T
    NKG = NR * NT
    KGC = KG // 128

    nc = bass.Bass("TRN2", target_bir_lowering=False)

    def din(name, shape, dt=F32):
        return nc.dram_tensor(name, list(shape), dt, kind="ExternalInput").ap()

    x_d = din("x", [TPC, D])
    wg_d = [din("ffn1_w_gate", [D, DFF]), din("ffn2_w_gate", [D, DFF])]
    wu_d = [din("ffn1_w_up", [D, DFF]), din("ffn2_w_up", [D, DFF])]
    wd_d = [din("ffn1_w_down", [DFF, D]), din("ffn2_w_down", [DFF, D])]
    gpre_d = [din("ffn1_pre_g", [1, D]), din("ffn2_pre_g", [1, D])]
    gpost_d = [din("ffn1_post_g", [1, D]), din("ffn2_post_g", [1, D])]
    gmixpre_d = din("mix_pre_g", [1, D])
    gmixpost_d = din("mix_post_g", [1, D])
    win_d = din("w_in", [D, WIN_COLS])
    wpa_d = din("w_proj_a", [1024, D])
    wpb_d = din("w_proj_b", [1024, D])
    wout_d = din("w_out", [D, D])
    gbias_d = din("gate_biasT", [128, 32])
    sink_d = din("sink_bc", [128, 8])
    lamv_d = din("lamv", [128, 4])
    subg_d = din("sublnT", [128, 2])
    ropeC_d = din("ropeC", [128, TPC])
    ropeS_d = din("ropeS", [128, TPC])
    amask_d = din("amask", [128, 8, 128], BF16)
    ident_d = din("ident", [128, 128], BF16)
    y_d = nc.dram_tensor("y", [TPC, D], F32, kind="ExternalOutput").ap()

    qsp_d = nc.dram_tensor("q_sp", [16 * 128, TPC], BF16).ap()
    x1sp_d = nc.dram_tensor("x1_sp", [TPC, D], F32).ap()
    def dint(name, shape):
        return nc.dram_tensor(name, list(shape), BF16).ap()
    kTA_l = [dint("kTA_l%d" % i, [768, T]) for i in range(NT)]
    kTB_l = [dint("kTB_l%d" % i, [512, T]) for i in range(NT)]
    vA_l = [dint("vA_l%d" % i, [T, 768]) for i in range(NT)]
    vB_l = [dint("vB_l%d" % i, [T, 512]) for i in range(NT)]
    kTA_g = [dint("kTA_g%d" % i, [NR * 768, T]) for i in range(NT)]
    kTB_g = [dint("kTB_g%d" % i, [NR * 512, T]) for i in range(NT)]
    vA_g = [dint("vA_g%d" % i, [NR * T, 768]) for i in range(NT)]
    vB_g = [dint("vB_g%d" % i, [NR * T, 512]) for i in range(NT)]
    p = Prog(nc)
    es = ExitStack()
    with es:
        def sb(name, shape, dt):
            return es.enter_context(nc.sbuf_tensor(name, list(shape), dt))

        x_sb = sb("x_sb", [128, 4, D], F32)
        regA = sb("regA", [128, 24576], BF16)
        regB = sb("regB", [128, 16384], BF16)
        wbuf = [sb("wbuf%d" % i, [128, 16, 256], BF16) for i in range(4)]
        wdbuf = [sb("wdbuf%d" % i, [128, 4, 512], BF16) for i in range(2)]
        gbuf = sb("gbuf", [128, D], F32)
        xsb = [sb("xsb%d" % i, [128, D], BF16) for i in range(2)]
        sgj = sb("sgj", [128, 1024], F32)
        ptr = sb("ptr", [128, 2048], BF16)
        ropeC = sb("ropeC_sb", [128, T], F32)
        ropeS = sb("ropeS_sb", [128, T], F32)
        esink = sb("esink", [128, 8, 128], F32)
        amask = sb("amask_sb", [128, 8, 128], BF16)
        ident = sb("ident_sb", [128, 128], BF16)
        ones_b = sb("ones_b", [128, 128], BF16)
        ones_f = sb("ones_f", [128, 128], F32)
        stage = [sb("stage%d" % i, [128, T], BF16) for i in range(2)]
        vstage = sb("vstage", [128, 4, 256], BF16)
        gbias = sb("gbias", [128, 32], F32)
        subg = sb("subg", [128, 2], F32)
        lamv = sb("lamv_sb", [128, 4], F32)
        small = sb("small", [128, 64], F32)
        PS = [es.enter_context(nc.psum_tensor("ps%d" % i, [128, 512], F32)) for i in range(8)]

        aT = regA[:, 0:JF * T].rearrange("p (j t) -> p j t", t=T)
        qT = regA[:, 0:8192].rearrange("p (c t) -> p c t", t=T)
        oT = regA[:, 8192:16384].rearrange("p (c t) -> p c t", t=T)
        kvb = [regA[:, 16384 + i * 4096:16384 + (i + 1) * 4096] for i in range(2)]
        fb_mix = regA[:, 0:16384].bitcast(F32).rearrange("p (t d) -> p t d", d=D)
        gtmp = regA[:, 0:8192].bitcast(F32).rearrange("p (i t) -> p i t", t=T)
        hT = regB[:, 0:8192].rearrange("p (k t) -> p k t", t=T)
        mT = regB[:, 8192:16384].rearrange("p (k t) -> p k t", t=T)
        fb_ffn = regB[:].bitcast(F32).rearrange("p (t d) -> p t d", d=D)
        atmp = regB[:, 8192:16384].bitcast(F32).rearrange("p (i t) -> p i t", t=T)
        sg = [sgj[:, 0:512], sgj[:, 512:1024]]
        junk = sgj[:].bitcast(BF16)
        pt = [ptr[:, i * 512:(i + 1) * 512] for i in range(4)]
        rtmp = [ptr[:, 0:1024].bitcast(F32), ptr[:, 1024:2048].bitcast(F32)]

        AT_KEYS = ["A0", "A1", "kv0", "kv1"]
        ss = small[:, 0:4]
        ms = small[:, 4:8]
        sd = small[:, 8:12]
        rstd = small[:, 12:16]
        prod = small[:, 16:18]
        elam = small[:, 18:20]
        neglam = small[:, 20:21]
        esk = small[:, 24:32]
        subg_s = small[:, 32:34]

        wv = lambda w: w.rearrange("(k p) n -> p k n", p=128)

        st = {"wb": 0, "wd": 0, "xs": 0, "sg": 0, "stg": 0, "pt": 0, "kv": 0, "psT": 0}

        wbs = nc.dram_tensor("wbs", [160, 128, 4096], BF16).ap()
        wds = nc.dram_tensor("wds", [112, 128, 2048], BF16).ap()
        scr = {"wb": {}, "wd": {}}

        def load_wbuf(src_ap, k0=0, k1=16, buf=None, uid=None):
            if buf is None:
                buf = st["wb"]
                st["wb"] = (buf + 1) % 4
            keys = [("wb", buf, 0)] if k1 <= 8 else ([("wb", buf, 8)] if k0 >= 8 else [("wb", buf, 0), ("wb", buf, 8)])
            nk = k1 - k0
            dstv = wbuf[buf][:, k0:k1, :]
            if uid in scr["wb"]:
                sc = wbs[scr["wb"][uid], :, 0:nk * 256].rearrange("p (k c) -> p k c", c=256)
                p.dma("gpsimd", lambda e: e.dma_start(out=dstv, in_=sc), reads=[("wbs", uid)], writes=keys)
            else:
                p.dma("gpsimd", lambda e: e.dma_start(out=dstv, in_=src_ap), writes=keys)
                idx = len(scr["wb"])
                scr["wb"][uid] = idx
                sc = wbs[idx, :, 0:nk * 256].rearrange("p (k c) -> p k c", c=256)
                p.dma("sync", lambda e: e.dma_start(out=sc, in_=dstv), reads=keys, writes=[("wbs", uid)])
            return buf

        def load_wd(src_ap, uid=None):
            buf = st["wd"]
            st["wd"] = (buf + 1) % 2
            if uid in scr["wd"]:
                sc = wds[scr["wd"][uid]].rearrange("p (k c) -> p k c", c=512)
                p.dma("gpsimd", lambda e: e.dma_start(out=wdbuf[buf][:], in_=sc), reads=[("wds", uid)], writes=[("wd", buf)])
            else:
                p.dma("gpsimd", lambda e: e.dma_start(out=wdbuf[buf][:], in_=src_ap), writes=[("wd", buf)])
                idx = len(scr["wd"])
                scr["wd"][uid] = idx
                sc = wds[idx].rearrange("p (k c) -> p k c", c=512)
                p.dma("sync", lambda e: e.dma_start(out=sc, in_=wdbuf[buf][:]), reads=[("wd", buf)], writes=[("wds", uid)])
            return buf

        def mm_group(out_ap, pairs, reads, writes):
            n = len(pairs)

            def f(e):
                for i, (l, r) in enumerate(pairs):
                    ins = e.matmul(out_ap, lhsT=l, rhs=r, start=(i == 0), stop=(i == n - 1))
                return ins
            p.op("tensor", f, reads=reads, writes=writes)

        for dst, src, key in ((amask[:], amask_d, "amask"), (ident[:], ident_d, "ident"), (gbias[:], gbias_d, "gbias"),
                              (subg[:], subg_d, "subg"), (lamv[:], lamv_d, "lamv"), (esk, sink_d, "esk")):
            p.dma("sync", lambda e, dst=dst, src=src: e.dma_start(out=dst, in_=src), writes=[key])
        p.op("vector", lambda e: e.memset(ones_b[:], 1.0), writes=["ones_b"])
        p.op("vector", lambda e: e.memset(ones_f[:], 1.0), writes=["ones_f"])
        p.op("scalar", lambda e: e.activation(out=esk, in_=esk, func=AF.Exp), reads=["esk"], writes=["esk"])
        p.op("vector", lambda e: e.tensor_copy(out=esink[:], in_=esk.unsqueeze(2).broadcast_to([128, 8, 128])),
             reads=["esk"], writes=["esink"])
        p.op("vector", lambda e: e.tensor_tensor(out=prod, in0=lamv[:, 0:2], in1=lamv[:, 2:4], op=ALU.mult),
             reads=["lamv"], writes=["prod"])
        mm_group(PS[0][:, 0:2], [(ones_f[:], prod)], reads=["ones_f", "prod"], writes=[("ps", 0)])
        p.op("scalar", lambda e: e.activation(out=elam, in_=PS[0][:, 0:2], func=AF.Exp),
             reads=[("ps", 0)], writes=["elam"])
        p.op("vector", lambda e: e.tensor_tensor(out=neglam, in0=elam[:, 1:2], in1=elam[:, 0:1], op=ALU.subtract),
             reads=["elam"], writes=["neglam"])
        p.op("vector", lambda e: e.tensor_scalar(out=neglam, in0=neglam, scalar1=-LAM_INIT, scalar2=None, op0=ALU.add),
             reads=["neglam"], writes=["neglam"])
        p.op("vector", lambda e: e.tensor_scalar(out=subg_s, in0=subg[:], scalar1=1.0 - LAM_INIT, scalar2=None, op0=ALU.mult),
             reads=["subg"], writes=["subg_s"])

        def load_gain(g_d):
            p.dma("sync", lambda e: e.dma_start(out=gbuf[:], in_=g_d.broadcast_to([128, D])), writes=["gbuf"])

        def row_rstd(src_t, t, extra_reads):
            p.op("scalar", lambda e: e.activation(out=junk, in_=src_t, func=AF.Square, accum_out=ss[:, t:t + 1]),
                 reads=extra_reads, writes=[("ss", t), "sg0", "sg1"])
            p.op("vector", lambda e: e.tensor_scalar(out=ms[:, t:t + 1], in0=ss[:, t:t + 1], scalar1=1.0 / D, scalar2=EPS,
                                                     op0=ALU.mult, op1=ALU.add), reads=[("ss", t)], writes=[("ms", t)])
            p.op("scalar", lambda e: e.activation(out=sd[:, t:t + 1], in_=ms[:, t:t + 1], func=AF.Sqrt),
                 reads=[("ms", t)], writes=[("sd", t)])
            p.op("vector", lambda e: e.reciprocal(out=rstd[:, t:t + 1], in_=sd[:, t:t + 1]),
                 reads=[("sd", t)], writes=[("rstd", t)])

        def norm_to_hT(g_d):
            load_gain(g_d)
            for t in range(4):
                row_rstd(x_sb[:, t, :], t, [("x", t)])
                xi = st["xs"]
                st["xs"] = 1 - xi
                p.op("vector", lambda e, t=t, xi=xi: e.scalar_tensor_tensor(
                    out=xsb[xi][:], in0=x_sb[:, t, :], scalar=rstd[:, t:t + 1], in1=gbuf[:], op0=ALU.mult, op1=ALU.mult),
                    reads=[("x", t), ("rstd", t), "gbuf"], writes=[("xsb", xi)])
                for half in range(2):
                    b = 4 + st["psT"]
                    st["psT"] = (st["psT"] + 1) % 4
                    psv = PS[b][:].bitcast(BF16)

                    def tr(e, xi=xi, half=half, psv=psv):
                        for kk in range(8):
                            k = half * 8 + kk
                            ins = e.transpose(psv[:, kk * 128:(kk + 1) * 128], xsb[xi][:, k * 128:(k + 1) * 128], ident[:])
                        return ins
                    p.op("tensor", tr, reads=[("xsb", xi), "ident"], writes=[("ps", b)])
                    src = psv.rearrange("p (k t) -> p k t", t=128)
                    dst = hT[:, half * 8:(half + 1) * 8, t * 128:(t + 1) * 128]
                    if half == 0:
                        p.op("vector", lambda e, src=src, dst=dst: e.tensor_copy(out=dst, in_=src),
                             reads=[("ps", b)], writes=["hT"])
                    else:
                        p.op("scalar", lambda e, src=src, dst=dst: e.activation(out=dst, in_=src, func=AF.Copy),
                             reads=[("ps", b)], writes=["hT"])

        def ffn_stage1(wg, wu, wname):
            for j2 in range(JF // 2):
                bg = load_wbuf(wv(wg)[:, :, j2 * 256:(j2 + 1) * 256], uid=("g", wname, j2))
                bu = load_wbuf(wv(wu)[:, :, j2 * 256:(j2 + 1) * 256], uid=("u", wname, j2))
                for jj in range(2):
                    j = j2 * 2 + jj
                    pg, pu = (0, 1) if j % 2 == 0 else (2, 3)
                    mm_group(PS[pg][:], [(wbuf[bg][:, k, jj * 128:(jj + 1) * 128], hT[:, k, :]) for k in range(KC)],
                             reads=[("wb", bg, 0), ("wb", bg, 8), "hT"], writes=[("ps", pg)])
                    mm_group(PS[pu][:], [(wbuf[bu][:, k, jj * 128:(jj + 1) * 128], hT[:, k, :]) for k in range(KC)],
                             reads=[("wb", bu, 0), ("wb", bu, 8), "hT"], writes=[("ps", pu)])
                    si = st["sg"]
                    st["sg"] = 1 - si
                    p.op("scalar", lambda e, pg=pg, si=si: e.activation(out=sg[si], in_=PS[pg][:], func=AF.Silu),
                         reads=[("ps", pg)], writes=["sg%d" % si])
                    p.op("vector", lambda e, pu=pu, si=si, j=j: e.tensor_tensor(out=aT[:, j, :], in0=sg[si], in1=PS[pu][:], op=ALU.mult),
                         reads=[("ps", pu), "sg%d" % si], writes=[("aT", j)])

        def down_proj(src, src_key, w_d, nk, fb, fb_keys, wname):
            for n in range(4):
                banks = (4, 5, 6, 7) if n % 2 == 0 else (0, 1, 2, 3)
                ngr = nk // 4
                for kg in range(ngr):
                    b = load_wd(wv(w_d)[:, kg * 4:(kg + 1) * 4, n * 512:(n + 1) * 512], uid=(wname, n, kg))

                    def f(e, kg=kg, b=b, banks=banks, ngr=ngr):
                        for t in range(4):
                            for k in range(4):
                                ins = e.matmul(PS[banks[t]][:], lhsT=src[:, kg * 4 + k, t * 128:(t + 1) * 128], rhs=wdbuf[b][:, k, :],
                                               start=(kg == 0 and k == 0), stop=(kg == ngr - 1 and k == 3))
                        return ins
                    p.op("tensor", f, reads=[("wd", b)] + [src_key(kg * 4 + k) for k in range(4)],
                         writes=[("ps", bk) for bk in banks])
                for t in range(4):
                    dst = fb[:, t, n * 512:(n + 1) * 512]
                    if t % 2 == 0:
                        p.op("vector", lambda e, dst=dst, bk=banks[t]: e.tensor_copy(out=dst, in_=PS[bk][:]),
                             reads=[("ps", banks[t])], writes=[fb_keys[t]])
                    else:
                        p.op("scalar", lambda e, dst=dst, bk=banks[t]: e.activation(out=dst, in_=PS[bk][:], func=AF.Copy),
                             reads=[("ps", banks[t])], writes=[fb_keys[t]])

        def post_norm_res(fb, fb_keys, g_d, factor):
            load_gain(g_d)
            for t in range(4):
                row_rstd(fb[:, t, :], t, [fb_keys[t]])
                p.op("vector", lambda e, t=t: e.scalar_tensor_tensor(
                    out=fb[:, t, :], in0=fb[:, t, :], scalar=rstd[:, t:t + 1], in1=gbuf[:], op0=ALU.mult, op1=ALU.mult),
                    reads=[fb_keys[t], ("rstd", t), "gbuf"], writes=[fb_keys[t]])
                p.op("vector", lambda e, t=t: e.scalar_tensor_tensor(
                    out=x_sb[:, t, :], in0=fb[:, t, :], scalar=float(factor), in1=x_sb[:, t, :], op0=ALU.mult, op1=ALU.add),
                    reads=[fb_keys[t], ("x", t)], writes=[("x", t)])

        FB_FFN_KEYS = ["hT", "hT", "mT", "mT"]
        FB_MIX_KEYS = ["A0", "A0", "A1", "A1"]

        def ffn(l):
            norm_to_hT(gpre_d[l])
            p.alias(["A0", "A1", "kv0", "kv1"], [("aT", j) for j in range(JF)])
            ffn_stage1(wg_d[l], wu_d[l], l)
            down_proj(aT, lambda j: ("aT", j), wd_d[l], JF, fb_ffn, FB_FFN_KEYS, ("d", l))
            p.alias([("aT", j) for j in range(JF)], ["A0", "A1", "kv0", "kv1"])
            post_norm_res(fb_ffn, FB_FFN_KEYS, gpost_d[l], 0.5)

        def load_x(src_d, i, rkey=None):
            for t in range(4):
                r0 = i * T + t * 128
                p.dma("sync", lambda e, t=t, r0=r0: e.dma_start(out=x_sb[:, t, :], in_=src_d[r0:r0 + 128, :]),
                      reads=[(rkey, i, t)] if rkey else [], writes=[("x", t)])

        def store_x(dst_d, i, key):
            for t in range(4):
                r0 = i * T + t * 128
                p.dma("sync", lambda e, t=t, r0=r0: e.dma_start(out=dst_d[r0:r0 + 128, :], in_=x_sb[:, t, :]),
                      reads=[("x", t)], writes=[(key, i, t)])

        def qkv(i):
            c0 = i * T
            p.dma("sync", lambda e: e.dma_start(out=ropeC[:], in_=ropeC_d[:, c0:c0 + T]), writes=["ropeC"])
            p.dma("sync", lambda e: e.dma_start(out=ropeS[:], in_=ropeS_d[:, c0:c0 + T]), writes=["ropeS"])
            fm = []
            for c in range(8):
                fm.append((c * 128, qsp_d[c * 128:(c + 1) * 128, c0:c0 + T], ("qsp", i)))
            for g in range(2):
                fm.append((1024 + g * 128, kTA_l[i][g * 128:(g + 1) * 128, :], ("kTA", i, g)))
            for c in range(8):
                fm.append((1536 + c * 128, qsp_d[(8 + c) * 128:(9 + c) * 128, c0:c0 + T], ("qsp", i)))
            for c in range(8):
                if c < 4:
                    fm.append((2560 + c * 128, kTA_l[i][(2 + c) * 128:(3 + c) * 128, :], ("kTA", i, 2 + c)))
                else:
                    fm.append((2560 + c * 128, kTB_l[i][(c - 4) * 128:(c - 3) * 128, :], ("kTB", i, c - 4)))
            for pr in range(len(fm) // 2):
                col0 = fm[2 * pr][0]
                b = load_wbuf(wv(win_d)[:, :, col0:col0 + 256], uid=("in", col0))
                for jj in range(2):
                    _, dst_d, dkey = fm[2 * pr + jj]
                    bk = (2 * pr + jj) % 4
                    mm_group(PS[bk][:], [(wbuf[b][:, k, jj * 128:(jj + 1) * 128], hT[:, k, :]) for k in range(KC)],
                             reads=[("wb", b, 0), ("wb", b, 8), "hT"], writes=[("ps", bk)])
                    p.op("vector", lambda e, bk=bk: e.tensor_tensor(out=rtmp[0], in0=PS[bk][:], in1=ropeC[:], op=ALU.mult),
                         reads=[("ps", bk), "ropeC"], writes=["pt0", "pt1"])
                    p.op("vector", lambda e, bk=bk: e.tensor_tensor(out=rtmp[1][0:64, :], in0=PS[bk][64:128, :], in1=ropeS[0:64, :], op=ALU.mult),
                         reads=[("ps", bk), "ropeS"], writes=["pt2"])
                    p.op("vector", lambda e, bk=bk: e.tensor_tensor(out=rtmp[1][64:128, :], in0=PS[bk][0:64, :], in1=ropeS[64:128, :], op=ALU.mult),
                         reads=[("ps", bk), "ropeS"], writes=["pt3"])
                    si = st["stg"]
                    st["stg"] = 1 - si
                    p.op("gpsimd", lambda e, si=si: e.tensor_tensor(out=stage[si][:], in0=rtmp[0], in1=rtmp[1], op=ALU.add),
                         reads=["pt0", "pt1", "pt2", "pt3"], writes=[("stage", si)])
                    p.dma("sync", lambda e, si=si, dst_d=dst_d: e.dma_start(out=dst_d, in_=stage[si][:]),
                          reads=[("stage", si)], writes=[dkey if dkey[0] != "qsp" else ("qsp", i, 2 * pr + jj)])
            vs = [(1280, vA_l[i], 0, "vA"), (3584, vA_l[i], 256, "vA"), (3840, vA_l[i], 512, "vA"),
                  (4096, vB_l[i], 0, "vB"), (4352, vB_l[i], 256, "vB")]
            for (col0, vdst, vc0, vkey) in vs:
                b = load_wbuf(wv(win_d)[:, :, col0:col0 + 256], uid=("in", col0))
                for t in range(4):
                    bk = 4 + t
                    mm_group(PS[bk][:, 0:256],
                             [(hT[:, k, t * 128:(t + 1) * 128], wbuf[b][:, k, :]) for k in range(KC)],
                             reads=[("wb", b, 0), ("wb", b, 8), "hT"], writes=[("ps", bk)])
                    p.op("scalar", lambda e, bk=bk, t=t: e.activation(out=vstage[:, t, :], in_=PS[bk][:, 0:256], func=AF.Copy),
                         reads=[("ps", bk)], writes=["vstage"])
                dst = vdst[:, vc0:vc0 + 256].rearrange("(t p) c -> p t c", p=128)
                p.dma("sync", lambda e, dst=dst: e.dma_start(out=dst, in_=vstage[:]), reads=["vstage"], writes=[(vkey, i, vc0)])

        def run_all():
            _ck(1)
            for i in range(NT):
                load_x(x_d, i)
                norm_to_hT(gpre_d[0]) if _KSTOP == 2 else None
                _ck(2)
                ffn(0)
                _ck(3)
                store_x(x1sp_d, i, "x1sp")
                norm_to_hT(gmixpre_d)
                qkv(i)
            _ck(4)

            groups = [list(range(g * NR, (g + 1) * NR)) for g in range(NCORES // NR)]
            kTA_keys = lambda i: [("kTA", i, c) for c in range(6)]
            kTB_keys = lambda i: [("kTB", i, c) for c in range(4)]
            vA_keys = lambda i: [("vA", i, c) for c in (0, 256, 512)]
            vB_keys = lambda i: [("vB", i, c) for c in (0, 256)]
            for i in range(NT):
                for (src, dst, rk, wk) in ((kTA_l[i], kTA_g[i], kTA_keys(i), ("kTAg", i)), (kTB_l[i], kTB_g[i], kTB_keys(i), ("kTBg", i)),
                                           (vA_l[i], vA_g[i], vA_keys(i), ("vAg", i)), (vB_l[i], vB_g[i], vB_keys(i), ("vBg", i))):
                    p.cc(lambda e, src=src, dst=dst: e.collective_compute(
                        "AllGather", ALU.bypass, replica_groups=groups, ins=[src], outs=[dst]), rk, [wk], 4 * NT)
            _ck(5)

            def next_pt():
                i = st["pt"]
                st["pt"] = (i + 1) % 4
                return i

            def attn_b(i):
                for h in range(4):
                    units = [(kg, comp, c) for kg in range(NKG) for comp in range(2) for c in range(KGC)]
                    U = len(units)
                    loaded = {}
                    ptidx = {}

                    def ensure_loaded(kg, h=h, loaded=loaded):
                        if kg in loaded:
                            return loaded[kg]
                        r, ti = kg // NT, kg % NT
                        kb = st["kv"]
                        st["kv"] = (kb + 1) % 4
                        base = kvb[kb // 2][:, (kb % 2) * 2048:(kb % 2 + 1) * 2048]
                        kbuf = base[:, 0:2 * KG].rearrange("p (c k) -> p c k", k=KG)
                        vbuf = base[:, 2 * KG:2 * KG + KGC * 256].rearrange("p (c e) -> p c e", e=256)
                        if h < 2:
                            row0 = r * 768 + (2 + h * 2) * 128
                            ksrc = kTA_g[ti][row0:row0 + 256, :].rearrange("(c p) k -> p c k", p=128)
                            vsrc = vA_g[ti][r * T:(r + 1) * T, 256 + h * 256:512 + h * 256].rearrange("(c p) e -> p c e", p=128)
                            kkey, vkey = ("kTAg", ti), ("vAg", ti)
                        else:
                            row0 = r * 512 + (h - 2) * 256
                            ksrc = kTB_g[ti][row0:row0 + 256, :].rearrange("(c p) k -> p c k", p=128)
                            vsrc = vB_g[ti][r * T:(r + 1) * T, (h - 2) * 256:(h - 1) * 256].rearrange("(c p) e -> p c e", p=128)
                            kkey, vkey = ("kTBg", ti), ("vBg", ti)
                        p.dma("sync", lambda e, kbuf=kbuf, ksrc=ksrc: e.dma_start(out=kbuf, in_=ksrc),
                              reads=[kkey], writes=[("kbK", kb)])
                        p.dma("sync", lambda e, vbuf=vbuf, vsrc=vsrc: e.dma_start(out=vbuf, in_=vsrc),
                              reads=[vkey], writes=[("kbV", kb)])
                        loaded[kg] = (kb, kbuf, vbuf)
                        return loaded[kg]

                    def S_(u, h=h):
                        kg, comp, c = units[u]
                        kb, kbuf, vbuf = ensure_loaded(kg)
                        sbk = u % 2
                        mm_group(PS[sbk][:], [(kbuf[:, comp, c * 128:(c + 1) * 128], qT[:, 8 + h * 2 + comp, :])],
                                 reads=[("kbK", kb), "qT"], writes=[("ps", sbk)])

                    def E_(u, ptidx=ptidx):
                        sbk = u % 2
                        pi = next_pt()
                        ptidx[u] = pi
                        p.op("scalar", lambda e, sbk=sbk, pi=pi: e.activation(out=pt[pi], in_=PS[sbk][:], func=AF.Exp, scale=SCALE),
                             reads=[("ps", sbk)], writes=["pt%d" % pi])

                    def PV_(u, ptidx=ptidx):
                        kg, comp, c = units[u]
                        kb, kbuf, vbuf = ensure_loaded(kg)
                        pi = ptidx[u]
                        first = (kg == 0 and c == 0)
                        last = (kg == NKG - 1 and c == KGC - 1)
                        ob = 2 + comp * 2

                        def f(e, vbuf=vbuf, c=c, pi=pi, ob=ob, comp=comp, first=first, last=last):
                            e.matmul(PS[ob][:], lhsT=vbuf[:, c, 0:128], rhs=pt[pi], start=first, stop=last)
                            e.matmul(PS[ob + 1][:], lhsT=vbuf[:, c, 128:256], rhs=pt[pi], start=first, stop=last)
                            return e.matmul(PS[6 + comp][:], lhsT=ones_b[:], rhs=pt[pi], start=first, stop=last)
                        p.op("tensor", f, reads=[("kbV", kb), "pt%d" % pi, "ones_b"],
                             writes=[("ps", ob), ("ps", ob + 1), ("ps", 6 + comp)])

                    S_(0)
                    S_(1)
                    E_(0)
                    for u in range(U):
                        PV_(u)
                        if u + 2 < U:
                            S_(u + 2)
                        if u + 1 < U:
                            E_(u + 1)
                    r1, r2, ta, o0, o1, sq0, sq1, rn = [atmp[:, q, :] for q in range(8)]
                    K = lambda q: ("atmp", q)
                    p.op("vector", lambda e: e.reciprocal(out=r1, in_=PS[6][:]), reads=[("ps", 6)], writes=[K(0)])
                    p.op("vector", lambda e: e.reciprocal(out=r2, in_=PS[7][:]), reads=[("ps", 7)], writes=[K(1)])
                    p.op("vector", lambda e: e.tensor_scalar(out=r2, in0=r2, scalar1=neglam, scalar2=None, op0=ALU.mult),
                         reads=[K(1), "neglam"], writes=[K(1)])
                    for ec, oo, sq in ((0, o0, sq0), (1, o1, sq1)):
                        p.op("vector", lambda e, ec=ec: e.tensor_tensor(out=ta, in0=PS[2 + ec][:], in1=r1, op=ALU.mult),
                             reads=[("ps", 2 + ec), K(0)], writes=[K(2)])
                        p.op("vector", lambda e, ec=ec, oo=oo: e.tensor_tensor(out=oo, in0=PS[4 + ec][:], in1=r2, op=ALU.mult),
                             reads=[("ps", 4 + ec), K(1)], writes=[K(3 + ec)])
                        p.op("gpsimd", lambda e, oo=oo: e.tensor_tensor(out=oo, in0=oo, in1=ta, op=ALU.add),
                             reads=[K(2), K(3 + ec)], writes=[K(3 + ec)])
                        p.op("scalar", lambda e, oo=oo, sq=sq: e.activation(out=sq, in_=oo, func=AF.Square),
                             reads=[K(3 + ec)], writes=[K(5 + ec)])
                    mm_group(PS[0][:], [(ones_f[:], sq0), (ones_f[:], sq1)], reads=["ones_f", K(5), K(6)], writes=[("ps", 0)])
                    p.op("vector", lambda e: e.tensor_scalar(out=rn, in0=PS[0][:], scalar1=1.0 / 256.0, scalar2=EPS, op0=ALU.mult, op1=ALU.add),
                         reads=[("ps", 0)], writes=[K(7)])
                    p.op("scalar", lambda e: e.activation(out=rn, in_=rn, func=AF.Sqrt), reads=[K(7)], writes=[K(7)])
                    p.op("vector", lambda e: e.reciprocal(out=rn, in_=rn), reads=[K(7)], writes=[K(7)])
                    for ec, oo in ((0, o0), (1, o1)):
                        p.op("vector", lambda e, ec=ec, oo=oo, h=h: e.scalar_tensor_tensor(
                            out=oT[:, 8 + h * 2 + ec, :], in0=oo, scalar=subg_s[:, ec:ec + 1], in1=rn, op0=ALU.mult, op1=ALU.mult),
                            reads=[K(3 + ec), K(7), "subg_s"], writes=[("oT", 8 + h * 2 + ec)])

            def attn_a(i):
                n0 = i * 4
                lo = max(n0 - 1, 0)
                hi = min(n0 + 4, NB - 1)
                nblk = hi - lo + 1
                kab = kvb[0][:, 0:2 * 768].rearrange("p (g k) -> p g k", k=768)
                vab = kvb[0][:, 1536:1536 + 6 * 256].rearrange("p (b e) -> p b e", e=256)
                kcb = kvb[1][:, 0:2 * 768].rearrange("p (g k) -> p g k", k=768)
                vcb = kvb[1][:, 1536:1536 + 6 * 256].rearrange("p (b e) -> p b e", e=256)
                for m in range(lo, hi + 1):
                    ti, bi = m // 4, m % 4
                    p.dma("sync", lambda e, m=m, ti=ti, bi=bi: e.dma_start(
                        out=kab[:, :, (m - lo) * 128:(m - lo + 1) * 128],
                        in_=kTA_l[ti][0:256, bi * 128:(bi + 1) * 128].rearrange("(g p) k -> p g k", p=128)),
                        reads=kTA_keys(ti), writes=[("kvK", 0), ("kvV", 0)])
                    p.dma("sync", lambda e, m=m, ti=ti, bi=bi: e.dma_start(
                        out=vab[:, m - lo, :], in_=vA_l[ti][bi * 128:(bi + 1) * 128, 0:256]),
                        reads=vA_keys(ti), writes=[("kvK", 0), ("kvV", 0)])
                cands = []
                if i == 0:
                    cands += [(s_, r, "prev") for s_, r in enumerate((0, 1, 2))]
                if i == NT - 1:
                    cands += [(3 + s_, r, "next") for s_, r in enumerate((1, 2, 3))]
                for (slot, r, which) in cands:
                    ti = NT - 1 if which == "prev" else 0
                    col0 = T - 128 if which == "prev" else 0
                    p.dma("sync", lambda e, slot=slot, r=r, col0=col0, ti=ti: e.dma_start(
                        out=kcb[:, :, slot * 128:(slot + 1) * 128],
                        in_=kTA_g[ti][r * 768:r * 768 + 256, col0:col0 + 128].rearrange("(g p) k -> p g k", p=128)),
                        reads=[("kTAg", ti)], writes=[("kvK", 1), ("kvV", 1)])
                    p.dma("sync", lambda e, slot=slot, r=r, col0=col0, ti=ti: e.dma_start(
                        out=vcb[:, slot, :], in_=vA_g[ti][r * T + col0:r * T + col0 + 128, 0:256]),
                        reads=[("vAg", ti)], writes=[("kvK", 1), ("kvV", 1)])
                for nb in range(4):
                    n = n0 + nb
                    for g in range(2):
                        chunks = []
                        own = lambda m: (kab[:, g, (m - lo) * 128:(m - lo + 1) * 128], vab[:, m - lo, g * 128:(g + 1) * 128], 0)
                        cnd = lambda slot: (kcb[:, g, slot * 128:(slot + 1) * 128], vcb[:, slot, g * 128:(g + 1) * 128], 1)
                        if n == 0:
                            for s_ in range(3):
                                chunks.append(cnd(s_) + (2 + s_,))
                        else:
                            chunks.append(own(n - 1) + (0,))
                        chunks.append(own(n) + (None,))
                        if n == NB - 1:
                            for s_ in range(3):
                                chunks.append(cnd(3 + s_) + (5 + s_,))
                        else:
                            chunks.append(own(n + 1) + (1,))
                        qv = qT[:, g * 4:(g + 1) * 4, nb * 128:(nb + 1) * 128]
                        ob = 2 + (nb * 2 + g) % 2
                        sb_ = 4 + (nb * 2 + g) % 2
                        nch = len(chunks)
                        for ci, (kap, vap, which, mi) in enumerate(chunks):
                            sbk = ci % 2
                            mm_group(PS[sbk][:], [(kap, qv)], reads=[("kvK", which), ("kvV", which), "qT"], writes=[("ps", sbk)])
                            pi = next_pt()
                            p.op("scalar", lambda e, sbk=sbk, pi=pi: e.activation(out=pt[pi], in_=PS[sbk][:], func=AF.Exp, scale=SCALE),
                                 reads=[("ps", sbk)], writes=["pt%d" % pi])
                            if mi is not None:
                                ptv = pt[pi].rearrange("p (r q) -> p r q", q=128)
                                mv = amask[:, mi:mi + 1, :].broadcast_to([128, 4, 128])
                                p.op("vector", lambda e, ptv=ptv, mv=mv: e.tensor_tensor(out=ptv, in0=ptv, in1=mv, op=ALU.mult),
                                     reads=["pt%d" % pi, "amask"], writes=["pt%d" % pi])

                            def f(e, vap=vap, pi=pi, ob=ob, sb_=sb_, first=(ci == 0), last=(ci == nch - 1)):
                                e.matmul(PS[ob][:], lhsT=vap, rhs=pt[pi], start=first, stop=last)
                                return e.matmul(PS[sb_][:], lhsT=ones_b[:], rhs=pt[pi], start=first, stop=last)
                            p.op("tensor", f, reads=[("kvK", which), ("kvV", which), "pt%d" % pi, "ones_b"], writes=[("ps", ob), ("ps", sb_)])
                        den = atmp[:, (nb * 2 + g) % 2, :]
                        dk = ("atmp", (nb * 2 + g) % 2)
                        p.op("vector", lambda e, den=den, sb_=sb_, g=g: e.tensor_tensor(
                            out=den, in0=PS[sb_][:], in1=esink[:, g * 4:(g + 1) * 4, :].rearrange("p r q -> p (r q)"), op=ALU.add),
                            reads=[("ps", sb_), "esink"], writes=[dk])
                        p.op("vector", lambda e, den=den: e.reciprocal(out=den, in_=den), reads=[dk], writes=[dk])
                        dst = oT[:, g * 4:(g + 1) * 4, nb * 128:(nb + 1) * 128]
                        p.op("vector", lambda e, dst=dst, ob=ob, den=den: e.tensor_tensor(
                            out=dst, in0=PS[ob][:].rearrange("p (r q) -> p r q", q=128), in1=den.rearrange("p (r q) -> p r q", q=128), op=ALU.mult),
                            reads=[("ps", ob), dk], writes=[("oT", g * 4 + r_) for r_ in range(4)])

            def gates_merge(i):
                p.alias(["qT"], [("gt", q) for q in range(8)])
                oT_keys = [("oT", c) for c in range(16)]
                for j2 in range(8):
                    ba = load_wbuf(wv(win_d)[:, :, 4608 + j2 * 256:4608 + (j2 + 1) * 256], uid=("in", 4608 + j2 * 256))
                    bb = load_wbuf(wv(win_d)[:, :, 6656 + j2 * 256:6656 + (j2 + 1) * 256], uid=("in", 6656 + j2 * 256))
                    bp = load_wbuf(wv(wpa_d)[:, :, j2 * 256:(j2 + 1) * 256], 0, 8, uid=("pa", j2))
                    load_wbuf(wv(wpb_d)[:, :, j2 * 256:(j2 + 1) * 256], 8, 16, buf=bp, uid=("pb", j2))
                    for jj in range(2):
                        j = j2 * 2 + jj
                        bs = (0, 1, 2, 3) if j % 2 == 0 else (4, 5, 6, 7)
                        cs = slice(jj * 128, (jj + 1) * 128)
                        mm_group(PS[bs[0]][:], [(wbuf[ba][:, k, cs], hT[:, k, :]) for k in range(KC)],
                                 reads=[("wb", ba, 0), ("wb", ba, 8), "hT"], writes=[("ps", bs[0])])
                        mm_group(PS[bs[1]][:], [(wbuf[bb][:, k, cs], hT[:, k, :]) for k in range(KC)],
                                 reads=[("wb", bb, 0), ("wb", bb, 8), "hT"], writes=[("ps", bs[1])])
                        mm_group(PS[bs[2]][:], [(wbuf[bp][:, k, cs], oT[:, k, :]) for k in range(8)],
                                 reads=[("wb", bp, 0)] + oT_keys, writes=[("ps", bs[2])])
                        mm_group(PS[bs[3]][:], [(wbuf[bp][:, 8 + k, cs], oT[:, 8 + k, :]) for k in range(8)],
                                 reads=[("wb", bp, 8)] + oT_keys, writes=[("ps", bs[3])])
                        q0 = (j % 2) * 4
                        ga, gb_, ta, tb = [gtmp[:, q0 + q, :] for q in range(4)]
                        GK = lambda q: ("gt", q0 + q)
                        p.op("scalar", lambda e, ga=ga, b0=bs[0], j=j: e.activation(out=ga, in_=PS[b0][:], func=AF.Sigmoid, bias=gbias[:, j:j + 1]),
                             reads=[("ps", bs[0]), "gbias"], writes=[GK(0)])
                        p.op("scalar", lambda e, gb_=gb_, b1=bs[1], j=j: e.activation(out=gb_, in_=PS[b1][:], func=AF.Sigmoid, bias=gbias[:, 16 + j:17 + j]),
                             reads=[("ps", bs[1]), "gbias"], writes=[GK(1)])
                        p.op("vector", lambda e, ga=ga, ta=ta, b2=bs[2]: e.tensor_tensor(out=ta, in0=PS[b2][:], in1=ga, op=ALU.mult),
                             reads=[("ps", bs[2]), GK(0)], writes=[GK(2)])
                        p.op("vector", lambda e, gb_=gb_, tb=tb, b3=bs[3]: e.tensor_tensor(out=tb, in0=PS[b3][:], in1=gb_, op=ALU.mult),
                             reads=[("ps", bs[3]), GK(1)], writes=[GK(3)])
                        p.op("gpsimd", lambda e, ta=ta, tb=tb, j=j: e.tensor_tensor(out=mT[:, j, :], in0=ta, in1=tb, op=ALU.add),
                             reads=[GK(2), GK(3)], writes=[("mTc", j)])

            for i in range(NT):
                c0 = i * T
                load_x(x1sp_d, i, "x1sp")
                norm_to_hT(gmixpre_d)
                p.alias(["A0", "A1"], ["qT"] + [("oT", c) for c in range(16)])
                p.alias(["kv0", "kv1"], [("kbK", j) for j in range(4)] + [("kbV", j) for j in range(4)])
                p.alias(["mT"] + [("mTc", j) for j in range(16)], [("atmp", q) for q in range(8)])
                p.dma("sync", lambda e, c0=c0: e.dma_start(out=qT, in_=qsp_d[:, c0:c0 + T].rearrange("(c p) t -> p c t", p=128)),
                      reads=[("qsp", i, c) for c in range(26)], writes=["qT"])
                attn_b(i)
                p.alias([("kbK", j) for j in range(4)] + [("kbV", j) for j in range(4)], [("kvK", 0), ("kvV", 0), ("kvK", 1), ("kvV", 1)])
                _ck(6)
                attn_a(i)
                _ck(7)
                p.alias([("atmp", q) for q in range(8)], [("mTc", j) for j in range(16)])
                gates_merge(i)
                _ck(8)
                p.alias(["qT"] + [("gt", q) for q in range(8)] + [("oT", c) for c in range(16)], ["A0", "A1"])
                p.alias([("kvK", 0), ("kvV", 0), ("kvK", 1), ("kvV", 1)] + [("kbK", j) for j in range(4)] + [("kbV", j) for j in range(4)], ["kv0", "kv1"])
                down_proj(mT, lambda k: ("mTc", k), wout_d, KC, fb_mix, FB_MIX_KEYS, "wout")
                post_norm_res(fb_mix, FB_MIX_KEYS, gmixpost_d, 1.0)
                p.alias([("mTc", j) for j in range(16)], ["mT"])
                ffn(1)
                store_x(y_d, i, "y")

        try:
            run_all()
        except _Stop:
            pass
        p.wait_all("sync", list(p.last_w.keys()))
        p.emit()
    return nc


def _host_consts(TPC, rank):
    pos = (rank * TPC + np.arange(TPC)).astype(np.float32)
    inv = (np.float32(10000.0) ** (-np.arange(0, HD, 2, dtype=np.float32) / np.float32(HD))).astype(np.float32)
    ang = (pos[None, :] * inv[:, None]).astype(np.float32)
    c = np.cos(ang.astype(np.float64)).astype(np.float32)
    s = np.sin(ang.astype(np.float64)).astype(np.float32)
    ropeC = np.concatenate([c, c], 0)
    ropeS = np.concatenate([-s, s], 0)
    j = np.arange(128)[:, None]
    i = np.arange(128)[None, :]
    tri_prev = (j >= i).astype(np.float32)
    tri_next = (j <= i).astype(np.float32)
    am = np.zeros((128, 8, 128), np.float32)
    am[:, 0] = tri_prev
    am[:, 1] = tri_next
    for s_, r in enumerate((0, 1, 2)):
        if r == rank - 1:
            am[:, 2 + s_] = tri_prev
    for s_, r in enumerate((1, 2, 3)):
        if r == rank + 1:
            am[:, 5 + s_] = tri_next
    return ropeC, ropeS, am.astype(ml_dtypes.bfloat16), np.eye(128, dtype=np.float32).astype(ml_dtypes.bfloat16)


def make_in_maps(inputs, TPC):
    x = np.asarray(inputs["x"], np.float32)
    xf = x.reshape(-1, D)
    f = lambda k: np.ascontiguousarray(np.asarray(inputs[k], np.float32)[0])
    shared = {k: f(k) for k in ("ffn1_w_gate", "ffn1_w_up", "ffn1_w_down", "ffn2_w_gate", "ffn2_w_up", "ffn2_w_down",
                                 "w_in", "w_proj_a", "w_proj_b", "w_out")}
    for k in ("ffn1_pre_g", "ffn1_post_g", "ffn2_pre_g", "ffn2_post_g", "mix_pre_g", "mix_post_g"):
        shared[k] = np.ascontiguousarray(np.asarray(inputs[k], np.float32).reshape(1, D))
    shared["gate_biasT"] = np.ascontiguousarray(f("gate_bias").reshape(32, 128).T)
    shared["sink_bc"] = np.ascontiguousarray(np.broadcast_to(f("sink_logit").reshape(1, 8), (128, 8)))
    shared["lamv"] = np.ascontiguousarray(np.stack([f("lambda_q1"), f("lambda_q2"), f("lambda_k1"), f("lambda_k2")], 1))
    shared["sublnT"] = np.ascontiguousarray(f("subln_g").reshape(2, 128).T)
    in_maps = []
    for c in range(NCORES):
        rank = c % NR
        ropeC, ropeS, am, ident = _host_consts(TPC, rank)
        m = dict(shared)
        m["x"] = np.ascontiguousarray(xf[c * TPC:(c + 1) * TPC])
        m["ropeC"], m["ropeS"], m["amask"], m["ident"] = ropeC, ropeS, am, ident
        in_maps.append(m)
    return in_maps


_NC_CACHE = {}


def kernel(**inputs):
    x = np.asarray(inputs["x"])
    B, S_, _ = x.shape
    TPC = (B * S_) // NCORES
    if TPC not in _NC_CACHE:
        _NC_CACHE[TPC] = build_nc(TPC)
    nc = _NC_CACHE[TPC]
    in_maps = make_in_maps(inputs, TPC)
    res = run_bass_kernel_spmd(nc, in_maps, core_ids=list(range(NCORES)))
    y = np.concatenate([np.asarray(r["y"], np.float32) for r in res.results], 0)
    return y.reshape(B, S_, D)
```

```python
import math
from contextlib import ExitStack

import numpy as np
import ml_dtypes
import concourse.bass as bass
import concourse.mybir as mybir
from concourse.bass_utils import run_bass_kernel_spmd

F32 = mybir.dt.float32
BF16 = mybir.dt.bfloat16
AF = mybir.ActivationFunctionType
ALU = mybir.AluOpType

NCORES = 8
NR = 4
D = 2048
DFF = 5632
HD = 128
WIN_COLS = 8704
EPS = 1e-6
LAM_INIT = 0.8 - 0.6 * math.exp(-0.3 * 0)
T = 512
KC = D // 128
JF = DFF // 128
SCALE = HD ** -0.5

ENGINES = ("tensor", "vector", "scalar", "gpsimd", "sync")
N_DMA_SEMS = 12


class _Op:
    __slots__ = ("fn", "waits", "semkey", "incval")

    def __init__(self, fn, waits, semkey, incval):
        self.fn = fn
        self.waits = waits
        self.semkey = semkey
        self.incval = incval


class Prog:
    def __init__(self, nc):
        self.nc = nc
        self.ops = {e: [] for e in ENGINES}
        self.cnt = {e: 0 for e in ENGINES}
        self.dma_rr = {"sync": 0, "gpsimd": 0}
        self.dma_cnt = {}
        self.last_w = {}
        self.readers = {}
        self.known = {e: {} for e in ENGINES}
        self.semkeys = [e for e in ENGINES if e != "sync"] + ["cc"]
        for q in ("sync", "gpsimd"):
            for i in range(N_DMA_SEMS):
                self.semkeys.append(("dma", q, i))
                self.dma_cnt[("dma", q, i)] = 0

    def _deps(self, eng, reads, writes):
        need = {}

        def add(tok):
            if tok is None:
                return
            k, v = tok
            if eng == "tensor" and k == "tensor":
                return
            if need.get(k, 0) < v:
                need[k] = v

        for r in reads:
            add(self.last_w.get(r))
        for w in writes:
            add(self.last_w.get(w))
            for t in self.readers.get(w, ()):
                add(t)
        return need

    def _commit(self, tok, reads, writes):
        for w in writes:
            self.last_w[w] = tok
            self.readers[w] = []
        for r in reads:
            self.readers.setdefault(r, []).append(tok)

    def _filter(self, eng, need):
        kn = self.known[eng]
        out = []
        for k, v in need.items():
            if kn.get(k, 0) < v:
                kn[k] = v
                out.append((k, v))
        return out

    def op(self, eng, fn, reads=(), writes=()):
        reads = tuple(reads)
        writes = tuple(writes)
        waits = self._filter(eng, self._deps(eng, reads, writes))
        self.cnt[eng] += 1
        tok = (eng, self.cnt[eng])
        self.ops[eng].append(_Op(fn, waits, eng, 1))
        self._commit(tok, reads, writes)
        return tok

    def dma(self, q, fn, reads=(), writes=()):
        reads = tuple(reads)
        writes = tuple(writes)
        need = self._deps(q, reads, writes)
        i = self.dma_rr[q]
        self.dma_rr[q] = (i + 1) % N_DMA_SEMS
        sk = ("dma", q, i)
        prev = self.dma_cnt[sk]
        if prev and need.get(sk, 0) < 16 * prev:
            need[sk] = 16 * prev
        waits = self._filter(q, need)
        self.dma_cnt[sk] = prev + 1
        tok = (sk, 16 * (prev + 1))
        self.ops[q].append(_Op(fn, waits, sk, 16))
        self._commit(tok, reads, writes)
        return tok

    def cc(self, fn, reads, writes, total):
        reads = tuple(reads)
        writes = tuple(writes)
        waits = self._filter("gpsimd", self._deps("gpsimd", reads, writes))
        self.ops["gpsimd"].append(_Op(fn, waits, "cc", 1))
        self._commit(("cc", total), reads, writes)

    def alias(self, src_keys, dst_keys):
        toks = []
        for s in src_keys:
            if self.last_w.get(s) is not None:
                toks.append(self.last_w[s])
            toks.extend(self.readers.get(s, ()))
        for d in dst_keys:
            self.readers.setdefault(d, []).extend(toks)

    def wait_all(self, eng, keys):
        waits = self._filter(eng, self._deps(eng, keys, keys))
        self.ops[eng].append(_Op(None, waits, None, 0))

    def emit(self):
        nc = self.nc
        with ExitStack() as es:
            sems = {}
            for k in self.semkeys:
                nm = k if isinstance(k, str) else "d_%s_%d" % (k[1], k[2])
                sems[k] = es.enter_context(nc.semaphore("s_" + nm))
            block = es.enter_context(nc.Block())

            def run(eng_name):
                def body(e):
                    for o in self.ops[eng_name]:
                        for (k, v) in o.waits:
                            e.wait_ge(sems[k], v)
                        if o.fn is not None:
                            o.fn(e).then_inc(sems[o.semkey], o.incval)
                return body

            block.tensor(run("tensor"))
            block.vector(run("vector"))
            block.scalar(run("scalar"))
            block.gpsimd(run("gpsimd"))
            block.sync(run("sync"))


class _Stop(Exception):
    pass


import os as _os
_KSTOP = int(_os.environ.get("KSTOP", "99"))


def _ck(n):
    if _KSTOP <= n:
        raise _Stop()


def build_nc(TPC):
    NT = TPC // T
    NB = TPC // 128
    S = NR * TPC
    KG = T
    NKG = NR * NT
    KGC = KG // 128

    nc = bass.Bass("TRN2", target_bir_lowering=False)

    def din(name, shape, dt=F32):
        return nc.dram_tensor(name, list(shape), dt, kind="ExternalInput").ap()

    x_d = din("x", [TPC, D])
    wg_d = [din("ffn1_w_gate", [D, DFF]), din("ffn2_w_gate", [D, DFF])]
    wu_d = [din("ffn1_w_up", [D, DFF]), din("ffn2_w_up", [D, DFF])]
    wd_d = [din("ffn1_w_down", [DFF, D]), din("ffn2_w_down", [DFF, D])]
    gpre_d = [din("ffn1_pre_g", [1, D]), din("ffn2_pre_g", [1, D])]
    gpost_d = [din("ffn1_post_g", [1, D]), din("ffn2_post_g", [1, D])]
    gmixpre_d = din("mix_pre_g", [1, D])
    gmixpost_d = din("mix_post_g", [1, D])
    win_d = din("w_in", [D, WIN_COLS])
    wpa_d = din("w_proj_a", [1024, D])
    wpb_d = din("w_proj_b", [1024, D])
    wout_d = din("w_out", [D, D])
    gbias_d = din("gate_biasT", [128, 32])
    sink_d = din("sink_bc", [128, 8])
    lamv_d = din("lamv", [128, 4])
    subg_d = din("sublnT", [128, 2])
    ropeC_d = din("ropeC", [128, TPC])
    ropeS_d = din("ropeS", [128, TPC])
    amask_d = din("amask", [128, 8, 128], BF16)
    ident_d = din("ident", [128, 128], BF16)
    y_d = nc.dram_tensor("y", [TPC, D], F32, kind="ExternalOutput").ap()

    qsp_d = nc.dram_tensor("q_sp", [16 * 128, TPC], BF16).ap()
    x1sp_d = nc.dram_tensor("x1_sp", [TPC, D], F32).ap()
    def dint(name, shape):
        return nc.dram_tensor(name, list(shape), BF16).ap()
    kTA_l = [dint("kTA_l%d" % i, [768, T]) for i in range(NT)]
    kTB_l = [dint("kTB_l%d" % i, [512, T]) for i in range(NT)]
    vA_l = [dint("vA_l%d" % i, [T, 768]) for i in range(NT)]
    vB_l = [dint("vB_l%d" % i, [T, 512]) for i in range(NT)]
    kTA_g = [dint("kTA_g%d" % i, [NR * 768, T]) for i in range(NT)]
    kTB_g = [dint("kTB_g%d" % i, [NR * 512, T]) for i in range(NT)]
    vA_g = [dint("vA_g%d" % i, [NR * T, 768]) for i in range(NT)]
    vB_g = [dint("vB_g%d" % i, [NR * T, 512]) for i in range(NT)]
    p = Prog(nc)
    es = ExitStack()
    with es:
        def sb(name, shape, dt):
            return es.enter_context(nc.sbuf_tensor(name, list(shape), dt))

        x_sb = sb("x_sb", [128, 4, D], F32)
        regA = sb("regA", [128, 24576], BF16)
        regB = sb("regB", [128, 16384], BF16)
        wbuf = [sb("wbuf%d" % i, [128, 16, 256], BF16) for i in range(4)]
        wdbuf = [sb("wdbuf%d" % i, [128, 4, 512], BF16) for i in range(2)]
        gbuf = sb("gbuf", [128, D], F32)
        xsb = [sb("xsb%d" % i, [128, D], BF16) for i in range(2)]
        sgj = sb("sgj", [128, 1024], F32)
        ptr = sb("ptr", [128, 2048], BF16)
        ropeC = sb("ropeC_sb", [128, T], F32)
        ropeS = sb("ropeS_sb", [128, T], F32)
        esink = sb("esink", [128, 8, 128], F32)
        amask = sb("amask_sb", [128, 8, 128], BF16)
        ident = sb("ident_sb", [128, 128], BF16)
        ones_b = sb("ones_b", [128, 128], BF16)
        ones_f = sb("ones_f", [128, 128], F32)
        stage = [sb("stage%d" % i, [128, T], BF16) for i in range(2)]
        vstage = sb("vstage", [128, 4, 256], BF16)
        gbias = sb("gbias", [128, 32], F32)
        subg = sb("subg", [128, 2], F32)
        lamv = sb("lamv_sb", [128, 4], F32)
        small = sb("small", [128, 64], F32)
        PS = [es.enter_context(nc.psum_tensor("ps%d" % i, [128, 512], F32)) for i in range(8)]

        aT = regA[:, 0:JF * T].rearrange("p (j t) -> p j t", t=T)
        qT = regA[:, 0:8192].rearrange("p (c t) -> p c t", t=T)
        oT = regA[:, 8192:16384].rearrange("p (c t) -> p c t", t=T)
        kvb = [regA[:, 16384 + i * 4096:16384 + (i + 1) * 4096] for i in range(2)]
        fb_mix = regA[:, 0:16384].bitcast(F32).rearrange("p (t d) -> p t d", d=D)
        gtmp = regA[:, 0:8192].bitcast(F32).rearrange("p (i t) -> p i t", t=T)
        hT = regB[:, 0:8192].rearrange("p (k t) -> p k t", t=T)
        mT = regB[:, 8192:16384].rearrange("p (k t) -> p k t", t=T)
        fb_ffn = regB[:].bitcast(F32).rearrange("p (t d) -> p t d", d=D)
        atmp = regB[:, 8192:16384].bitcast(F32).rearrange("p (i t) -> p i t", t=T)
        sg = [sgj[:, 0:512], sgj[:, 512:1024]]
        junk = sgj[:].bitcast(BF16)
        pt = [ptr[:, i * 512:(i + 1) * 512] for i in range(4)]
        rtmp = [ptr[:, 0:1024].bitcast(F32), ptr[:, 1024:2048].bitcast(F32)]

        AT_KEYS = ["A0", "A1", "kv0", "kv1"]
        ss = small[:, 0:4]
        ms = small[:, 4:8]
        sd = small[:, 8:12]
        rstd = small[:, 12:16]
        prod = small[:, 16:18]
        elam = small[:, 18:20]
        neglam = small[:, 20:21]
        esk = small[:, 24:32]
        subg_s = small[:, 32:34]

        wv = lambda w: w.rearrange("(k p) n -> p k n", p=128)

        st = {"wb": 0, "wd": 0, "xs": 0, "sg": 0, "stg": 0, "pt": 0, "kv": 0, "psT": 0}

        wbs = nc.dram_tensor("wbs", [160, 128, 4096], BF16).ap()
        wds = nc.dram_tensor("wds", [112, 128, 2048], BF16).ap()
        scr = {"wb": {}, "wd": {}}

        def load_wbuf(src_ap, k0=0, k1=16, buf=None, uid=None):
            if buf is None:
                buf = st["wb"]
                st["wb"] = (buf + 1) % 4
            keys = [("wb", buf, 0)] if k1 <= 8 else ([("wb", buf, 8)] if k0 >= 8 else [("wb", buf, 0), ("wb", buf, 8)])
            nk = k1 - k0
            dstv = wbuf[buf][:, k0:k1, :]
            if uid in scr["wb"]:
                sc = wbs[scr["wb"][uid], :, 0:nk * 256].rearrange("p (k c) -> p k c", c=256)
                p.dma("gpsimd", lambda e: e.dma_start(out=dstv, in_=sc), reads=[("wbs", uid)], writes=keys)
            else:
                p.dma("gpsimd", lambda e: e.dma_start(out=dstv, in_=src_ap), writes=keys)
                idx = len(scr["wb"])
                scr["wb"][uid] = idx
                sc = wbs[idx, :, 0:nk * 256].rearrange("p (k c) -> p k c", c=256)
                p.dma("sync", lambda e: e.dma_start(out=sc, in_=dstv), reads=keys, writes=[("wbs", uid)])
            return buf

        def load_wd(src_ap, uid=None):
            buf = st["wd"]
            st["wd"] = (buf + 1) % 2
            if uid in scr["wd"]:
                sc = wds[scr["wd"][uid]].rearrange("p (k c) -> p k c", c=512)
                p.dma("gpsimd", lambda e: e.dma_start(out=wdbuf[buf][:], in_=sc), reads=[("wds", uid)], writes=[("wd", buf)])
            else:
                p.dma("gpsimd", lambda e: e.dma_start(out=wdbuf[buf][:], in_=src_ap), writes=[("wd", buf)])
                idx = len(scr["wd"])
                scr["wd"][uid] = idx
                sc = wds[idx].rearrange("p (k c) -> p k c", c=512)
                p.dma("sync", lambda e: e.dma_start(out=sc, in_=wdbuf[buf][:]), reads=[("wd", buf)], writes=[("wds", uid)])
            return buf

        def mm_group(out_ap, pairs, reads, writes):
            n = len(pairs)

            def f(e):
                for i, (l, r) in enumerate(pairs):
                    ins = e.matmul(out_ap, lhsT=l, rhs=r, start=(i == 0), stop=(i == n - 1))
                return ins
            p.op("tensor", f, reads=reads, writes=writes)

        for dst, src, key in ((amask[:], amask_d, "amask"), (ident[:], ident_d, "ident"), (gbias[:], gbias_d, "gbias"),
                              (subg[:], subg_d, "subg"), (lamv[:], lamv_d, "lamv"), (esk, sink_d, "esk")):
            p.dma("sync", lambda e, dst=dst, src=src: e.dma_start(out=dst, in_=src), writes=[key])
        p.op("vector", lambda e: e.memset(ones_b[:], 1.0), writes=["ones_b"])
        p.op("vector", lambda e: e.memset(ones_f[:], 1.0), writes=["ones_f"])
        p.op("scalar", lambda e: e.activation(out=esk, in_=esk, func=AF.Exp), reads=["esk"], writes=["esk"])
        p.op("vector", lambda e: e.tensor_copy(out=esink[:], in_=esk.unsqueeze(2).broadcast_to([128, 8, 128])),
             reads=["esk"], writes=["esink"])
        p.op("vector", lambda e: e.tensor_tensor(out=prod, in0=lamv[:, 0:2], in1=lamv[:, 2:4], op=ALU.mult),
             reads=["lamv"], writes=["prod"])
        mm_group(PS[0][:, 0:2], [(ones_f[:], prod)], reads=["ones_f", "prod"], writes=[("ps", 0)])
        p.op("scalar", lambda e: e.activation(out=elam, in_=PS[0][:, 0:2], func=AF.Exp),
             reads=[("ps", 0)], writes=["elam"])
        p.op("vector", lambda e: e.tensor_tensor(out=neglam, in0=elam[:, 1:2], in1=elam[:, 0:1], op=ALU.subtract),
             reads=["elam"], writes=["neglam"])
        p.op("vector", lambda e: e.tensor_scalar(out=neglam, in0=neglam, scalar1=-LAM_INIT, scalar2=None, op0=ALU.add),
             reads=["neglam"], writes=["neglam"])
        p.op("vector", lambda e: e.tensor_scalar(out=subg_s, in0=subg[:], scalar1=1.0 - LAM_INIT, scalar2=None, op0=ALU.mult),
             reads=["subg"], writes=["subg_s"])

        def load_gain(g_d):
            p.dma("sync", lambda e: e.dma_start(out=gbuf[:], in_=g_d.broadcast_to([128, D])), writes=["gbuf"])

        def rows_rstd(srcs, read_keys):
            for t in range(4):
                p.op("scalar", lambda e, t=t: e.activation(out=junk, in_=srcs[t], func=AF.Square, accum_out=ss[:, t:t + 1]),
                     reads=[read_keys[t]], writes=[("ss", t), "sg0", "sg1"])
            p.op("vector", lambda e: e.tensor_scalar(out=ms, in0=ss, scalar1=1.0 / D, scalar2=EPS, op0=ALU.mult, op1=ALU.add),
                 reads=[("ss", t) for t in range(4)], writes=["ms"])
            p.op("scalar", lambda e: e.activation(out=sd, in_=ms, func=AF.Sqrt), reads=["ms"], writes=["sd"])
            p.op("vector", lambda e: e.reciprocal(out=rstd, in_=sd), reads=["sd"], writes=["rstd"])

        def norm_to_hT(g_d):
            load_gain(g_d)
            rows_rstd([x_sb[:, t, :] for t in range(4)], [("x", t) for t in range(4)])
            for t in range(4):
                xi = st["xs"]
                st["xs"] = 1 - xi
                p.op("vector", lambda e, t=t, xi=xi: e.scalar_tensor_tensor(
                    out=xsb[xi][:], in0=x_sb[:, t, :], scalar=rstd[:, t:t + 1], in1=gbuf[:], op0=ALU.mult, op1=ALU.mult),
                    reads=[("x", t), "rstd", "gbuf"], writes=[("xsb", xi)])
                for half in range(2):
                    b = 4 + st["psT"]
                    st["psT"] = (st["psT"] + 1) % 4
                    psv = PS[b][:].bitcast(BF16)

                    def tr(e, xi=xi, half=half, psv=psv):
                        for kk in range(8):
                            k = half * 8 + kk
                            ins = e.transpose(psv[:, kk * 128:(kk + 1) * 128], xsb[xi][:, k * 128:(k + 1) * 128], ident[:])
                        return ins
                    p.op("tensor", tr, reads=[("xsb", xi), "ident"], writes=[("ps", b)])
                    src = psv.rearrange("p (k t) -> p k t", t=128)
                    dst = hT[:, half * 8:(half + 1) * 8, t * 128:(t + 1) * 128]
                    p.op("scalar", lambda e, src=src, dst=dst: e.activation(out=dst, in_=src, func=AF.Copy),
                         reads=[("ps", b)], writes=["hT"])

        def ffn_stage1(wg, wu, wname):
            for j2 in range(JF // 2):
                bg = load_wbuf(wv(wg)[:, :, j2 * 256:(j2 + 1) * 256], uid=("g", wname, j2))
                bu = load_wbuf(wv(wu)[:, :, j2 * 256:(j2 + 1) * 256], uid=("u", wname, j2))
                for jj in range(2):
                    j = j2 * 2 + jj
                    pg, pu = (0, 1) if j % 2 == 0 else (2, 3)
                    mm_group(PS[pg][:], [(wbuf[bg][:, k, jj * 128:(jj + 1) * 128], hT[:, k, :]) for k in range(KC)],
                             reads=[("wb", bg, 0), ("wb", bg, 8), "hT"], writes=[("ps", pg)])
                    mm_group(PS[pu][:], [(wbuf[bu][:, k, jj * 128:(jj + 1) * 128], hT[:, k, :]) for k in range(KC)],
                             reads=[("wb", bu, 0), ("wb", bu, 8), "hT"], writes=[("ps", pu)])
                    si = st["sg"]
                    st["sg"] = 1 - si
                    p.op("scalar", lambda e, pg=pg, si=si: e.activation(out=sg[si], in_=PS[pg][:], func=AF.Silu),
                         reads=[("ps", pg)], writes=["sg%d" % si])
                    p.op("vector", lambda e, pu=pu, si=si, j=j: e.tensor_tensor(out=aT[:, j, :], in0=sg[si], in1=PS[pu][:], op=ALU.mult),
                         reads=[("ps", pu), "sg%d" % si], writes=[("aT", j)])

        def down_proj(src, src_key, w_d, nk, fb, fb_keys, wname):
            for n in range(4):
                banks = (4, 5, 6, 7) if n % 2 == 0 else (0, 1, 2, 3)
                ngr = nk // 4
                for kg in range(ngr):
                    b = load_wd(wv(w_d)[:, kg * 4:(kg + 1) * 4, n * 512:(n + 1) * 512], uid=(wname, n, kg))

                    def f(e, kg=kg, b=b, banks=banks, ngr=ngr):
                        for t in range(4):
                            for k in range(4):
                                ins = e.matmul(PS[banks[t]][:], lhsT=src[:, kg * 4 + k, t * 128:(t + 1) * 128], rhs=wdbuf[b][:, k, :],
                                               start=(kg == 0 and k == 0), stop=(kg == ngr - 1 and k == 3))
                        return ins
                    p.op("tensor", f, reads=[("wd", b)] + [src_key(kg * 4 + k) for k in range(4)],
                         writes=[("ps", bk) for bk in banks])
                for t in range(4):
                    dst = fb[:, t, n * 512:(n + 1) * 512]
                    if t % 2 == 0:
                        p.op("vector", lambda e, dst=dst, bk=banks[t]: e.tensor_copy(out=dst, in_=PS[bk][:]),
                             reads=[("ps", banks[t])], writes=[fb_keys[t]])
                    else:
                        p.op("scalar", lambda e, dst=dst, bk=banks[t]: e.activation(out=dst, in_=PS[bk][:], func=AF.Copy),
                             reads=[("ps", banks[t])], writes=[fb_keys[t]])

        def post_norm_res(fb, fb_keys, g_d, factor):
            load_gain(g_d)
            rows_rstd([fb[:, t, :] for t in range(4)], fb_keys)
            for t in range(4):
                p.op("vector", lambda e, t=t: e.scalar_tensor_tensor(
                    out=fb[:, t, :], in0=fb[:, t, :], scalar=rstd[:, t:t + 1], in1=gbuf[:], op0=ALU.mult, op1=ALU.mult),
                    reads=[fb_keys[t], "rstd", "gbuf"], writes=[fb_keys[t]])
                p.op("vector", lambda e, t=t: e.scalar_tensor_tensor(
                    out=x_sb[:, t, :], in0=fb[:, t, :], scalar=float(factor), in1=x_sb[:, t, :], op0=ALU.mult, op1=ALU.add),
                    reads=[fb_keys[t], ("x", t)], writes=[("x", t)])

        FB_FFN_KEYS = ["hT", "hT", "mT", "mT"]
        FB_MIX_KEYS = ["A0", "A0", "A1", "A1"]

        def ffn(l):
            norm_to_hT(gpre_d[l])
            p.alias(["A0", "A1", "kv0", "kv1"], [("aT", j) for j in range(JF)])
            ffn_stage1(wg_d[l], wu_d[l], l)
            down_proj(aT, lambda j: ("aT", j), wd_d[l], JF, fb_ffn, FB_FFN_KEYS, ("d", l))
            p.alias([("aT", j) for j in range(JF)], ["A0", "A1", "kv0", "kv1"])
            post_norm_res(fb_ffn, FB_FFN_KEYS, gpost_d[l], 0.5)

        def load_x(src_d, i, rkey=None):
            for t in range(4):
                r0 = i * T + t * 128
                p.dma("sync", lambda e, t=t, r0=r0: e.dma_start(out=x_sb[:, t, :], in_=src_d[r0:r0 + 128, :]),
                      reads=[(rkey, i, t)] if rkey else [], writes=[("x", t)])

        def store_x(dst_d, i, key):
            for t in range(4):
                r0 = i * T + t * 128
                p.dma("sync", lambda e, t=t, r0=r0: e.dma_start(out=dst_d[r0:r0 + 128, :], in_=x_sb[:, t, :]),
                      reads=[("x", t)], writes=[(key, i, t)])

        def qkv(i):
            c0 = i * T
            p.dma("sync", lambda e: e.dma_start(out=ropeC[:], in_=ropeC_d[:, c0:c0 + T]), writes=["ropeC"])
            p.dma("sync", lambda e: e.dma_start(out=ropeS[:], in_=ropeS_d[:, c0:c0 + T]), writes=["ropeS"])
            fm = []
            for c in range(8):
                fm.append((c * 128, qsp_d[c * 128:(c + 1) * 128, c0:c0 + T], ("qsp", i)))
            for g in range(2):
                fm.append((1024 + g * 128, kTA_l[i][g * 128:(g + 1) * 128, :], ("kTA", i, g)))
            for c in range(8):
                fm.append((1536 + c * 128, qsp_d[(8 + c) * 128:(9 + c) * 128, c0:c0 + T], ("qsp", i)))
            for c in range(8):
                if c < 4:
                    fm.append((2560 + c * 128, kTA_l[i][(2 + c) * 128:(3 + c) * 128, :], ("kTA", i, 2 + c)))
                else:
                    fm.append((2560 + c * 128, kTB_l[i][(c - 4) * 128:(c - 3) * 128, :], ("kTB", i, c - 4)))
            for pr in range(len(fm) // 2):
                col0 = fm[2 * pr][0]
                b = load_wbuf(wv(win_d)[:, :, col0:col0 + 256], uid=("in", col0))
                for jj in range(2):
                    _, dst_d, dkey = fm[2 * pr + jj]
                    bk = (2 * pr + jj) % 4
                    mm_group(PS[bk][:], [(wbuf[b][:, k, jj * 128:(jj + 1) * 128], hT[:, k, :]) for k in range(KC)],
                             reads=[("wb", b, 0), ("wb", b, 8), "hT"], writes=[("ps", bk)])
                    p.op("vector", lambda e, bk=bk: e.tensor_tensor(out=rtmp[0], in0=PS[bk][:], in1=ropeC[:], op=ALU.mult),
                         reads=[("ps", bk), "ropeC"], writes=["pt0", "pt1"])
                    p.op("vector", lambda e, bk=bk: e.tensor_tensor(out=rtmp[1][0:64, :], in0=PS[bk][64:128, :], in1=ropeS[0:64, :], op=ALU.mult),
                         reads=[("ps", bk), "ropeS"], writes=["pt2"])
                    p.op("vector", lambda e, bk=bk: e.tensor_tensor(out=rtmp[1][64:128, :], in0=PS[bk][0:64, :], in1=ropeS[64:128, :], op=ALU.mult),
                         reads=[("ps", bk), "ropeS"], writes=["pt3"])
                    si = st["stg"]
                    st["stg"] = 1 - si
                    p.op("vector", lambda e, si=si: e.tensor_tensor(out=stage[si][:], in0=rtmp[0], in1=rtmp[1], op=ALU.add),
                         reads=["pt0", "pt1", "pt2", "pt3"], writes=[("stage", si)])
                    p.dma("sync", lambda e, si=si, dst_d=dst_d: e.dma_start(out=dst_d, in_=stage[si][:]),
                          reads=[("stage", si)], writes=[dkey if dkey[0] != "qsp" else ("qsp", i, 2 * pr + jj)])
            vs = [(1280, vA_l[i], 0, "vA"), (3584, vA_l[i], 256, "vA"), (3840, vA_l[i], 512, "vA"),
                  (4096, vB_l[i], 0, "vB"), (4352, vB_l[i], 256, "vB")]
            for (col0, vdst, vc0, vkey) in vs:
                b = load_wbuf(wv(win_d)[:, :, col0:col0 + 256], uid=("in", col0))
                for t in range(4):
                    bk = 4 + t
                    mm_group(PS[bk][:, 0:256],
                             [(hT[:, k, t * 128:(t + 1) * 128], wbuf[b][:, k, :]) for k in range(KC)],
                             reads=[("wb", b, 0), ("wb", b, 8), "hT"], writes=[("ps", bk)])
                    p.op("scalar", lambda e, bk=bk, t=t: e.activation(out=vstage[:, t, :], in_=PS[bk][:, 0:256], func=AF.Copy),
                         reads=[("ps", bk)], writes=["vstage"])
                dst = vdst[:, vc0:vc0 + 256].rearrange("(t p) c -> p t c", p=128)
                p.dma("sync", lambda e, dst=dst: e.dma_start(out=dst, in_=vstage[:]), reads=["vstage"], writes=[(vkey, i, vc0)])

        def run_all():
            _ck(1)
            for i in range(NT):
                load_x(x_d, i)
                norm_to_hT(gpre_d[0]) if _KSTOP == 2 else None
                _ck(2)
                ffn(0)
                _ck(3)
                store_x(x1sp_d, i, "x1sp")
                norm_to_hT(gmixpre_d)
                qkv(i)
            _ck(4)

            groups = [list(range(g * NR, (g + 1) * NR)) for g in range(NCORES // NR)]
            kTA_keys = lambda i: [("kTA", i, c) for c in range(6)]
            kTB_keys = lambda i: [("kTB", i, c) for c in range(4)]
            vA_keys = lambda i: [("vA", i, c) for c in (0, 256, 512)]
            vB_keys = lambda i: [("vB", i, c) for c in (0, 256)]
            for i in range(NT):
                for (src, dst, rk, wk) in ((kTA_l[i], kTA_g[i], kTA_keys(i), ("kTAg", i)), (kTB_l[i], kTB_g[i], kTB_keys(i), ("kTBg", i)),
                                           (vA_l[i], vA_g[i], vA_keys(i), ("vAg", i)), (vB_l[i], vB_g[i], vB_keys(i), ("vBg", i))):
                    p.cc(lambda e, src=src, dst=dst: e.collective_compute(
                        "AllGather", ALU.bypass, replica_groups=groups, ins=[src], outs=[dst]), rk, [wk], 4 * NT)
            _ck(5)

            def next_pt():
                i = st["pt"]
                st["pt"] = (i + 1) % 4
                return i

            def attn_b(i):
                for h in range(4):
                    units = [(kg, comp, c) for kg in range(NKG) for comp in range(2) for c in range(KGC)]
                    U = len(units)
                    loaded = {}
                    ptidx = {}

                    def ensure_loaded(kg, h=h, loaded=loaded):
                        if kg in loaded:
                            return loaded[kg]
                        r, ti = kg // NT, kg % NT
                        kb = st["kv"]
                        st["kv"] = (kb + 1) % 4
                        base = kvb[kb // 2][:, (kb % 2) * 2048:(kb % 2 + 1) * 2048]
                        kbuf = base[:, 0:2 * KG].rearrange("p (c k) -> p c k", k=KG)
                        vbuf = base[:, 2 * KG:2 * KG + KGC * 256].rearrange("p (c e) -> p c e", e=256)
                        if h < 2:
                            row0 = r * 768 + (2 + h * 2) * 128
                            ksrc = kTA_g[ti][row0:row0 + 256, :].rearrange("(c p) k -> p c k", p=128)
                            vsrc = vA_g[ti][r * T:(r + 1) * T, 256 + h * 256:512 + h * 256].rearrange("(c p) e -> p c e", p=128)
                            kkey, vkey = ("kTAg", ti), ("vAg", ti)
                        else:
                            row0 = r * 512 + (h - 2) * 256
                            ksrc = kTB_g[ti][row0:row0 + 256, :].rearrange("(c p) k -> p c k", p=128)
                            vsrc = vB_g[ti][r * T:(r + 1) * T, (h - 2) * 256:(h - 1) * 256].rearrange("(c p) e -> p c e", p=128)
                            kkey, vkey = ("kTBg", ti), ("vBg", ti)
                        p.dma("sync", lambda e, kbuf=kbuf, ksrc=ksrc: e.dma_start(out=kbuf, in_=ksrc),
                              reads=[kkey], writes=[("kbK", kb)])
                        p.dma("sync", lambda e, vbuf=vbuf, vsrc=vsrc: e.dma_start(out=vbuf, in_=vsrc),
                              reads=[vkey], writes=[("kbV", kb)])
                        loaded[kg] = (kb, kbuf, vbuf)
                        return loaded[kg]

                    def S_(u, h=h):
                        kg, comp, c = units[u]
                        kb, kbuf, vbuf = ensure_loaded(kg)
                        sbk = u % 2
                        mm_group(PS[sbk][:], [(kbuf[:, comp, c * 128:(c + 1) * 128], qT[:, 8 + h * 2 + comp, :])],
                                 reads=[("kbK", kb), "qT"], writes=[("ps", sbk)])

                    def E_(u, ptidx=ptidx):
                        sbk = u % 2
                        pi = next_pt()
                        ptidx[u] = pi
                        p.op("scalar", lambda e, sbk=sbk, pi=pi: e.activation(out=pt[pi], in_=PS[sbk][:], func=AF.Exp, scale=SCALE),
                             reads=[("ps", sbk)], writes=["pt%d" % pi])

                    def PV_(u, ptidx=ptidx):
                        kg, comp, c = units[u]
                        kb, kbuf, vbuf = ensure_loaded(kg)
                        pi = ptidx[u]
                        first = (kg == 0 and c == 0)
                        last = (kg == NKG - 1 and c == KGC - 1)
                        ob = 2 + comp * 2

                        def f(e, vbuf=vbuf, c=c, pi=pi, ob=ob, comp=comp, first=first, last=last):
                            e.matmul(PS[ob][:], lhsT=vbuf[:, c, 0:128], rhs=pt[pi], start=first, stop=last)
                            return e.matmul(PS[ob + 1][:], lhsT=vbuf[:, c, 128:256], rhs=pt[pi], start=first, stop=last)
                        p.op("tensor", f, reads=[("kbV", kb), "pt%d" % pi], writes=[("ps", ob), ("ps", ob + 1)])
                        if first:
                            p.op("vector", lambda e, comp=comp, pi=pi: e.tensor_copy(out=sg[comp], in_=pt[pi]),
                                 reads=["pt%d" % pi], writes=["sg%d" % comp])
                        else:
                            p.op("vector", lambda e, comp=comp, pi=pi: e.tensor_tensor(out=sg[comp], in0=sg[comp], in1=pt[pi], op=ALU.add),
                                 reads=["pt%d" % pi, "sg%d" % comp], writes=["sg%d" % comp])

                    S_(0)
                    S_(1)
                    E_(0)
                    for u in range(U):
                        PV_(u)
                        if u + 2 < U:
                            S_(u + 2)
                        if u + 1 < U:
                            E_(u + 1)
                    r1, r2, ta, o0, o1, sq0, sq1, rn = [atmp[:, q, :] for q in range(8)]
                    K = lambda q: ("atmp", q)
                    mm_group(PS[6][:], [(ones_f[:], sg[0])], reads=["ones_f", "sg0"], writes=[("ps", 6)])
                    mm_group(PS[7][:], [(ones_f[:], sg[1])], reads=["ones_f", "sg1"], writes=[("ps", 7)])
                    p.op("vector", lambda e: e.reciprocal(out=r1, in_=PS[6][:]), reads=[("ps", 6)], writes=[K(0)])
                    p.op("vector", lambda e: e.reciprocal(out=r2, in_=PS[7][:]), reads=[("ps", 7)], writes=[K(1)])
                    p.op("vector", lambda e: e.tensor_scalar(out=r2, in0=r2, scalar1=neglam, scalar2=None, op0=ALU.mult),
                         reads=[K(1), "neglam"], writes=[K(1)])
                    for ec, oo, sq in ((0, o0, sq0), (1, o1, sq1)):
                        p.op("vector", lambda e, ec=ec: e.tensor_tensor(out=ta, in0=PS[2 + ec][:], in1=r1, op=ALU.mult),
                             reads=[("ps", 2 + ec), K(0)], writes=[K(2)])
                        p.op("vector", lambda e, ec=ec, oo=oo: e.tensor_tensor(out=oo, in0=PS[4 + ec][:], in1=r2, op=ALU.mult),
                             reads=[("ps", 4 + ec), K(1)], writes=[K(3 + ec)])
                        p.op("vector", lambda e, oo=oo: e.tensor_tensor(out=oo, in0=oo, in1=ta, op=ALU.add),
                             reads=[K(2), K(3 + ec)], writes=[K(3 + ec)])
                        p.op("scalar", lambda e, oo=oo, sq=sq: e.activation(out=sq, in_=oo, func=AF.Square),
                             reads=[K(3 + ec)], writes=[K(5 + ec)])
                    mm_group(PS[0][:], [(ones_f[:], sq0), (ones_f[:], sq1)], reads=["ones_f", K(5), K(6)], writes=[("ps", 0)])
                    p.op("vector", lambda e: e.tensor_scalar(out=rn, in0=PS[0][:], scalar1=1.0 / 256.0, scalar2=EPS, op0=ALU.mult, op1=ALU.add),
                         reads=[("ps", 0)], writes=[K(7)])
                    p.op("scalar", lambda e: e.activation(out=rn, in_=rn, func=AF.Sqrt), reads=[K(7)], writes=[K(7)])
                    p.op("vector", lambda e: e.reciprocal(out=rn, in_=rn), reads=[K(7)], writes=[K(7)])
                    for ec, oo in ((0, o0), (1, o1)):
                        p.op("vector", lambda e, ec=ec, oo=oo, h=h: e.scalar_tensor_tensor(
                            out=oT[:, 8 + h * 2 + ec, :], in0=oo, scalar=subg_s[:, ec:ec + 1], in1=rn, op0=ALU.mult, op1=ALU.mult),
                            reads=[K(3 + ec), K(7), "subg_s"], writes=[("oT", 8 + h * 2 + ec)])

            def attn_a(i):
                n0 = i * 4
                lo = max(n0 - 1, 0)
                hi = min(n0 + 4, NB - 1)
                nblk = hi - lo + 1
                kab = kvb[0][:, 0:2 * 768].rearrange("p (g k) -> p g k", k=768)
                vab = kvb[0][:, 1536:1536 + 6 * 256].rearrange("p (b e) -> p b e", e=256)
                kcb = kvb[1][:, 0:2 * 768].rearrange("p (g k) -> p g k", k=768)
                vcb = kvb[1][:, 1536:1536 + 6 * 256].rearrange("p (b e) -> p b e", e=256)
                for m in range(lo, hi + 1):
                    ti, bi = m // 4, m % 4
                    p.dma("sync", lambda e, m=m, ti=ti, bi=bi: e.dma_start(
                        out=kab[:, :, (m - lo) * 128:(m - lo + 1) * 128],
                        in_=kTA_l[ti][0:256, bi * 128:(bi + 1) * 128].rearrange("(g p) k -> p g k", p=128)),
                        reads=kTA_keys(ti), writes=[("kvK", 0), ("kvV", 0)])
                    p.dma("sync", lambda e, m=m, ti=ti, bi=bi: e.dma_start(
                        out=vab[:, m - lo, :], in_=vA_l[ti][bi * 128:(bi + 1) * 128, 0:256]),
                        reads=vA_keys(ti), writes=[("kvK", 0), ("kvV", 0)])
                cands = []
                if i == 0:
                    cands += [(s_, r, "prev") for s_, r in enumerate((0, 1, 2))]
                if i == NT - 1:
                    cands += [(3 + s_, r, "next") for s_, r in enumerate((1, 2, 3))]
                for (slot, r, which) in cands:
                    ti = NT - 1 if which == "prev" else 0
                    col0 = T - 128 if which == "prev" else 0
                    p.dma("sync", lambda e, slot=slot, r=r, col0=col0, ti=ti: e.dma_start(
                        out=kcb[:, :, slot * 128:(slot + 1) * 128],
                        in_=kTA_g[ti][r * 768:r * 768 + 256, col0:col0 + 128].rearrange("(g p) k -> p g k", p=128)),
                        reads=[("kTAg", ti)], writes=[("kvK", 1), ("kvV", 1)])
                    p.dma("sync", lambda e, slot=slot, r=r, col0=col0, ti=ti: e.dma_start(
                        out=vcb[:, slot, :], in_=vA_g[ti][r * T + col0:r * T + col0 + 128, 0:256]),
                        reads=[("vAg", ti)], writes=[("kvK", 1), ("kvV", 1)])
                for nb in range(4):
                    n = n0 + nb
                    for g in range(2):
                        chunks = []
                        own = lambda m: (kab[:, g, (m - lo) * 128:(m - lo + 1) * 128], vab[:, m - lo, g * 128:(g + 1) * 128], 0)
                        cnd = lambda slot: (kcb[:, g, slot * 128:(slot + 1) * 128], vcb[:, slot, g * 128:(g + 1) * 128], 1)
                        if n == 0:
                            for s_ in range(3):
                                chunks.append(cnd(s_) + (2 + s_,))
                        else:
                            chunks.append(own(n - 1) + (0,))
                        chunks.append(own(n) + (None,))
                        if n == NB - 1:
                            for s_ in range(3):
                                chunks.append(cnd(3 + s_) + (5 + s_,))
                        else:
                            chunks.append(own(n + 1) + (1,))
                        qv = qT[:, g * 4:(g + 1) * 4, nb * 128:(nb + 1) * 128]
                        ob = 2 + (nb * 2 + g) % 2
                        sb_ = 4 + (nb * 2 + g) % 2
                        nch = len(chunks)
                        for ci, (kap, vap, which, mi) in enumerate(chunks):
                            sbk = ci % 2
                            mm_group(PS[sbk][:], [(kap, qv)], reads=[("kvK", which), ("kvV", which), "qT"], writes=[("ps", sbk)])
                            pi = next_pt()
                            p.op("scalar", lambda e, sbk=sbk, pi=pi: e.activation(out=pt[pi], in_=PS[sbk][:], func=AF.Exp, scale=SCALE),
                                 reads=[("ps", sbk)], writes=["pt%d" % pi])
                            if mi is not None:
                                ptv = pt[pi].rearrange("p (r q) -> p r q", q=128)
                                mv = amask[:, mi:mi + 1, :].broadcast_to([128, 4, 128])
                                p.op("vector", lambda e, ptv=ptv, mv=mv: e.tensor_tensor(out=ptv, in0=ptv, in1=mv, op=ALU.mult),
                                     reads=["pt%d" % pi, "amask"], writes=["pt%d" % pi])

                            def f(e, vap=vap, pi=pi, ob=ob, sb_=sb_, first=(ci == 0), last=(ci == nch - 1)):
                                e.matmul(PS[ob][:], lhsT=vap, rhs=pt[pi], start=first, stop=last)
                                return e.matmul(PS[sb_][:], lhsT=ones_b[:], rhs=pt[pi], start=first, stop=last)
                            p.op("tensor", f, reads=[("kvK", which), ("kvV", which), "pt%d" % pi, "ones_b"], writes=[("ps", ob), ("ps", sb_)])
                        den = atmp[:, (nb * 2 + g) % 2, :]
                        dk = ("atmp", (nb * 2 + g) % 2)
                        p.op("vector", lambda e, den=den, sb_=sb_, g=g: e.tensor_tensor(
                            out=den, in0=PS[sb_][:], in1=esink[:, g * 4:(g + 1) * 4, :].rearrange("p r q -> p (r q)"), op=ALU.add),
                            reads=[("ps", sb_), "esink"], writes=[dk])
                        p.op("vector", lambda e, den=den: e.reciprocal(out=den, in_=den), reads=[dk], writes=[dk])
                        dst = oT[:, g * 4:(g + 1) * 4, nb * 128:(nb + 1) * 128]
                        p.op("vector", lambda e, dst=dst, ob=ob, den=den: e.tensor_tensor(
                            out=dst, in0=PS[ob][:].rearrange("p (r q) -> p r q", q=128), in1=den.rearrange("p (r q) -> p r q", q=128), op=ALU.mult),
                            reads=[("ps", ob), dk], writes=[("oT", g * 4 + r_) for r_ in range(4)])

            def gates_merge(i):
                p.alias(["qT"], [("gt", q) for q in range(8)])
                oT_keys = [("oT", c) for c in range(16)]
                for j2 in range(8):
                    ba = load_wbuf(wv(win_d)[:, :, 4608 + j2 * 256:4608 + (j2 + 1) * 256], uid=("in", 4608 + j2 * 256))
                    bb = load_wbuf(wv(win_d)[:, :, 6656 + j2 * 256:6656 + (j2 + 1) * 256], uid=("in", 6656 + j2 * 256))
                    bp = load_wbuf(wv(wpa_d)[:, :, j2 * 256:(j2 + 1) * 256], 0, 8, uid=("pa", j2))
                    load_wbuf(wv(wpb_d)[:, :, j2 * 256:(j2 + 1) * 256], 8, 16, buf=bp, uid=("pb", j2))
                    for jj in range(2):
                        j = j2 * 2 + jj
                        bs = (0, 1, 2, 3) if j % 2 == 0 else (4, 5, 6, 7)
                        cs = slice(jj * 128, (jj + 1) * 128)
                        mm_group(PS[bs[0]][:], [(wbuf[ba][:, k, cs], hT[:, k, :]) for k in range(KC)],
                                 reads=[("wb", ba, 0), ("wb", ba, 8), "hT"], writes=[("ps", bs[0])])
                        mm_group(PS[bs[1]][:], [(wbuf[bb][:, k, cs], hT[:, k, :]) for k in range(KC)],
                                 reads=[("wb", bb, 0), ("wb", bb, 8), "hT"], writes=[("ps", bs[1])])
                        mm_group(PS[bs[2]][:], [(wbuf[bp][:, k, cs], oT[:, k, :]) for k in range(8)],
                                 reads=[("wb", bp, 0)] + oT_keys, writes=[("ps", bs[2])])
                        mm_group(PS[bs[3]][:], [(wbuf[bp][:, 8 + k, cs], oT[:, 8 + k, :]) for k in range(8)],
                                 reads=[("wb", bp, 8)] + oT_keys, writes=[("ps", bs[3])])
                        q0 = (j % 2) * 4
                        ga, gb_, ta, tb = [gtmp[:, q0 + q, :] for q in range(4)]
                        GK = lambda q: ("gt", q0 + q)
                        p.op("scalar", lambda e, ga=ga, b0=bs[0], j=j: e.activation(out=ga, in_=PS[b0][:], func=AF.Sigmoid, bias=gbias[:, j:j + 1]),
                             reads=[("ps", bs[0]), "gbias"], writes=[GK(0)])
                        p.op("scalar", lambda e, gb_=gb_, b1=bs[1], j=j: e.activation(out=gb_, in_=PS[b1][:], func=AF.Sigmoid, bias=gbias[:, 16 + j:17 + j]),
                             reads=[("ps", bs[1]), "gbias"], writes=[GK(1)])
                        p.op("vector", lambda e, ga=ga, ta=ta, b2=bs[2]: e.tensor_tensor(out=ta, in0=PS[b2][:], in1=ga, op=ALU.mult),
                             reads=[("ps", bs[2]), GK(0)], writes=[GK(2)])
                        p.op("vector", lambda e, gb_=gb_, tb=tb, b3=bs[3]: e.tensor_tensor(out=tb, in0=PS[b3][:], in1=gb_, op=ALU.mult),
                             reads=[("ps", bs[3]), GK(1)], writes=[GK(3)])
                        p.op("vector", lambda e, ta=ta, tb=tb, j=j: e.tensor_tensor(out=mT[:, j, :], in0=ta, in1=tb, op=ALU.add),
                             reads=[GK(2), GK(3)], writes=[("mTc", j)])

            for i in range(NT):
                c0 = i * T
                load_x(x1sp_d, i, "x1sp")
                norm_to_hT(gmixpre_d)
                p.alias(["A0", "A1"], ["qT"] + [("oT", c) for c in range(16)])
                p.alias(["kv0", "kv1"], [("kbK", j) for j in range(4)] + [("kbV", j) for j in range(4)])
                p.alias(["mT"] + [("mTc", j) for j in range(16)], [("atmp", q) for q in range(8)])
                p.dma("sync", lambda e, c0=c0: e.dma_start(out=qT, in_=qsp_d[:, c0:c0 + T].rearrange("(c p) t -> p c t", p=128)),
                      reads=[("qsp", i, c) for c in range(26)], writes=["qT"])
                attn_b(i)
                p.alias([("kbK", j) for j in range(4)] + [("kbV", j) for j in range(4)], [("kvK", 0), ("kvV", 0), ("kvK", 1), ("kvV", 1)])
                _ck(6)
                attn_a(i)
                _ck(7)
                p.alias([("atmp", q) for q in range(8)], [("mTc", j) for j in range(16)])
                gates_merge(i)
                _ck(8)
                p.alias(["qT"] + [("gt", q) for q in range(8)] + [("oT", c) for c in range(16)], ["A0", "A1"])
                p.alias([("kvK", 0), ("kvV", 0), ("kvK", 1), ("kvV", 1)] + [("kbK", j) for j in range(4)] + [("kbV", j) for j in range(4)], ["kv0", "kv1"])
                down_proj(mT, lambda k: ("mTc", k), wout_d, KC, fb_mix, FB_MIX_KEYS, "wout")
                post_norm_res(fb_mix, FB_MIX_KEYS, gmixpost_d, 1.0)
                p.alias([("mTc", j) for j in range(16)], ["mT"])
                ffn(1)
                store_x(y_d, i, "y")

        try:
            run_all()
        except _Stop:
            pass
        p.wait_all("sync", list(p.last_w.keys()))
        p.emit()
    return nc


def _host_consts(TPC, rank):
    pos = (rank * TPC + np.arange(TPC)).astype(np.float32)
    inv = (np.float32(10000.0) ** (-np.arange(0, HD, 2, dtype=np.float32) / np.float32(HD))).astype(np.float32)
    ang = (pos[None, :] * inv[:, None]).astype(np.float32)
    c = np.cos(ang.astype(np.float64)).astype(np.float32)
    s = np.sin(ang.astype(np.float64)).astype(np.float32)
    ropeC = np.concatenate([c, c], 0)
    ropeS = np.concatenate([-s, s], 0)
    j = np.arange(128)[:, None]
    i = np.arange(128)[None, :]
    tri_prev = (j >= i).astype(np.float32)
    tri_next = (j <= i).astype(np.float32)
    am = np.zeros((128, 8, 128), np.float32)
    am[:, 0] = tri_prev
    am[:, 1] = tri_next
    for s_, r in enumerate((0, 1, 2)):
        if r == rank - 1:
            am[:, 2 + s_] = tri_prev
    for s_, r in enumerate((1, 2, 3)):
        if r == rank + 1:
            am[:, 5 + s_] = tri_next
    return ropeC, ropeS, am.astype(ml_dtypes.bfloat16), np.eye(128, dtype=np.float32).astype(ml_dtypes.bfloat16)


def make_in_maps(inputs, TPC):
    x = np.asarray(inputs["x"], np.float32)
    xf = x.reshape(-1, D)
    f = lambda k: np.ascontiguousarray(np.asarray(inputs[k], np.float32)[0])
    shared = {k: f(k) for k in ("ffn1_w_gate", "ffn1_w_up", "ffn1_w_down", "ffn2_w_gate", "ffn2_w_up", "ffn2_w_down",
                                 "w_in", "w_proj_a", "w_proj_b", "w_out")}
    for k in ("ffn1_pre_g", "ffn1_post_g", "ffn2_pre_g", "ffn2_post_g", "mix_pre_g", "mix_post_g"):
        shared[k] = np.ascontiguousarray(np.asarray(inputs[k], np.float32).reshape(1, D))
    shared["gate_biasT"] = np.ascontiguousarray(f("gate_bias").reshape(32, 128).T)
    shared["sink_bc"] = np.ascontiguousarray(np.broadcast_to(f("sink_logit").reshape(1, 8), (128, 8)))
    shared["lamv"] = np.ascontiguousarray(np.stack([f("lambda_q1"), f("lambda_q2"), f("lambda_k1"), f("lambda_k2")], 1))
    shared["sublnT"] = np.ascontiguousarray(f("subln_g").reshape(2, 128).T)
    in_maps = []
    for c in range(NCORES):
        rank = c % NR
        ropeC, ropeS, am, ident = _host_consts(TPC, rank)
        m = dict(shared)
        m["x"] = np.ascontiguousarray(xf[c * TPC:(c + 1) * TPC])
        m["ropeC"], m["ropeS"], m["amask"], m["ident"] = ropeC, ropeS, am, ident
        in_maps.append(m)
    return in_maps


_NC_CACHE = {}


def kernel(**inputs):
    x = np.asarray(inputs["x"])
    B, S_, _ = x.shape
    TPC = (B * S_) // NCORES
    if TPC not in _NC_CACHE:
        _NC_CACHE[TPC] = build_nc(TPC)
    nc = _NC_CACHE[TPC]
    in_maps = make_in_maps(inputs, TPC)
    res = run_bass_kernel_spmd(nc, in_maps, core_ids=list(range(NCORES)))
    y = np.concatenate([np.asarray(r["y"], np.float32) for r in res.results], 0)
    return y.reshape(B, S_, D)
```

```python
import math
from contextlib import ExitStack

import numpy as np
import ml_dtypes
import concourse.bass as bass
import concourse.mybir as mybir
from concourse.bass_utils import run_bass_kernel_spmd

F32 = mybir.dt.float32
BF16 = mybir.dt.bfloat16
AF = mybir.ActivationFunctionType
ALU = mybir.AluOpType

NCORES = 8
NR = 4
D = 2048
DFF = 5632
HD = 128
WIN_COLS = 8704
EPS = 1e-6
LAM_INIT = 0.8 - 0.6 * math.exp(-0.3 * 0)
T = 512
KC = D // 128
JF = DFF // 128
SCALE = HD ** -0.5

ENGINES = ("tensor", "vector", "scalar", "gpsimd", "sync")
N_DMA_SEMS = 12


class _Op:
    __slots__ = ("fn", "waits", "semkey", "incval")

    def __init__(self, fn, waits, semkey, incval):
        self.fn = fn
        self.waits = waits
        self.semkey = semkey
        self.incval = incval


class Prog:
    def __init__(self, nc):
        self.nc = nc
        self.ops = {e: [] for e in ENGINES}
        self.cnt = {e: 0 for e in ENGINES}
        self.dma_rr = {"sync": 0, "gpsimd": 0}
        self.dma_cnt = {}
        self.last_w = {}
        self.readers = {}
        self.known = {e: {} for e in ENGINES}
        self.semkeys = [e for e in ENGINES if e != "sync"] + ["cc"]
        for q in ("sync", "gpsimd"):
            for i in range(N_DMA_SEMS):
                self.semkeys.append(("dma", q, i))
                self.dma_cnt[("dma", q, i)] = 0

    def _deps(self, eng, reads, writes):
        need = {}

        def add(tok):
            if tok is None:
                return
            k, v = tok
            if eng == "tensor" and k == "tensor":
                return
            if need.get(k, 0) < v:
                need[k] = v

        for r in reads:
            add(self.last_w.get(r))
        for w in writes:
            add(self.last_w.get(w))
            for t in self.readers.get(w, ()):
                add(t)
        return need

    def _commit(self, tok, reads, writes):
        for w in writes:
            self.last_w[w] = tok
            self.readers[w] = []
        for r in reads:
            self.readers.setdefault(r, []).append(tok)

    def _filter(self, eng, need):
        kn = self.known[eng]
        out = []
        for k, v in need.items():
            if kn.get(k, 0) < v:
                kn[k] = v
                out.append((k, v))
        return out

    def op(self, eng, fn, reads=(), writes=()):
        reads = tuple(reads)
        writes = tuple(writes)
        waits = self._filter(eng, self._deps(eng, reads, writes))
        self.cnt[eng] += 1
        tok = (eng, self.cnt[eng])
        self.ops[eng].append(_Op(fn, waits, eng, 1))
        self._commit(tok, reads, writes)
        return tok

    def dma(self, q, fn, reads=(), writes=()):
        reads = tuple(reads)
        writes = tuple(writes)
        need = self._deps(q, reads, writes)
        i = self.dma_rr[q]
        self.dma_rr[q] = (i + 1) % N_DMA_SEMS
        sk = ("dma", q, i)
        prev = self.dma_cnt[sk]
        if prev and need.get(sk, 0) < 16 * prev:
            need[sk] = 16 * prev
        waits = self._filter(q, need)
        self.dma_cnt[sk] = prev + 1
        tok = (sk, 16 * (prev + 1))
        self.ops[q].append(_Op(fn, waits, sk, 16))
        self._commit(tok, reads, writes)
        return tok

    def cc(self, fn, reads, writes, total):
        reads = tuple(reads)
        writes = tuple(writes)
        waits = self._filter("gpsimd", self._deps("gpsimd", reads, writes))
        self.ops["gpsimd"].append(_Op(fn, waits, "cc", 1))
        self._commit(("cc", total), reads, writes)

    def alias(self, src_keys, dst_keys):
        toks = []
        for s in src_keys:
            if self.last_w.get(s) is not None:
                toks.append(self.last_w[s])
            toks.extend(self.readers.get(s, ()))
        for d in dst_keys:
            self.readers.setdefault(d, []).extend(toks)

    def wait_all(self, eng, keys):
        waits = self._filter(eng, self._deps(eng, keys, keys))
        self.ops[eng].append(_Op(None, waits, None, 0))

    def emit(self):
        nc = self.nc
        with ExitStack() as es:
            sems = {}
            for k in self.semkeys:
                nm = k if isinstance(k, str) else "d_%s_%d" % (k[1], k[2])
                sems[k] = es.enter_context(nc.semaphore("s_" + nm))
            block = es.enter_context(nc.Block())

            def run(eng_name):
                def body(e):
                    for o in self.ops[eng_name]:
                        for (k, v) in o.waits:
                            e.wait_ge(sems[k], v)
                        if o.fn is not None:
                            o.fn(e).then_inc(sems[o.semkey], o.incval)
                return body

            block.tensor(run("tensor"))
            block.vector(run("vector"))
            block.scalar(run("scalar"))
            block.gpsimd(run("gpsimd"))
            block.sync(run("sync"))


class _Stop(Exception):
    pass


import os as _os
_KSTOP = int(_os.environ.get("KSTOP", "99"))


def _ck(n):
    if _KSTOP <= n:
        raise _Stop()


def build_nc(TPC):
    NT = TPC // T
    NB = TPC // 128
    S = NR * TPC
    KG = T
    NKG = NR * NT
    KGC = KG // 128

    nc = bass.Bass("TRN2", target_bir_lowering=False)

    def din(name, shape, dt=F32):
        return nc.dram_tensor(name, list(shape), dt, kind="ExternalInput").ap()

    x_d = din("x", [TPC, D])
    wg_d = [din("ffn1_w_gate", [D, DFF]), din("ffn2_w_gate", [D, DFF])]
    wu_d = [din("ffn1_w_up", [D, DFF]), din("ffn2_w_up", [D, DFF])]
    wd_d = [din("ffn1_w_down", [DFF, D]), din("ffn2_w_down", [DFF, D])]
    gpre_d = [din("ffn1_pre_g", [1, D]), din("ffn2_pre_g", [1, D])]
    gpost_d = [din("ffn1_post_g", [1, D]), din("ffn2_post_g", [1, D])]
    gmixpre_d = din("mix_pre_g", [1, D])
    gmixpost_d = din("mix_post_g", [1, D])
    win_d = din("w_in", [D, WIN_COLS])
    wpa_d = din("w_proj_a", [1024, D])
    wpb_d = din("w_proj_b", [1024, D])
    wout_d = din("w_out", [D, D])
    gbias_d = din("gate_biasT", [128, 32])
    sink_d = din("sink_bc", [128, 8])
    lamv_d = din("lamv", [128, 4])
    subg_d = din("sublnT", [128, 2])
    ropeC_d = din("ropeC", [128, TPC])
    ropeS_d = din("ropeS", [128, TPC])
    amask_d = din("amask", [128, 8, 128], BF16)
    ident_d = din("ident", [128, 128], BF16)
    y_d = nc.dram_tensor("y", [TPC, D], F32, kind="ExternalOutput").ap()

    qsp_d = nc.dram_tensor("q_sp", [16 * 128, TPC], BF16).ap()
    x1sp_d = nc.dram_tensor("x1_sp", [TPC, D], F32).ap()
    def dint(name, shape):
        return nc.dram_tensor(name, list(shape), BF16).ap()
    kTA_l = [dint("kTA_l%d" % i, [768, T]) for i in range(NT)]
    kTB_l = [dint("kTB_l%d" % i, [512, T]) for i in range(NT)]
    vA_l = [dint("vA_l%d" % i, [T, 768]) for i in range(NT)]
    vB_l = [dint("vB_l%d" % i, [T, 512]) for i in range(NT)]
    kTA_g = [dint("kTA_g%d" % i, [NR * 768, T]) for i in range(NT)]
    kTB_g = [dint("kTB_g%d" % i, [NR * 512, T]) for i in range(NT)]
    vA_g = [dint("vA_g%d" % i, [NR * T, 768]) for i in range(NT)]
    vB_g = [dint("vB_g%d" % i, [NR * T, 512]) for i in range(NT)]
    p = Prog(nc)
    es = ExitStack()
    with es:
        def sb(name, shape, dt):
            return es.enter_context(nc.sbuf_tensor(name, list(shape), dt))

        x_sb = sb("x_sb", [128, 4, D], F32)
        regA = sb("regA", [128, 24576], BF16)
        regB = sb("regB", [128, 16384], BF16)
        wbuf = [sb("wbuf%d" % i, [128, 16, 256], BF16) for i in range(4)]
        wdbuf = [sb("wdbuf%d" % i, [128, 4, 512], BF16) for i in range(2)]
        gbuf = sb("gbuf", [128, D], F32)
        xsb = [sb("xsb%d" % i, [128, D], BF16) for i in range(2)]
        sgj = sb("sgj", [128, 1024], F32)
        ptr = sb("ptr", [128, 2048], BF16)
        ropeC = sb("ropeC_sb", [128, T], F32)
        ropeS = sb("ropeS_sb", [128, T], F32)
        esink = sb("esink", [128, 8, 128], F32)
        amask = sb("amask_sb", [128, 8, 128], BF16)
        ident = sb("ident_sb", [128, 128], BF16)
        ones_b = sb("ones_b", [128, 128], BF16)
        ones_f = sb("ones_f", [128, 128], F32)
        stage = [sb("stage%d" % i, [128, T], BF16) for i in range(2)]
        vstage = sb("vstage", [128, 4, 256], BF16)
        gbias = sb("gbias", [128, 32], F32)
        subg = sb("subg", [128, 2], F32)
        lamv = sb("lamv_sb", [128, 4], F32)
        small = sb("small", [128, 64], F32)
        PS = [es.enter_context(nc.psum_tensor("ps%d" % i, [128, 512], F32)) for i in range(8)]

        aT = regA[:, 0:JF * T].rearrange("p (j t) -> p j t", t=T)
        qT = regA[:, 0:8192].rearrange("p (c t) -> p c t", t=T)
        oT = regA[:, 8192:16384].rearrange("p (c t) -> p c t", t=T)
        kvb = [regA[:, 16384 + i * 4096:16384 + (i + 1) * 4096] for i in range(2)]
        fb_mix = regA[:, 0:16384].bitcast(F32).rearrange("p (t d) -> p t d", d=D)
        gtmp = regA[:, 0:8192].bitcast(F32).rearrange("p (i t) -> p i t", t=T)
        hT = regB[:, 0:8192].rearrange("p (k t) -> p k t", t=T)
        mT = regB[:, 8192:16384].rearrange("p (k t) -> p k t", t=T)
        fb_ffn = regB[:].bitcast(F32).rearrange("p (t d) -> p t d", d=D)
        atmp = regB[:, 8192:16384].bitcast(F32).rearrange("p (i t) -> p i t", t=T)
        sg = [sgj[:, 0:512], sgj[:, 512:1024]]
        junk = sgj[:].bitcast(BF16)
        pt = [ptr[:, i * 512:(i + 1) * 512] for i in range(4)]
        rtmp = [ptr[:, 0:1024].bitcast(F32), ptr[:, 1024:2048].bitcast(F32)]

        AT_KEYS = ["A0", "A1", "kv0", "kv1"]
        ss = small[:, 0:4]
        ms = small[:, 4:8]
        sd = small[:, 8:12]
        rstd = small[:, 12:16]
        prod = small[:, 16:18]
        elam = small[:, 18:20]
        neglam = small[:, 20:21]
        esk = small[:, 24:32]
        subg_s = small[:, 32:34]

        wv = lambda w: w.rearrange("(k p) n -> p k n", p=128)

        st = {"wb": 0, "wd": 0, "xs": 0, "sg": 0, "stg": 0, "pt": 0, "kv": 0, "psT": 0}

        wbs = nc.dram_tensor("wbs", [160, 128, 4096], BF16).ap()
        wds = nc.dram_tensor("wds", [112, 128, 2048], BF16).ap()
        scr = {"wb": {}, "wd": {}}

        def load_wbuf(src_ap, k0=0, k1=16, buf=None, uid=None):
            if buf is None:
                buf = st["wb"]
                st["wb"] = (buf + 1) % 4
            keys = [("wb", buf, 0)] if k1 <= 8 else ([("wb", buf, 8)] if k0 >= 8 else [("wb", buf, 0), ("wb", buf, 8)])
            nk = k1 - k0
            dstv = wbuf[buf][:, k0:k1, :]
            if uid in scr["wb"]:
                sc = wbs[scr["wb"][uid], :, 0:nk * 256].rearrange("p (k c) -> p k c", c=256)
                p.dma("gpsimd", lambda e: e.dma_start(out=dstv, in_=sc), reads=[("wbs", uid)], writes=keys)
            else:
                p.dma("gpsimd", lambda e: e.dma_start(out=dstv, in_=src_ap), writes=keys)
                idx = len(scr["wb"])
                scr["wb"][uid] = idx
                sc = wbs[idx, :, 0:nk * 256].rearrange("p (k c) -> p k c", c=256)
                p.dma("sync", lambda e: e.dma_start(out=sc, in_=dstv), reads=keys, writes=[("wbs", uid)])
            return buf

        def load_wd(src_ap, uid=None):
            buf = st["wd"]
            st["wd"] = (buf + 1) % 2
            if uid in scr["wd"]:
                sc = wds[scr["wd"][uid]].rearrange("p (k c) -> p k c", c=512)
                p.dma("gpsimd", lambda e: e.dma_start(out=wdbuf[buf][:], in_=sc), reads=[("wds", uid)], writes=[("wd", buf)])
            else:
                p.dma("gpsimd", lambda e: e.dma_start(out=wdbuf[buf][:], in_=src_ap), writes=[("wd", buf)])
                idx = len(scr["wd"])
                scr["wd"][uid] = idx
                sc = wds[idx].rearrange("p (k c) -> p k c", c=512)
                p.dma("sync", lambda e: e.dma_start(out=sc, in_=wdbuf[buf][:]), reads=[("wd", buf)], writes=[("wds", uid)])
            return buf

        def mm_group(out_ap, pairs, reads, writes):
            n = len(pairs)

            def f(e):
                for i, (l, r) in enumerate(pairs):
                    ins = e.matmul(out_ap, lhsT=l, rhs=r, start=(i == 0), stop=(i == n - 1))
                return ins
            p.op("tensor", f, reads=reads, writes=writes)

        for dst, src, key in ((amask[:], amask_d, "amask"), (ident[:], ident_d, "ident"), (gbias[:], gbias_d, "gbias"),
                              (subg[:], subg_d, "subg"), (lamv[:], lamv_d, "lamv"), (esk, sink_d, "esk")):
            p.dma("sync", lambda e, dst=dst, src=src: e.dma_start(out=dst, in_=src), writes=[key])
        p.op("vector", lambda e: e.memset(ones_b[:], 1.0), writes=["ones_b"])
        p.op("vector", lambda e: e.memset(ones_f[:], 1.0), writes=["ones_f"])
        p.op("scalar", lambda e: e.activation(out=esk, in_=esk, func=AF.Exp), reads=["esk"], writes=["esk"])
        p.op("vector", lambda e: e.tensor_copy(out=esink[:], in_=esk.unsqueeze(2).broadcast_to([128, 8, 128])),
             reads=["esk"], writes=["esink"])
        p.op("vector", lambda e: e.tensor_tensor(out=prod, in0=lamv[:, 0:2], in1=lamv[:, 2:4], op=ALU.mult),
             reads=["lamv"], writes=["prod"])
        mm_group(PS[0][:, 0:2], [(ones_f[:], prod)], reads=["ones_f", "prod"], writes=[("ps", 0)])
        p.op("scalar", lambda e: e.activation(out=elam, in_=PS[0][:, 0:2], func=AF.Exp),
             reads=[("ps", 0)], writes=["elam"])
        p.op("vector", lambda e: e.tensor_tensor(out=neglam, in0=elam[:, 1:2], in1=elam[:, 0:1], op=ALU.subtract),
             reads=["elam"], writes=["neglam"])
        p.op("vector", lambda e: e.tensor_scalar(out=neglam, in0=neglam, scalar1=-LAM_INIT, scalar2=None, op0=ALU.add),
             reads=["neglam"], writes=["neglam"])
        p.op("vector", lambda e: e.tensor_scalar(out=subg_s, in0=subg[:], scalar1=1.0 - LAM_INIT, scalar2=None, op0=ALU.mult),
             reads=["subg"], writes=["subg_s"])

        def load_gain(g_d):
            p.dma("sync", lambda e: e.dma_start(out=gbuf[:], in_=g_d.broadcast_to([128, D])), writes=["gbuf"])

        def rows_rstd(srcs, read_keys):
            for t in range(4):
                p.op("scalar", lambda e, t=t: e.activation(out=junk, in_=srcs[t], func=AF.Square, accum_out=ss[:, t:t + 1]),
                     reads=[read_keys[t]], writes=[("ss", t), "sg0", "sg1"])
            p.op("vector", lambda e: e.tensor_scalar(out=ms, in0=ss, scalar1=1.0 / D, scalar2=EPS, op0=ALU.mult, op1=ALU.add),
                 reads=[("ss", t) for t in range(4)], writes=["ms"])
            p.op("scalar", lambda e: e.activation(out=sd, in_=ms, func=AF.Sqrt), reads=["ms"], writes=["sd"])
            p.op("vector", lambda e: e.reciprocal(out=rstd, in_=sd), reads=["sd"], writes=["rstd"])

        def norm_to_hT(g_d):
            load_gain(g_d)
            rows_rstd([x_sb[:, t, :] for t in range(4)], [("x", t) for t in range(4)])
            for t in range(4):
                xi = st["xs"]
                st["xs"] = 1 - xi
                p.op("vector", lambda e, t=t, xi=xi: e.scalar_tensor_tensor(
                    out=xsb[xi][:], in0=x_sb[:, t, :], scalar=rstd[:, t:t + 1], in1=gbuf[:], op0=ALU.mult, op1=ALU.mult),
                    reads=[("x", t), "rstd", "gbuf"], writes=[("xsb", xi)])
                for half in range(2):
                    b = 4 + st["psT"]
                    st["psT"] = (st["psT"] + 1) % 4
                    psv = PS[b][:].bitcast(BF16)

                    def tr(e, xi=xi, half=half, psv=psv):
                        for kk in range(8):
                            k = half * 8 + kk
                            ins = e.transpose(psv[:, kk * 128:(kk + 1) * 128], xsb[xi][:, k * 128:(k + 1) * 128], ident[:])
                        return ins
                    p.op("tensor", tr, reads=[("xsb", xi), "ident"], writes=[("ps", b)])
                    src = psv.rearrange("p (k t) -> p k t", t=128)
                    dst = hT[:, half * 8:(half + 1) * 8, t * 128:(t + 1) * 128]
                    p.op("scalar", lambda e, src=src, dst=dst: e.activation(out=dst, in_=src, func=AF.Copy),
                         reads=[("ps", b)], writes=["hT"])

        def ffn_stage1(wg, wu, wname):
            for j2 in range(JF // 2):
                bg = load_wbuf(wv(wg)[:, :, j2 * 256:(j2 + 1) * 256], uid=("g", wname, j2))
                bu = load_wbuf(wv(wu)[:, :, j2 * 256:(j2 + 1) * 256], uid=("u", wname, j2))
                for jj in range(2):
                    j = j2 * 2 + jj
                    pg, pu = (0, 1) if j % 2 == 0 else (2, 3)
                    mm_group(PS[pg][:], [(wbuf[bg][:, k, jj * 128:(jj + 1) * 128], hT[:, k, :]) for k in range(KC)],
                             reads=[("wb", bg, 0), ("wb", bg, 8), "hT"], writes=[("ps", pg)])
                    mm_group(PS[pu][:], [(wbuf[bu][:, k, jj * 128:(jj + 1) * 128], hT[:, k, :]) for k in range(KC)],
                             reads=[("wb", bu, 0), ("wb", bu, 8), "hT"], writes=[("ps", pu)])
                    si = st["sg"]
                    st["sg"] = 1 - si
                    p.op("scalar", lambda e, pg=pg, si=si: e.activation(out=sg[si], in_=PS[pg][:], func=AF.Silu),
                         reads=[("ps", pg)], writes=["sg%d" % si])
                    p.op("vector", lambda e, pu=pu, si=si, j=j: e.tensor_tensor(out=aT[:, j, :], in0=sg[si], in1=PS[pu][:], op=ALU.mult),
                         reads=[("ps", pu), "sg%d" % si], writes=[("aT", j)])

        def down_proj(src, src_key, w_d, nk, fb, fb_keys, wname):
            for n in range(4):
                banks = (4, 5, 6, 7) if n % 2 == 0 else (0, 1, 2, 3)
                ngr = nk // 4
                for kg in range(ngr):
                    b = load_wd(wv(w_d)[:, kg * 4:(kg + 1) * 4, n * 512:(n + 1) * 512], uid=(wname, n, kg))

                    def f(e, kg=kg, b=b, banks=banks, ngr=ngr):
                        for t in range(4):
                            for k in range(4):
                                ins = e.matmul(PS[banks[t]][:], lhsT=src[:, kg * 4 + k, t * 128:(t + 1) * 128], rhs=wdbuf[b][:, k, :],
                                               start=(kg == 0 and k == 0), stop=(kg == ngr - 1 and k == 3))
                        return ins
                    p.op("tensor", f, reads=[("wd", b)] + [src_key(kg * 4 + k) for k in range(4)],
                         writes=[("ps", bk) for bk in banks])
                for t in range(4):
                    dst = fb[:, t, n * 512:(n + 1) * 512]
                    if t % 2 == 0:
                        p.op("vector", lambda e, dst=dst, bk=banks[t]: e.tensor_copy(out=dst, in_=PS[bk][:]),
                             reads=[("ps", banks[t])], writes=[fb_keys[t]])
                    else:
                        p.op("scalar", lambda e, dst=dst, bk=banks[t]: e.activation(out=dst, in_=PS[bk][:], func=AF.Copy),
                             reads=[("ps", banks[t])], writes=[fb_keys[t]])

        def post_norm_res(fb, fb_keys, g_d, factor):
            load_gain(g_d)
            rows_rstd([fb[:, t, :] for t in range(4)], fb_keys)
            for t in range(4):
                p.op("vector", lambda e, t=t: e.scalar_tensor_tensor(
                    out=fb[:, t, :], in0=fb[:, t, :], scalar=rstd[:, t:t + 1], in1=gbuf[:], op0=ALU.mult, op1=ALU.mult),
                    reads=[fb_keys[t], "rstd", "gbuf"], writes=[fb_keys[t]])
                p.op("vector", lambda e, t=t: e.scalar_tensor_tensor(
                    out=x_sb[:, t, :], in0=fb[:, t, :], scalar=float(factor), in1=x_sb[:, t, :], op0=ALU.mult, op1=ALU.add),
                    reads=[fb_keys[t], ("x", t)], writes=[("x", t)])

        FB_FFN_KEYS = ["hT", "hT", "mT", "mT"]
        FB_MIX_KEYS = ["A0", "A0", "A1", "A1"]

        def ffn(l):
            norm_to_hT(gpre_d[l])
            p.alias(["A0", "A1", "kv0", "kv1"], [("aT", j) for j in range(JF)])
            ffn_stage1(wg_d[l], wu_d[l], l)
            down_proj(aT, lambda j: ("aT", j), wd_d[l], JF, fb_ffn, FB_FFN_KEYS, ("d", l))
            p.alias([("aT", j) for j in range(JF)], ["A0", "A1", "kv0", "kv1"])
            post_norm_res(fb_ffn, FB_FFN_KEYS, gpost_d[l], 0.5)

        def load_x(src_d, i, rkey=None):
            for t in range(4):
                r0 = i * T + t * 128
                p.dma("sync", lambda e, t=t, r0=r0: e.dma_start(out=x_sb[:, t, :], in_=src_d[r0:r0 + 128, :]),
                      reads=[(rkey, i, t)] if rkey else [], writes=[("x", t)])

        def store_x(dst_d, i, key):
            for t in range(4):
                r0 = i * T + t * 128
                p.dma("sync", lambda e, t=t, r0=r0: e.dma_start(out=dst_d[r0:r0 + 128, :], in_=x_sb[:, t, :]),
                      reads=[("x", t)], writes=[(key, i, t)])

        def qkv(i):
            c0 = i * T
            p.dma("sync", lambda e: e.dma_start(out=ropeC[:], in_=ropeC_d[:, c0:c0 + T]), writes=["ropeC"])
            p.dma("sync", lambda e: e.dma_start(out=ropeS[:], in_=ropeS_d[:, c0:c0 + T]), writes=["ropeS"])
            fm = []
            for c in range(8):
                fm.append((c * 128, qsp_d[c * 128:(c + 1) * 128, c0:c0 + T], ("qsp", i)))
            for g in range(2):
                fm.append((1024 + g * 128, kTA_l[i][g * 128:(g + 1) * 128, :], ("kTA", i, g)))
            for c in range(8):
                fm.append((1536 + c * 128, qsp_d[(8 + c) * 128:(9 + c) * 128, c0:c0 + T], ("qsp", i)))
            for c in range(8):
                if c < 4:
                    fm.append((2560 + c * 128, kTA_l[i][(2 + c) * 128:(3 + c) * 128, :], ("kTA", i, 2 + c)))
                else:
                    fm.append((2560 + c * 128, kTB_l[i][(c - 4) * 128:(c - 3) * 128, :], ("kTB", i, c - 4)))
            for pr in range(len(fm) // 2):
                col0 = fm[2 * pr][0]
                b = load_wbuf(wv(win_d)[:, :, col0:col0 + 256], uid=("in", col0))
                for jj in range(2):
                    _, dst_d, dkey = fm[2 * pr + jj]
                    bk = (2 * pr + jj) % 4
                    mm_group(PS[bk][:], [(wbuf[b][:, k, jj * 128:(jj + 1) * 128], hT[:, k, :]) for k in range(KC)],
                             reads=[("wb", b, 0), ("wb", b, 8), "hT"], writes=[("ps", bk)])
                    p.op("vector", lambda e, bk=bk: e.tensor_tensor(out=rtmp[0], in0=PS[bk][:], in1=ropeC[:], op=ALU.mult),
                         reads=[("ps", bk), "ropeC"], writes=["pt0", "pt1"])
                    p.op("vector", lambda e, bk=bk: e.tensor_tensor(out=rtmp[1][0:64, :], in0=PS[bk][64:128, :], in1=ropeS[0:64, :], op=ALU.mult),
                         reads=[("ps", bk), "ropeS"], writes=["pt2"])
                    p.op("vector", lambda e, bk=bk: e.tensor_tensor(out=rtmp[1][64:128, :], in0=PS[bk][0:64, :], in1=ropeS[64:128, :], op=ALU.mult),
                         reads=[("ps", bk), "ropeS"], writes=["pt3"])
                    si = st["stg"]
                    st["stg"] = 1 - si
                    p.op("vector", lambda e, si=si: e.tensor_tensor(out=stage[si][:], in0=rtmp[0], in1=rtmp[1], op=ALU.add),
                         reads=["pt0", "pt1", "pt2", "pt3"], writes=[("stage", si)])
                    p.dma("sync", lambda e, si=si, dst_d=dst_d: e.dma_start(out=dst_d, in_=stage[si][:]),
                          reads=[("stage", si)], writes=[dkey if dkey[0] != "qsp" else ("qsp", i, 2 * pr + jj)])
            vs = [(1280, vA_l[i], 0, "vA"), (3584, vA_l[i], 256, "vA"), (3840, vA_l[i], 512, "vA"),
                  (4096, vB_l[i], 0, "vB"), (4352, vB_l[i], 256, "vB")]
            for (col0, vdst, vc0, vkey) in vs:
                b = load_wbuf(wv(win_d)[:, :, col0:col0 + 256], uid=("in", col0))
                for t in range(4):
                    bk = 4 + t
                    mm_group(PS[bk][:, 0:256],
                             [(hT[:, k, t * 128:(t + 1) * 128], wbuf[b][:, k, :]) for k in range(KC)],
                             reads=[("wb", b, 0), ("wb", b, 8), "hT"], writes=[("ps", bk)])
                    p.op("scalar", lambda e, bk=bk, t=t: e.activation(out=vstage[:, t, :], in_=PS[bk][:, 0:256], func=AF.Copy),
                         reads=[("ps", bk)], writes=["vstage"])
                dst = vdst[:, vc0:vc0 + 256].rearrange("(t p) c -> p t c", p=128)
                p.dma("sync", lambda e, dst=dst: e.dma_start(out=dst, in_=vstage[:]), reads=["vstage"], writes=[(vkey, i, vc0)])

        def run_all():
            _ck(1)
            for i in range(NT):
                load_x(x_d, i)
                norm_to_hT(gpre_d[0]) if _KSTOP == 2 else None
                _ck(2)
                ffn(0)
                _ck(3)
                store_x(x1sp_d, i, "x1sp")
                norm_to_hT(gmixpre_d)
                qkv(i)
            _ck(4)

            groups = [list(range(g * NR, (g + 1) * NR)) for g in range(NCORES // NR)]
            kTA_keys = lambda i: [("kTA", i, c) for c in range(6)]
            kTB_keys = lambda i: [("kTB", i, c) for c in range(4)]
            vA_keys = lambda i: [("vA", i, c) for c in (0, 256, 512)]
            vB_keys = lambda i: [("vB", i, c) for c in (0, 256)]
            for i in range(NT):
                for (src, dst, rk, wk) in ((kTA_l[i], kTA_g[i], kTA_keys(i), ("kTAg", i)), (kTB_l[i], kTB_g[i], kTB_keys(i), ("kTBg", i)),
                                           (vA_l[i], vA_g[i], vA_keys(i), ("vAg", i)), (vB_l[i], vB_g[i], vB_keys(i), ("vBg", i))):
                    p.cc(lambda e, src=src, dst=dst: e.collective_compute(
                        "AllGather", ALU.bypass, replica_groups=groups, ins=[src], outs=[dst]), rk, [wk], 4 * NT)
            _ck(5)

            def next_pt():
                i = st["pt"]
                st["pt"] = (i + 1) % 4
                return i

            def attn_b(i):
                for h in range(4):
                    units = [(kg, comp, c) for kg in range(NKG) for comp in range(2) for c in range(KGC)]
                    U = len(units)
                    loaded = {}
                    ptidx = {}
                    acc_cnt = [0, 0]

                    def ensure_loaded(kg, h=h, loaded=loaded):
                        if kg in loaded:
                            return loaded[kg]
                        r, ti = kg // NT, kg % NT
                        kb = st["kv"]
                        st["kv"] = (kb + 1) % 4
                        base = kvb[kb // 2][:, (kb % 2) * 2048:(kb % 2 + 1) * 2048]
                        kbuf = base[:, 0:2 * KG].rearrange("p (c k) -> p c k", k=KG)
                        vbuf = base[:, 2 * KG:2 * KG + KGC * 256].rearrange("p (c e) -> p c e", e=256)
                        if h < 2:
                            row0 = r * 768 + (2 + h * 2) * 128
                            ksrc = kTA_g[ti][row0:row0 + 256, :].rearrange("(c p) k -> p c k", p=128)
                            vsrc = vA_g[ti][r * T:(r + 1) * T, 256 + h * 256:512 + h * 256].rearrange("(c p) e -> p c e", p=128)
                            kkey, vkey = ("kTAg", ti), ("vAg", ti)
                        else:
                            row0 = r * 512 + (h - 2) * 256
                            ksrc = kTB_g[ti][row0:row0 + 256, :].rearrange("(c p) k -> p c k", p=128)
                            vsrc = vB_g[ti][r * T:(r + 1) * T, (h - 2) * 256:(h - 1) * 256].rearrange("(c p) e -> p c e", p=128)
                            kkey, vkey = ("kTBg", ti), ("vBg", ti)
                        p.dma("sync", lambda e, kbuf=kbuf, ksrc=ksrc: e.dma_start(out=kbuf, in_=ksrc),
                              reads=[kkey], writes=[("kbK", kb)])
                        p.dma("sync", lambda e, vbuf=vbuf, vsrc=vsrc: e.dma_start(out=vbuf, in_=vsrc),
                              reads=[vkey], writes=[("kbV", kb)])
                        loaded[kg] = (kb, kbuf, vbuf)
                        return loaded[kg]

                    def S_(u, h=h):
                        kg, comp, c = units[u]
                        kb, kbuf, vbuf = ensure_loaded(kg)
                        sbk = u % 2
                        mm_group(PS[sbk][:], [(kbuf[:, comp, c * 128:(c + 1) * 128], qT[:, 8 + h * 2 + comp, :])],
                                 reads=[("kbK", kb), "qT"], writes=[("ps", sbk)])

                    def E_(u, ptidx=ptidx):
                        sbk = u % 2
                        pi = next_pt()
                        ptidx[u] = pi
                        p.op("scalar", lambda e, sbk=sbk, pi=pi: e.activation(out=pt[pi], in_=PS[sbk][:], func=AF.Exp, scale=SCALE),
                             reads=[("ps", sbk)], writes=["pt%d" % pi])

                    def PV_(u, ptidx=ptidx):
                        kg, comp, c = units[u]
                        kb, kbuf, vbuf = ensure_loaded(kg)
                        pi = ptidx[u]
                        first = (kg == 0 and c == 0)
                        last = (kg == NKG - 1 and c == KGC - 1)
                        ob = 2 + comp * 2

                        def f(e, vbuf=vbuf, c=c, pi=pi, ob=ob, comp=comp, first=first, last=last):
                            e.matmul(PS[ob][:], lhsT=vbuf[:, c, 0:128], rhs=pt[pi], start=first, stop=last)
                            return e.matmul(PS[ob + 1][:], lhsT=vbuf[:, c, 128:256], rhs=pt[pi], start=first, stop=last)
                        p.op("tensor", f, reads=[("kbV", kb), "pt%d" % pi], writes=[("ps", ob), ("ps", ob + 1)])
                        k_ = acc_cnt[comp]
                        acc_cnt[comp] += 1
                        on_pool = (k_ % 3 == 2)
                        eng = "gpsimd" if on_pool else "vector"
                        acc = atmp[:, 5 + comp, :] if on_pool else sg[comp]
                        akey = ("atmp", 5 + comp) if on_pool else "sg%d" % comp
                        if k_ == 0 or k_ == 2:
                            p.op(eng, lambda e, acc=acc, pi=pi: e.tensor_copy(out=acc, in_=pt[pi]),
                                 reads=["pt%d" % pi], writes=[akey])
                        else:
                            p.op(eng, lambda e, acc=acc, pi=pi: e.tensor_tensor(out=acc, in0=acc, in1=pt[pi], op=ALU.add),
                                 reads=["pt%d" % pi, akey], writes=[akey])

                    S_(0)
                    S_(1)
                    E_(0)
                    for u in range(U):
                        PV_(u)
                        if u + 2 < U:
                            S_(u + 2)
                        if u + 1 < U:
                            E_(u + 1)
                    r1, r2, ta, o0, o1, sq0, sq1, rn = [atmp[:, q, :] for q in range(8)]
                    K = lambda q: ("atmp", q)
                    mm_group(PS[6][:], [(ones_f[:], sg[0]), (ones_f[:], atmp[:, 5, :])], reads=["ones_f", "sg0", ("atmp", 5)], writes=[("ps", 6)])
                    mm_group(PS[7][:], [(ones_f[:], sg[1]), (ones_f[:], atmp[:, 6, :])], reads=["ones_f", "sg1", ("atmp", 6)], writes=[("ps", 7)])
                    p.op("vector", lambda e: e.reciprocal(out=r1, in_=PS[6][:]), reads=[("ps", 6)], writes=[K(0)])
                    p.op("vector", lambda e: e.reciprocal(out=r2, in_=PS[7][:]), reads=[("ps", 7)], writes=[K(1)])
                    p.op("vector", lambda e: e.tensor_scalar(out=r2, in0=r2, scalar1=neglam, scalar2=None, op0=ALU.mult),
                         reads=[K(1), "neglam"], writes=[K(1)])
                    for ec, oo, sq in ((0, o0, sq0), (1, o1, sq1)):
                        p.op("vector", lambda e, ec=ec: e.tensor_tensor(out=ta, in0=PS[2 + ec][:], in1=r1, op=ALU.mult),
                             reads=[("ps", 2 + ec), K(0)], writes=[K(2)])
                        p.op("vector", lambda e, ec=ec, oo=oo: e.tensor_tensor(out=oo, in0=PS[4 + ec][:], in1=r2, op=ALU.mult),
                             reads=[("ps", 4 + ec), K(1)], writes=[K(3 + ec)])
                        p.op("vector", lambda e, oo=oo: e.tensor_tensor(out=oo, in0=oo, in1=ta, op=ALU.add),
                             reads=[K(2), K(3 + ec)], writes=[K(3 + ec)])
                        p.op("scalar", lambda e, oo=oo, sq=sq: e.activation(out=sq, in_=oo, func=AF.Square),
                             reads=[K(3 + ec)], writes=[K(5 + ec)])
                    mm_group(PS[0][:], [(ones_f[:], sq0), (ones_f[:], sq1)], reads=["ones_f", K(5), K(6)], writes=[("ps", 0)])
                    p.op("vector", lambda e: e.tensor_scalar(out=rn, in0=PS[0][:], scalar1=1.0 / 256.0, scalar2=EPS, op0=ALU.mult, op1=ALU.add),
                         reads=[("ps", 0)], writes=[K(7)])
                    p.op("scalar", lambda e: e.activation(out=rn, in_=rn, func=AF.Sqrt), reads=[K(7)], writes=[K(7)])
                    p.op("vector", lambda e: e.reciprocal(out=rn, in_=rn), reads=[K(7)], writes=[K(7)])
                    for ec, oo in ((0, o0), (1, o1)):
                        p.op("vector", lambda e, ec=ec, oo=oo, h=h: e.scalar_tensor_tensor(
                            out=oT[:, 8 + h * 2 + ec, :], in0=oo, scalar=subg_s[:, ec:ec + 1], in1=rn, op0=ALU.mult, op1=ALU.mult),
                            reads=[K(3 + ec), K(7), "subg_s"], writes=[("oT", 8 + h * 2 + ec)])

            def attn_a(i):
                n0 = i * 4
                lo = max(n0 - 1, 0)
                hi = min(n0 + 4, NB - 1)
                nblk = hi - lo + 1
                kab = kvb[0][:, 0:2 * 768].rearrange("p (g k) -> p g k", k=768)
                vab = kvb[0][:, 1536:1536 + 6 * 256].rearrange("p (b e) -> p b e", e=256)
                kcb = kvb[1][:, 0:2 * 768].rearrange("p (g k) -> p g k", k=768)
                vcb = kvb[1][:, 1536:1536 + 6 * 256].rearrange("p (b e) -> p b e", e=256)
                for m in range(lo, hi + 1):
                    ti, bi = m // 4, m % 4
                    p.dma("sync", lambda e, m=m, ti=ti, bi=bi: e.dma_start(
                        out=kab[:, :, (m - lo) * 128:(m - lo + 1) * 128],
                        in_=kTA_l[ti][0:256, bi * 128:(bi + 1) * 128].rearrange("(g p) k -> p g k", p=128)),
                        reads=kTA_keys(ti), writes=[("kvK", 0), ("kvV", 0)])
                    p.dma("sync", lambda e, m=m, ti=ti, bi=bi: e.dma_start(
                        out=vab[:, m - lo, :], in_=vA_l[ti][bi * 128:(bi + 1) * 128, 0:256]),
                        reads=vA_keys(ti), writes=[("kvK", 0), ("kvV", 0)])
                cands = []
                if i == 0:
                    cands += [(s_, r, "prev") for s_, r in enumerate((0, 1, 2))]
                if i == NT - 1:
                    cands += [(3 + s_, r, "next") for s_, r in enumerate((1, 2, 3))]
                for (slot, r, which) in cands:
                    ti = NT - 1 if which == "prev" else 0
                    col0 = T - 128 if which == "prev" else 0
                    p.dma("sync", lambda e, slot=slot, r=r, col0=col0, ti=ti: e.dma_start(
                        out=kcb[:, :, slot * 128:(slot + 1) * 128],
                        in_=kTA_g[ti][r * 768:r * 768 + 256, col0:col0 + 128].rearrange("(g p) k -> p g k", p=128)),
                        reads=[("kTAg", ti)], writes=[("kvK", 1), ("kvV", 1)])
                    p.dma("sync", lambda e, slot=slot, r=r, col0=col0, ti=ti: e.dma_start(
                        out=vcb[:, slot, :], in_=vA_g[ti][r * T + col0:r * T + col0 + 128, 0:256]),
                        reads=[("vAg", ti)], writes=[("kvK", 1), ("kvV", 1)])
                units = []
                for nb in range(4):
                    n = n0 + nb
                    for g in range(2):
                        chunks = []
                        own = lambda m: (kab[:, g, (m - lo) * 128:(m - lo + 1) * 128], vab[:, m - lo, g * 128:(g + 1) * 128], 0)
                        cnd = lambda slot: (kcb[:, g, slot * 128:(slot + 1) * 128], vcb[:, slot, g * 128:(g + 1) * 128], 1)
                        if n == 0:
                            for s_ in range(3):
                                chunks.append(cnd(s_) + (2 + s_,))
                        else:
                            chunks.append(own(n - 1) + (0,))
                        chunks.append(own(n) + (None,))
                        if n == NB - 1:
                            for s_ in range(3):
                                chunks.append(cnd(3 + s_) + (5 + s_,))
                        else:
                            chunks.append(own(n + 1) + (1,))
                        qv = qT[:, g * 4:(g + 1) * 4, nb * 128:(nb + 1) * 128]
                        idx = nb * 2 + g
                        for ci, (kap, vap, which, mi) in enumerate(chunks):
                            units.append(dict(kap=kap, vap=vap, which=which, mi=mi, qv=qv, ob=2 + idx % 2, sb=4 + idx % 2,
                                              first=(ci == 0), last=(ci == len(chunks) - 1), nb=nb, g=g, idx=idx))
                U = len(units)
                ptidx = {}

                def S_(u):
                    d = units[u]
                    sbk = u % 2
                    mm_group(PS[sbk][:], [(d["kap"], d["qv"])], reads=[("kvK", d["which"]), ("kvV", d["which"]), "qT"], writes=[("ps", sbk)])

                def E_(u):
                    d = units[u]
                    sbk = u % 2
                    pi = next_pt()
                    ptidx[u] = pi
                    p.op("scalar", lambda e, sbk=sbk, pi=pi: e.activation(out=pt[pi], in_=PS[sbk][:], func=AF.Exp, scale=SCALE),
                         reads=[("ps", sbk)], writes=["pt%d" % pi])
                    if d["mi"] is not None:
                        mi = d["mi"]
                        ptv = pt[pi].rearrange("p (r q) -> p r q", q=128)
                        mv = amask[:, mi:mi + 1, :].broadcast_to([128, 4, 128])
                        p.op("vector", lambda e, ptv=ptv, mv=mv: e.tensor_tensor(out=ptv, in0=ptv, in1=mv, op=ALU.mult),
                             reads=["pt%d" % pi, "amask"], writes=["pt%d" % pi])

                def PV_(u):
                    d = units[u]
                    pi = ptidx[u]
                    ob, sb_, g, nb = d["ob"], d["sb"], d["g"], d["nb"]

                    def f(e, vap=d["vap"], pi=pi, ob=ob, sb_=sb_, first=d["first"], last=d["last"]):
                        e.matmul(PS[ob][:], lhsT=vap, rhs=pt[pi], start=first, stop=last)
                        return e.matmul(PS[sb_][:], lhsT=ones_b[:], rhs=pt[pi], start=first, stop=last)
                    p.op("tensor", f, reads=[("kvK", d["which"]), ("kvV", d["which"]), "pt%d" % pi, "ones_b"], writes=[("ps", ob), ("ps", sb_)])
                    if d["last"]:
                        den = atmp[:, d["idx"] % 2, :]
                        dk = ("atmp", d["idx"] % 2)
                        p.op("vector", lambda e, den=den, sb_=sb_, g=g: e.tensor_tensor(
                            out=den, in0=PS[sb_][:], in1=esink[:, g * 4:(g + 1) * 4, :].rearrange("p r q -> p (r q)"), op=ALU.add),
                            reads=[("ps", sb_), "esink"], writes=[dk])
                        p.op("vector", lambda e, den=den: e.reciprocal(out=den, in_=den), reads=[dk], writes=[dk])
                        dst = oT[:, g * 4:(g + 1) * 4, nb * 128:(nb + 1) * 128]
                        p.op("vector", lambda e, dst=dst, ob=ob, den=den: e.tensor_tensor(
                            out=dst, in0=PS[ob][:].rearrange("p (r q) -> p r q", q=128), in1=den.rearrange("p (r q) -> p r q", q=128), op=ALU.mult),
                            reads=[("ps", ob), dk], writes=[("oT", g * 4 + r_) for r_ in range(4)])

                S_(0)
                S_(1)
                E_(0)
                for u in range(U):
                    PV_(u)
                    if u + 2 < U:
                        S_(u + 2)
                    if u + 1 < U:
                        E_(u + 1)

            def gates_merge(i):
                p.alias(["qT"], [("gt", q) for q in range(8)])
                oT_keys = [("oT", c) for c in range(16)]
                for j2 in range(8):
                    ba = load_wbuf(wv(win_d)[:, :, 4608 + j2 * 256:4608 + (j2 + 1) * 256], uid=("in", 4608 + j2 * 256))
                    bb = load_wbuf(wv(win_d)[:, :, 6656 + j2 * 256:6656 + (j2 + 1) * 256], uid=("in", 6656 + j2 * 256))
                    bp = load_wbuf(wv(wpa_d)[:, :, j2 * 256:(j2 + 1) * 256], 0, 8, uid=("pa", j2))
                    load_wbuf(wv(wpb_d)[:, :, j2 * 256:(j2 + 1) * 256], 8, 16, buf=bp, uid=("pb", j2))
                    for jj in range(2):
                        j = j2 * 2 + jj
                        bs = (0, 1, 2, 3) if j % 2 == 0 else (4, 5, 6, 7)
                        cs = slice(jj * 128, (jj + 1) * 128)
                        mm_group(PS[bs[0]][:], [(wbuf[ba][:, k, cs], hT[:, k, :]) for k in range(KC)],
                                 reads=[("wb", ba, 0), ("wb", ba, 8), "hT"], writes=[("ps", bs[0])])
                        mm_group(PS[bs[1]][:], [(wbuf[bb][:, k, cs], hT[:, k, :]) for k in range(KC)],
                                 reads=[("wb", bb, 0), ("wb", bb, 8), "hT"], writes=[("ps", bs[1])])
                        mm_group(PS[bs[2]][:], [(wbuf[bp][:, k, cs], oT[:, k, :]) for k in range(8)],
                                 reads=[("wb", bp, 0)] + oT_keys, writes=[("ps", bs[2])])
                        mm_group(PS[bs[3]][:], [(wbuf[bp][:, 8 + k, cs], oT[:, 8 + k, :]) for k in range(8)],
                                 reads=[("wb", bp, 8)] + oT_keys, writes=[("ps", bs[3])])
                        q0 = (j % 2) * 4
                        ga, gb_, ta, tb = [gtmp[:, q0 + q, :] for q in range(4)]
                        GK = lambda q: ("gt", q0 + q)
                        p.op("scalar", lambda e, ga=ga, b0=bs[0], j=j: e.activation(out=ga, in_=PS[b0][:], func=AF.Sigmoid, bias=gbias[:, j:j + 1]),
                             reads=[("ps", bs[0]), "gbias"], writes=[GK(0)])
                        p.op("scalar", lambda e, gb_=gb_, b1=bs[1], j=j: e.activation(out=gb_, in_=PS[b1][:], func=AF.Sigmoid, bias=gbias[:, 16 + j:17 + j]),
                             reads=[("ps", bs[1]), "gbias"], writes=[GK(1)])
                        p.op("vector", lambda e, ga=ga, ta=ta, b2=bs[2]: e.tensor_tensor(out=ta, in0=PS[b2][:], in1=ga, op=ALU.mult),
                             reads=[("ps", bs[2]), GK(0)], writes=[GK(2)])
                        p.op("vector", lambda e, gb_=gb_, tb=tb, b3=bs[3]: e.tensor_tensor(out=tb, in0=PS[b3][:], in1=gb_, op=ALU.mult),
                             reads=[("ps", bs[3]), GK(1)], writes=[GK(3)])
                        p.op("vector", lambda e, ta=ta, tb=tb, j=j: e.tensor_tensor(out=mT[:, j, :], in0=ta, in1=tb, op=ALU.add),
                             reads=[GK(2), GK(3)], writes=[("mTc", j)])

            for i in range(NT):
                c0 = i * T
                load_x(x1sp_d, i, "x1sp")
                norm_to_hT(gmixpre_d)
                p.alias(["A0", "A1"], ["qT"] + [("oT", c) for c in range(16)])
                p.alias(["kv0", "kv1"], [("kbK", j) for j in range(4)] + [("kbV", j) for j in range(4)])
                p.alias(["mT"] + [("mTc", j) for j in range(16)], [("atmp", q) for q in range(8)])
                p.dma("sync", lambda e, c0=c0: e.dma_start(out=qT, in_=qsp_d[:, c0:c0 + T].rearrange("(c p) t -> p c t", p=128)),
                      reads=[("qsp", i, c) for c in range(26)], writes=["qT"])
                attn_b(i)
                p.alias([("kbK", j) for j in range(4)] + [("kbV", j) for j in range(4)], [("kvK", 0), ("kvV", 0), ("kvK", 1), ("kvV", 1)])
                _ck(6)
                attn_a(i)
                _ck(7)
                p.alias([("atmp", q) for q in range(8)], [("mTc", j) for j in range(16)])
                gates_merge(i)
                _ck(8)
                p.alias(["qT"] + [("gt", q) for q in range(8)] + [("oT", c) for c in range(16)], ["A0", "A1"])
                p.alias([("kvK", 0), ("kvV", 0), ("kvK", 1), ("kvV", 1)] + [("kbK", j) for j in range(4)] + [("kbV", j) for j in range(4)], ["kv0", "kv1"])
                down_proj(mT, lambda k: ("mTc", k), wout_d, KC, fb_mix, FB_MIX_KEYS, "wout")
                post_norm_res(fb_mix, FB_MIX_KEYS, gmixpost_d, 1.0)
                p.alias([("mTc", j) for j in range(16)], ["mT"])
                ffn(1)
                store_x(y_d, i, "y")

        try:
            run_all()
        except _Stop:
            pass
        p.wait_all("sync", list(p.last_w.keys()))
        p.emit()
    return nc


def _host_consts(TPC, rank):
    pos = (rank * TPC + np.arange(TPC)).astype(np.float32)
    inv = (np.float32(10000.0) ** (-np.arange(0, HD, 2, dtype=np.float32) / np.float32(HD))).astype(np.float32)
    ang = (pos[None, :] * inv[:, None]).astype(np.float32)
    c = np.cos(ang.astype(np.float64)).astype(np.float32)
    s = np.sin(ang.astype(np.float64)).astype(np.float32)
    ropeC = np.concatenate([c, c], 0)
    ropeS = np.concatenate([-s, s], 0)
    j = np.arange(128)[:, None]
    i = np.arange(128)[None, :]
    tri_prev = (j >= i).astype(np.float32)
    tri_next = (j <= i).astype(np.float32)
    am = np.zeros((128, 8, 128), np.float32)
    am[:, 0] = tri_prev
    am[:, 1] = tri_next
    for s_, r in enumerate((0, 1, 2)):
        if r == rank - 1:
            am[:, 2 + s_] = tri_prev
    for s_, r in enumerate((1, 2, 3)):
        if r == rank + 1:
            am[:, 5 + s_] = tri_next
    return ropeC, ropeS, am.astype(ml_dtypes.bfloat16), np.eye(128, dtype=np.float32).astype(ml_dtypes.bfloat16)


def make_in_maps(inputs, TPC):
    x = np.asarray(inputs["x"], np.float32)
    xf = x.reshape(-1, D)
    f = lambda k: np.ascontiguousarray(np.asarray(inputs[k], np.float32)[0])
    shared = {k: f(k) for k in ("ffn1_w_gate", "ffn1_w_up", "ffn1_w_down", "ffn2_w_gate", "ffn2_w_up", "ffn2_w_down",
                                 "w_in", "w_proj_a", "w_proj_b", "w_out")}
    for k in ("ffn1_pre_g", "ffn1_post_g", "ffn2_pre_g", "ffn2_post_g", "mix_pre_g", "mix_post_g"):
        shared[k] = np.ascontiguousarray(np.asarray(inputs[k], np.float32).reshape(1, D))
    shared["gate_biasT"] = np.ascontiguousarray(f("gate_bias").reshape(32, 128).T)
    shared["sink_bc"] = np.ascontiguousarray(np.broadcast_to(f("sink_logit").reshape(1, 8), (128, 8)))
    shared["lamv"] = np.ascontiguousarray(np.stack([f("lambda_q1"), f("lambda_q2"), f("lambda_k1"), f("lambda_k2")], 1))
    shared["sublnT"] = np.ascontiguousarray(f("subln_g").reshape(2, 128).T)
    in_maps = []
    for c in range(NCORES):
        rank = c % NR
        ropeC, ropeS, am, ident = _host_consts(TPC, rank)
        m = dict(shared)
        m["x"] = np.ascontiguousarray(xf[c * TPC:(c + 1) * TPC])
        m["ropeC"], m["ropeS"], m["amask"], m["ident"] = ropeC, ropeS, am, ident
        in_maps.append(m)
    return in_maps


_NC_CACHE = {}


def kernel(**inputs):
    x = np.asarray(inputs["x"])
    B, S_, _ = x.shape
    TPC = (B * S_) // NCORES
    if TPC not in _NC_CACHE:
        _NC_CACHE[TPC] = build_nc(TPC)
    nc = _NC_CACHE[TPC]
    in_maps = make_in_maps(inputs, TPC)
    res = run_bass_kernel_spmd(nc, in_maps, core_ids=list(range(NCORES)))
    y = np.concatenate([np.asarray(r["y"], np.float32) for r in res.results], 0)
    return y.reshape(B, S_, D)
```

```python
import math
from contextlib import ExitStack

import numpy as np
import ml_dtypes
import concourse.bass as bass
import concourse.mybir as mybir
from concourse.bass_utils import run_bass_kernel_spmd

F32 = mybir.dt.float32
BF16 = mybir.dt.bfloat16
AF = mybir.ActivationFunctionType
ALU = mybir.AluOpType

NCORES = 8
NR = 4
D = 2048
DFF = 5632
HD = 128
WIN_COLS = 8704
EPS = 1e-6
LAM_INIT = 0.8 - 0.6 * math.exp(-0.3 * 0)
T = 512
KC = D // 128
JF = DFF // 128
SCALE = HD ** -0.5

ENGINES = ("tensor", "vector", "scalar", "gpsimd", "sync")
N_DMA_SEMS = 12


class _Op:
    __slots__ = ("fn", "waits", "semkey", "incval")

    def __init__(self, fn, waits, semkey, incval):
        self.fn = fn
        self.waits = waits
        self.semkey = semkey
        self.incval = incval


class Prog:
    def __init__(self, nc):
        self.nc = nc
        self.ops = {e: [] for e in ENGINES}
        self.cnt = {e: 0 for e in ENGINES}
        self.dma_rr = {"sync": 0, "gpsimd": 0}
        self.dma_cnt = {}
        self.last_w = {}
        self.readers = {}
        self.known = {e: {} for e in ENGINES}
        self.semkeys = [e for e in ENGINES if e != "sync"] + ["cc"]
        for q in ("sync", "gpsimd"):
            for i in range(N_DMA_SEMS):
                self.semkeys.append(("dma", q, i))
                self.dma_cnt[("dma", q, i)] = 0

    def _deps(self, eng, reads, writes):
        need = {}

        def add(tok):
            if tok is None:
                return
            k, v = tok
            if eng == "tensor" and k == "tensor":
                return
            if need.get(k, 0) < v:
                need[k] = v

        for r in reads:
            add(self.last_w.get(r))
        for w in writes:
            add(self.last_w.get(w))
            for t in self.readers.get(w, ()):
                add(t)
        return need

    def _commit(self, tok, reads, writes):
        for w in writes:
            self.last_w[w] = tok
            self.readers[w] = []
        for r in reads:
            self.readers.setdefault(r, []).append(tok)

    def _filter(self, eng, need):
        kn = self.known[eng]
        out = []
        for k, v in need.items():
            if kn.get(k, 0) < v:
                kn[k] = v
                out.append((k, v))
        return out

    def op(self, eng, fn, reads=(), writes=()):
        reads = tuple(reads)
        writes = tuple(writes)
        waits = self._filter(eng, self._deps(eng, reads, writes))
        self.cnt[eng] += 1
        tok = (eng, self.cnt[eng])
        self.ops[eng].append(_Op(fn, waits, eng, 1))
        self._commit(tok, reads, writes)
        return tok

    def dma(self, q, fn, reads=(), writes=()):
        reads = tuple(reads)
        writes = tuple(writes)
        need = self._deps(q, reads, writes)
        i = self.dma_rr[q]
        self.dma_rr[q] = (i + 1) % N_DMA_SEMS
        sk = ("dma", q, i)
        prev = self.dma_cnt[sk]
        if prev and need.get(sk, 0) < 16 * prev:
            need[sk] = 16 * prev
        waits = self._filter(q, need)
        self.dma_cnt[sk] = prev + 1
        tok = (sk, 16 * (prev + 1))
        self.ops[q].append(_Op(fn, waits, sk, 16))
        self._commit(tok, reads, writes)
        return tok

    def cc(self, fn, reads, writes, total):
        reads = tuple(reads)
        writes = tuple(writes)
        waits = self._filter("gpsimd", self._deps("gpsimd", reads, writes))
        self.ops["gpsimd"].append(_Op(fn, waits, "cc", 1))
        self._commit(("cc", total), reads, writes)

    def alias(self, src_keys, dst_keys):
        toks = []
        for s in src_keys:
            if self.last_w.get(s) is not None:
                toks.append(self.last_w[s])
            toks.extend(self.readers.get(s, ()))
        for d in dst_keys:
            self.readers.setdefault(d, []).extend(toks)

    def wait_all(self, eng, keys):
        waits = self._filter(eng, self._deps(eng, keys, keys))
        self.ops[eng].append(_Op(None, waits, None, 0))

    def emit(self):
        nc = self.nc
        with ExitStack() as es:
            sems = {}
            for k in self.semkeys:
                nm = k if isinstance(k, str) else "d_%s_%d" % (k[1], k[2])
                sems[k] = es.enter_context(nc.semaphore("s_" + nm))
            block = es.enter_context(nc.Block())

            def run(eng_name):
                def body(e):
                    for o in self.ops[eng_name]:
                        for (k, v) in o.waits:
                            e.wait_ge(sems[k], v)
                        if o.fn is not None:
                            o.fn(e).then_inc(sems[o.semkey], o.incval)
                return body

            block.tensor(run("tensor"))
            block.vector(run("vector"))
            block.scalar(run("scalar"))
            block.gpsimd(run("gpsimd"))
            block.sync(run("sync"))


class _Stop(Exception):
    pass


import os as _os
_KSTOP = int(_os.environ.get("KSTOP", "99"))


def _ck(n):
    if _KSTOP <= n:
        raise _Stop()


def build_nc(TPC):
    NT = TPC // T
    NB = TPC // 128
    S = NR * TPC
    KG = T
    NKG = NR * NT
    KGC = KG // 128

    nc = bass.Bass("TRN2", target_bir_lowering=False)

    def din(name, shape, dt=F32):
        return nc.dram_tensor(name, list(shape), dt, kind="ExternalInput").ap()

    x_d = din("x", [TPC, D])
    wg_d = [din("ffn1_w_gate", [D, DFF]), din("ffn2_w_gate", [D, DFF])]
    wu_d = [din("ffn1_w_up", [D, DFF]), din("ffn2_w_up", [D, DFF])]
    wd_d = [din("ffn1_w_down", [DFF, D]), din("ffn2_w_down", [DFF, D])]
    gpre_d = [din("ffn1_pre_g", [1, D]), din("ffn2_pre_g", [1, D])]
    gpost_d = [din("ffn1_post_g", [1, D]), din("ffn2_post_g", [1, D])]
    gmixpre_d = din("mix_pre_g", [1, D])
    gmixpost_d = din("mix_post_g", [1, D])
    win_d = din("w_in", [D, WIN_COLS])
    wpa_d = din("w_proj_a", [1024, D])
    wpb_d = din("w_proj_b", [1024, D])
    wout_d = din("w_out", [D, D])
    gbias_d = din("gate_biasT", [128, 32])
    sink_d = din("sink_bc", [128, 8])
    lamv_d = din("lamv", [128, 4])
    subg_d = din("sublnT", [128, 2])
    ropeC_d = din("ropeC", [128, TPC])
    ropeS_d = din("ropeS", [128, TPC])
    amask_d = din("amask", [128, 8, 128], BF16)
    ident_d = din("ident", [128, 128], BF16)
    y_d = nc.dram_tensor("y", [TPC, D], F32, kind="ExternalOutput").ap()

    qsp_d = nc.dram_tensor("q_sp", [16 * 128, TPC], BF16).ap()
    x1sp_d = nc.dram_tensor("x1_sp", [TPC, D], F32).ap()
    def dint(name, shape):
        return nc.dram_tensor(name, list(shape), BF16).ap()
    kTA_l = [dint("kTA_l%d" % i, [768, T]) for i in range(NT)]
    kTB_l = [dint("kTB_l%d" % i, [512, T]) for i in range(NT)]
    vA_l = [dint("vA_l%d" % i, [T, 768]) for i in range(NT)]
    vB_l = [dint("vB_l%d" % i, [T, 512]) for i in range(NT)]
    kTA_g = [dint("kTA_g%d" % i, [NR * 768, T]) for i in range(NT)]
    kTB_g = [dint("kTB_g%d" % i, [NR * 512, T]) for i in range(NT)]
    vA_g = [dint("vA_g%d" % i, [NR * T, 768]) for i in range(NT)]
    vB_g = [dint("vB_g%d" % i, [NR * T, 512]) for i in range(NT)]
    p = Prog(nc)
    es = ExitStack()
    with es:
        def sb(name, shape, dt):
            return es.enter_context(nc.sbuf_tensor(name, list(shape), dt))

        x_sb = sb("x_sb", [128, 4, D], F32)
        regA = sb("regA", [128, 24576], BF16)
        regB = sb("regB", [128, 16384], BF16)
        wbuf = [sb("wbuf%d" % i, [128, 16, 256], BF16) for i in range(4)]
        wdbuf = [sb("wdbuf%d" % i, [128, 4, 512], BF16) for i in range(2)]
        gbuf = sb("gbuf", [128, D], F32)
        xsb = [sb("xsb%d" % i, [128, D], BF16) for i in range(2)]
        sgj = sb("sgj", [128, 1024], F32)
        ptr = sb("ptr", [128, 2048], BF16)
        ropeC = sb("ropeC_sb", [128, T], F32)
        ropeS = sb("ropeS_sb", [128, T], F32)
        esink = sb("esink", [128, 8, 128], F32)
        amask = sb("amask_sb", [128, 8, 128], BF16)
        ident = sb("ident_sb", [128, 128], BF16)
        ones_b = sb("ones_b", [128, 128], BF16)
        ones_f = sb("ones_f", [128, 128], F32)
        stage = [sb("stage%d" % i, [128, T], BF16) for i in range(2)]
        vstage = sb("vstage", [128, 4, 256], BF16)
        gbias = sb("gbias", [128, 32], F32)
        subg = sb("subg", [128, 2], F32)
        lamv = sb("lamv_sb", [128, 4], F32)
        small = sb("small", [128, 64], F32)
        PS = [es.enter_context(nc.psum_tensor("ps%d" % i, [128, 512], F32)) for i in range(8)]

        aT = regA[:, 0:JF * T].rearrange("p (j t) -> p j t", t=T)
        qT = regA[:, 0:8192].rearrange("p (c t) -> p c t", t=T)
        oT = regA[:, 8192:16384].rearrange("p (c t) -> p c t", t=T)
        kvb = [regA[:, 16384 + i * 4096:16384 + (i + 1) * 4096] for i in range(2)]
        fb_mix = regA[:, 0:16384].bitcast(F32).rearrange("p (t d) -> p t d", d=D)
        gtmp = regA[:, 0:8192].bitcast(F32).rearrange("p (i t) -> p i t", t=T)
        hT = regB[:, 0:8192].rearrange("p (k t) -> p k t", t=T)
        mT = regB[:, 8192:16384].rearrange("p (k t) -> p k t", t=T)
        fb_ffn = regB[:].bitcast(F32).rearrange("p (t d) -> p t d", d=D)
        atmp = regB[:, 8192:16384].bitcast(F32).rearrange("p (i t) -> p i t", t=T)
        sg = [sgj[:, 0:512], sgj[:, 512:1024]]
        junk = sgj[:].bitcast(BF16)
        pt = [ptr[:, i * 512:(i + 1) * 512] for i in range(4)]
        rtmp = [ptr[:, 0:1024].bitcast(F32), ptr[:, 1024:2048].bitcast(F32)]

        AT_KEYS = ["A0", "A1", "kv0", "kv1"]
        ss = small[:, 0:4]
        ms = small[:, 4:8]
        sd = small[:, 8:12]
        rstd = small[:, 12:16]
        prod = small[:, 16:18]
        elam = small[:, 18:20]
        neglam = small[:, 20:21]
        esk = small[:, 24:32]
        subg_s = small[:, 32:34]

        wv = lambda w: w.rearrange("(k p) n -> p k n", p=128)

        st = {"wb": 0, "wd": 0, "xs": 0, "sg": 0, "stg": 0, "pt": 0, "kv": 0, "psT": 0}

        wbs = nc.dram_tensor("wbs", [160, 128, 4096], BF16).ap()
        wds = nc.dram_tensor("wds", [112, 128, 2048], BF16).ap()
        scr = {"wb": {}, "wd": {}}

        def load_wbuf(src_ap, k0=0, k1=16, buf=None, uid=None):
            if buf is None:
                buf = st["wb"]
                st["wb"] = (buf + 1) % 4
            keys = [("wb", buf, 0)] if k1 <= 8 else ([("wb", buf, 8)] if k0 >= 8 else [("wb", buf, 0), ("wb", buf, 8)])
            nk = k1 - k0
            dstv = wbuf[buf][:, k0:k1, :]
            if uid in scr["wb"]:
                sc = wbs[scr["wb"][uid], :, 0:nk * 256].rearrange("p (k c) -> p k c", c=256)
                p.dma("gpsimd", lambda e: e.dma_start(out=dstv, in_=sc), reads=[("wbs", uid)], writes=keys)
            else:
                p.dma("gpsimd", lambda e: e.dma_start(out=dstv, in_=src_ap), writes=keys)
                idx = len(scr["wb"])
                scr["wb"][uid] = idx
                sc = wbs[idx, :, 0:nk * 256].rearrange("p (k c) -> p k c", c=256)
                p.dma("sync", lambda e: e.dma_start(out=sc, in_=dstv), reads=keys, writes=[("wbs", uid)])
            return buf

        def load_wd(src_ap, uid=None):
            buf = st["wd"]
            st["wd"] = (buf + 1) % 2
            if uid in scr["wd"]:
                sc = wds[scr["wd"][uid]].rearrange("p (k c) -> p k c", c=512)
                p.dma("gpsimd", lambda e: e.dma_start(out=wdbuf[buf][:], in_=sc), reads=[("wds", uid)], writes=[("wd", buf)])
            else:
                p.dma("gpsimd", lambda e: e.dma_start(out=wdbuf[buf][:], in_=src_ap), writes=[("wd", buf)])
                idx = len(scr["wd"])
                scr["wd"][uid] = idx
                sc = wds[idx].rearrange("p (k c) -> p k c", c=512)
                p.dma("sync", lambda e: e.dma_start(out=sc, in_=wdbuf[buf][:]), reads=[("wd", buf)], writes=[("wds", uid)])
            return buf

        def mm_group(out_ap, pairs, reads, writes):
            n = len(pairs)

            def f(e):
                for i, (l, r) in enumerate(pairs):
                    ins = e.matmul(out_ap, lhsT=l, rhs=r, start=(i == 0), stop=(i == n - 1))
                return ins
            p.op("tensor", f, reads=reads, writes=writes)

        for dst, src, key in ((amask[:], amask_d, "amask"), (ident[:], ident_d, "ident"), (gbias[:], gbias_d, "gbias"),
                              (subg[:], subg_d, "subg"), (lamv[:], lamv_d, "lamv"), (esk, sink_d, "esk")):
            p.dma("sync", lambda e, dst=dst, src=src: e.dma_start(out=dst, in_=src), writes=[key])
        p.op("vector", lambda e: e.memset(ones_b[:], 1.0), writes=["ones_b"])
        p.op("vector", lambda e: e.memset(ones_f[:], 1.0), writes=["ones_f"])
        p.op("scalar", lambda e: e.activation(out=esk, in_=esk, func=AF.Exp), reads=["esk"], writes=["esk"])
        p.op("vector", lambda e: e.tensor_copy(out=esink[:], in_=esk.unsqueeze(2).broadcast_to([128, 8, 128])),
             reads=["esk"], writes=["esink"])
        p.op("vector", lambda e: e.tensor_tensor(out=prod, in0=lamv[:, 0:2], in1=lamv[:, 2:4], op=ALU.mult),
             reads=["lamv"], writes=["prod"])
        mm_group(PS[0][:, 0:2], [(ones_f[:], prod)], reads=["ones_f", "prod"], writes=[("ps", 0)])
        p.op("scalar", lambda e: e.activation(out=elam, in_=PS[0][:, 0:2], func=AF.Exp),
             reads=[("ps", 0)], writes=["elam"])
        p.op("vector", lambda e: e.tensor_tensor(out=neglam, in0=elam[:, 1:2], in1=elam[:, 0:1], op=ALU.subtract),
             reads=["elam"], writes=["neglam"])
        p.op("vector", lambda e: e.tensor_scalar(out=neglam, in0=neglam, scalar1=-LAM_INIT, scalar2=None, op0=ALU.add),
             reads=["neglam"], writes=["neglam"])
        p.op("vector", lambda e: e.tensor_scalar(out=subg_s, in0=subg[:], scalar1=1.0 - LAM_INIT, scalar2=None, op0=ALU.mult),
             reads=["subg"], writes=["subg_s"])

        def load_gain(g_d):
            p.dma("sync", lambda e: e.dma_start(out=gbuf[:], in_=g_d.broadcast_to([128, D])), writes=["gbuf"])

        def rows_rstd(srcs, read_keys):
            for t in range(4):
                p.op("scalar", lambda e, t=t: e.activation(out=junk, in_=srcs[t], func=AF.Square, accum_out=ss[:, t:t + 1]),
                     reads=[read_keys[t]], writes=[("ss", t), "sg0", "sg1"])
            p.op("vector", lambda e: e.tensor_scalar(out=ms, in0=ss, scalar1=1.0 / D, scalar2=EPS, op0=ALU.mult, op1=ALU.add),
                 reads=[("ss", t) for t in range(4)], writes=["ms"])
            p.op("scalar", lambda e: e.activation(out=sd, in_=ms, func=AF.Sqrt), reads=["ms"], writes=["sd"])
            p.op("vector", lambda e: e.reciprocal(out=rstd, in_=sd), reads=["sd"], writes=["rstd"])

        def norm_to_hT(g_d):
            load_gain(g_d)
            rows_rstd([x_sb[:, t, :] for t in range(4)], [("x", t) for t in range(4)])
            for t in range(4):
                xi = st["xs"]
                st["xs"] = 1 - xi
                p.op("vector", lambda e, t=t, xi=xi: e.scalar_tensor_tensor(
                    out=xsb[xi][:], in0=x_sb[:, t, :], scalar=rstd[:, t:t + 1], in1=gbuf[:], op0=ALU.mult, op1=ALU.mult),
                    reads=[("x", t), "rstd", "gbuf"], writes=[("xsb", xi)])
                for half in range(2):
                    b = 4 + st["psT"]
                    st["psT"] = (st["psT"] + 1) % 4
                    psv = PS[b][:].bitcast(BF16)

                    def tr(e, xi=xi, half=half, psv=psv):
                        for kk in range(8):
                            k = half * 8 + kk
                            ins = e.transpose(psv[:, kk * 128:(kk + 1) * 128], xsb[xi][:, k * 128:(k + 1) * 128], ident[:])
                        return ins
                    p.op("tensor", tr, reads=[("xsb", xi), "ident"], writes=[("ps", b)])
                    src = psv.rearrange("p (k t) -> p k t", t=128)
                    dst = hT[:, half * 8:(half + 1) * 8, t * 128:(t + 1) * 128]
                    p.op("scalar", lambda e, src=src, dst=dst: e.activation(out=dst, in_=src, func=AF.Copy),
                         reads=[("ps", b)], writes=["hT"])

        def ffn_stage1(wg, wu, wname):
            for j2 in range(JF // 2):
                bg = load_wbuf(wv(wg)[:, :, j2 * 256:(j2 + 1) * 256], uid=("g", wname, j2))
                bu = load_wbuf(wv(wu)[:, :, j2 * 256:(j2 + 1) * 256], uid=("u", wname, j2))
                for jj in range(2):
                    j = j2 * 2 + jj
                    pg, pu = (0, 1) if j % 2 == 0 else (2, 3)
                    mm_group(PS[pg][:], [(wbuf[bg][:, k, jj * 128:(jj + 1) * 128], hT[:, k, :]) for k in range(KC)],
                             reads=[("wb", bg, 0), ("wb", bg, 8), "hT"], writes=[("ps", pg)])
                    mm_group(PS[pu][:], [(wbuf[bu][:, k, jj * 128:(jj + 1) * 128], hT[:, k, :]) for k in range(KC)],
                             reads=[("wb", bu, 0), ("wb", bu, 8), "hT"], writes=[("ps", pu)])
                    si = st["sg"]
                    st["sg"] = 1 - si
                    p.op("scalar", lambda e, pg=pg, si=si: e.activation(out=sg[si], in_=PS[pg][:], func=AF.Silu),
                         reads=[("ps", pg)], writes=["sg%d" % si])
                    p.op("vector", lambda e, pu=pu, si=si, j=j: e.tensor_tensor(out=aT[:, j, :], in0=sg[si], in1=PS[pu][:], op=ALU.mult),
                         reads=[("ps", pu), "sg%d" % si], writes=[("aT", j)])

        def down_proj(src, src_key, w_d, nk, fb, fb_keys, wname):
            for n in range(4):
                banks = (4, 5, 6, 7) if n % 2 == 0 else (0, 1, 2, 3)
                ngr = nk // 4
                for kg in range(ngr):
                    b = load_wd(wv(w_d)[:, kg * 4:(kg + 1) * 4, n * 512:(n + 1) * 512], uid=(wname, n, kg))

                    def f(e, kg=kg, b=b, banks=banks, ngr=ngr):
                        for t in range(4):
                            for k in range(4):
                                ins = e.matmul(PS[banks[t]][:], lhsT=src[:, kg * 4 + k, t * 128:(t + 1) * 128], rhs=wdbuf[b][:, k, :],
                                               start=(kg == 0 and k == 0), stop=(kg == ngr - 1 and k == 3))
                        return ins
                    p.op("tensor", f, reads=[("wd", b)] + [src_key(kg * 4 + k) for k in range(4)],
                         writes=[("ps", bk) for bk in banks])
                for t in range(4):
                    dst = fb[:, t, n * 512:(n + 1) * 512]
                    if t % 2 == 0:
                        p.op("vector", lambda e, dst=dst, bk=banks[t]: e.tensor_copy(out=dst, in_=PS[bk][:]),
                             reads=[("ps", banks[t])], writes=[fb_keys[t]])
                    else:
                        p.op("scalar", lambda e, dst=dst, bk=banks[t]: e.activation(out=dst, in_=PS[bk][:], func=AF.Copy),
                             reads=[("ps", banks[t])], writes=[fb_keys[t]])

        def post_norm_res(fb, fb_keys, g_d, factor):
            load_gain(g_d)
            rows_rstd([fb[:, t, :] for t in range(4)], fb_keys)
            for t in range(4):
                p.op("vector", lambda e, t=t: e.scalar_tensor_tensor(
                    out=fb[:, t, :], in0=fb[:, t, :], scalar=rstd[:, t:t + 1], in1=gbuf[:], op0=ALU.mult, op1=ALU.mult),
                    reads=[fb_keys[t], "rstd", "gbuf"], writes=[fb_keys[t]])
                p.op("vector", lambda e, t=t: e.scalar_tensor_tensor(
                    out=x_sb[:, t, :], in0=fb[:, t, :], scalar=float(factor), in1=x_sb[:, t, :], op0=ALU.mult, op1=ALU.add),
                    reads=[fb_keys[t], ("x", t)], writes=[("x", t)])

        FB_FFN_KEYS = ["hT", "hT", "mT", "mT"]
        FB_MIX_KEYS = ["A0", "A0", "A1", "A1"]

        def ffn(l):
            norm_to_hT(gpre_d[l])
            p.alias(["A0", "A1", "kv0", "kv1"], [("aT", j) for j in range(JF)])
            ffn_stage1(wg_d[l], wu_d[l], l)
            down_proj(aT, lambda j: ("aT", j), wd_d[l], JF, fb_ffn, FB_FFN_KEYS, ("d", l))
            p.alias([("aT", j) for j in range(JF)], ["A0", "A1", "kv0", "kv1"])
            post_norm_res(fb_ffn, FB_FFN_KEYS, gpost_d[l], 0.5)

        def load_x(src_d, i, rkey=None):
            for t in range(4):
                r0 = i * T + t * 128
                p.dma("sync", lambda e, t=t, r0=r0: e.dma_start(out=x_sb[:, t, :], in_=src_d[r0:r0 + 128, :]),
                      reads=[(rkey, i, t)] if rkey else [], writes=[("x", t)])

        def store_x(dst_d, i, key):
            for t in range(4):
                r0 = i * T + t * 128
                p.dma("sync", lambda e, t=t, r0=r0: e.dma_start(out=dst_d[r0:r0 + 128, :], in_=x_sb[:, t, :]),
                      reads=[("x", t)], writes=[(key, i, t)])

        def qkv(i):
            c0 = i * T
            p.dma("sync", lambda e: e.dma_start(out=ropeC[:], in_=ropeC_d[:, c0:c0 + T]), writes=["ropeC"])
            p.dma("sync", lambda e: e.dma_start(out=ropeS[:], in_=ropeS_d[:, c0:c0 + T]), writes=["ropeS"])
            fm = []
            for c in range(8):
                fm.append((c * 128, qsp_d[c * 128:(c + 1) * 128, c0:c0 + T], ("qsp", i)))
            for g in range(2):
                fm.append((1024 + g * 128, kTA_l[i][g * 128:(g + 1) * 128, :], ("kTA", i, g)))
            for c in range(8):
                fm.append((1536 + c * 128, qsp_d[(8 + c) * 128:(9 + c) * 128, c0:c0 + T], ("qsp", i)))
            for c in range(8):
                if c < 4:
                    fm.append((2560 + c * 128, kTA_l[i][(2 + c) * 128:(3 + c) * 128, :], ("kTA", i, 2 + c)))
                else:
                    fm.append((2560 + c * 128, kTB_l[i][(c - 4) * 128:(c - 3) * 128, :], ("kTB", i, c - 4)))
            for pr in range(len(fm) // 2):
                col0 = fm[2 * pr][0]
                b = load_wbuf(wv(win_d)[:, :, col0:col0 + 256], uid=("in", col0))
                for jj in range(2):
                    _, dst_d, dkey = fm[2 * pr + jj]
                    bk = (2 * pr + jj) % 4
                    mm_group(PS[bk][:], [(wbuf[b][:, k, jj * 128:(jj + 1) * 128], hT[:, k, :]) for k in range(KC)],
                             reads=[("wb", b, 0), ("wb", b, 8), "hT"], writes=[("ps", bk)])
                    p.op("vector", lambda e, bk=bk: e.tensor_tensor(out=rtmp[0], in0=PS[bk][:], in1=ropeC[:], op=ALU.mult),
                         reads=[("ps", bk), "ropeC"], writes=["pt0", "pt1"])
                    p.op("vector", lambda e, bk=bk: e.tensor_tensor(out=rtmp[1][0:64, :], in0=PS[bk][64:128, :], in1=ropeS[0:64, :], op=ALU.mult),
                         reads=[("ps", bk), "ropeS"], writes=["pt2"])
                    p.op("vector", lambda e, bk=bk: e.tensor_tensor(out=rtmp[1][64:128, :], in0=PS[bk][0:64, :], in1=ropeS[64:128, :], op=ALU.mult),
                         reads=[("ps", bk), "ropeS"], writes=["pt3"])
                    si = st["stg"]
                    st["stg"] = 1 - si
                    p.op("vector", lambda e, si=si: e.tensor_tensor(out=stage[si][:], in0=rtmp[0], in1=rtmp[1], op=ALU.add),
                         reads=["pt0", "pt1", "pt2", "pt3"], writes=[("stage", si)])
                    p.dma("sync", lambda e, si=si, dst_d=dst_d: e.dma_start(out=dst_d, in_=stage[si][:]),
                          reads=[("stage", si)], writes=[dkey if dkey[0] != "qsp" else ("qsp", i, 2 * pr + jj)])
            vs = [(1280, vA_l[i], 0, "vA"), (3584, vA_l[i], 256, "vA"), (3840, vA_l[i], 512, "vA"),
                  (4096, vB_l[i], 0, "vB"), (4352, vB_l[i], 256, "vB")]
            for (col0, vdst, vc0, vkey) in vs:
                b = load_wbuf(wv(win_d)[:, :, col0:col0 + 256], uid=("in", col0))
                for t in range(4):
                    bk = 4 + t
                    mm_group(PS[bk][:, 0:256],
                             [(hT[:, k, t * 128:(t + 1) * 128], wbuf[b][:, k, :]) for k in range(KC)],
                             reads=[("wb", b, 0), ("wb", b, 8), "hT"], writes=[("ps", bk)])
                    p.op("scalar", lambda e, bk=bk, t=t: e.activation(out=vstage[:, t, :], in_=PS[bk][:, 0:256], func=AF.Copy),
                         reads=[("ps", bk)], writes=["vstage"])
                dst = vdst[:, vc0:vc0 + 256].rearrange("(t p) c -> p t c", p=128)
                p.dma("sync", lambda e, dst=dst: e.dma_start(out=dst, in_=vstage[:]), reads=["vstage"], writes=[(vkey, i, vc0)])

        def run_all():
            _ck(1)
            for i in range(NT):
                load_x(x_d, i)
                norm_to_hT(gpre_d[0]) if _KSTOP == 2 else None
                _ck(2)
                ffn(0)
                _ck(3)
                store_x(x1sp_d, i, "x1sp")
                norm_to_hT(gmixpre_d)
                qkv(i)
            _ck(4)

            groups = [list(range(g * NR, (g + 1) * NR)) for g in range(NCORES // NR)]
            kTA_keys = lambda i: [("kTA", i, c) for c in range(6)]
            kTB_keys = lambda i: [("kTB", i, c) for c in range(4)]
            vA_keys = lambda i: [("vA", i, c) for c in (0, 256, 512)]
            vB_keys = lambda i: [("vB", i, c) for c in (0, 256)]
            for i in range(NT):
                for (src, dst, rk, wk) in ((kTA_l[i], kTA_g[i], kTA_keys(i), ("kTAg", i)), (kTB_l[i], kTB_g[i], kTB_keys(i), ("kTBg", i)),
                                           (vA_l[i], vA_g[i], vA_keys(i), ("vAg", i)), (vB_l[i], vB_g[i], vB_keys(i), ("vBg", i))):
                    p.cc(lambda e, src=src, dst=dst: e.collective_compute(
                        "AllGather", ALU.bypass, replica_groups=groups, ins=[src], outs=[dst]), rk, [wk], 4 * NT)
            _ck(5)

            def next_pt():
                i = st["pt"]
                st["pt"] = (i + 1) % 4
                return i

            def attn_b(i):
                for h in range(4):
                    units = [(kg, comp, c) for kg in range(NKG) for comp in range(2) for c in range(KGC)]
                    U = len(units)
                    loaded = {}
                    ptidx = {}
                    acc_cnt = [0, 0]

                    def ensure_loaded(kg, h=h, loaded=loaded):
                        if kg in loaded:
                            return loaded[kg]
                        r, ti = kg // NT, kg % NT
                        kb = st["kv"]
                        st["kv"] = (kb + 1) % 4
                        base = kvb[kb // 2][:, (kb % 2) * 2048:(kb % 2 + 1) * 2048]
                        kbuf = base[:, 0:2 * KG].rearrange("p (c k) -> p c k", k=KG)
                        vbuf = base[:, 2 * KG:2 * KG + KGC * 256].rearrange("p (c e) -> p c e", e=256)
                        if h < 2:
                            row0 = r * 768 + (2 + h * 2) * 128
                            ksrc = kTA_g[ti][row0:row0 + 256, :].rearrange("(c p) k -> p c k", p=128)
                            vsrc = vA_g[ti][r * T:(r + 1) * T, 256 + h * 256:512 + h * 256].rearrange("(c p) e -> p c e", p=128)
                            kkey, vkey = ("kTAg", ti), ("vAg", ti)
                        else:
                            row0 = r * 512 + (h - 2) * 256
                            ksrc = kTB_g[ti][row0:row0 + 256, :].rearrange("(c p) k -> p c k", p=128)
                            vsrc = vB_g[ti][r * T:(r + 1) * T, (h - 2) * 256:(h - 1) * 256].rearrange("(c p) e -> p c e", p=128)
                            kkey, vkey = ("kTBg", ti), ("vBg", ti)
                        p.dma("sync", lambda e, kbuf=kbuf, ksrc=ksrc: e.dma_start(out=kbuf, in_=ksrc),
                              reads=[kkey], writes=[("kbK", kb)])
                        p.dma("sync", lambda e, vbuf=vbuf, vsrc=vsrc: e.dma_start(out=vbuf, in_=vsrc),
                              reads=[vkey], writes=[("kbV", kb)])
                        loaded[kg] = (kb, kbuf, vbuf)
                        return loaded[kg]

                    SB = ((0, 1), (6, 7))

                    def sbank(u):
                        return SB[(u // 2) % 2][u % 2]

                    def Sp(k, h=h):
                        prs, rds, wrs = [], ["qT"], []
                        for u in (2 * k, 2 * k + 1):
                            kg, comp, c = units[u]
                            kb, kbuf, vbuf = ensure_loaded(kg)
                            prs.append((PS[sbank(u)][:], kbuf[:, comp, c * 128:(c + 1) * 128], qT[:, 8 + h * 2 + comp, :]))
                            rds.append(("kbK", kb))
                            wrs.append(("ps", sbank(u)))

                        def f(e, prs=prs):
                            for (o_, l_, r_) in prs:
                                ins = e.matmul(o_, lhsT=l_, rhs=r_, start=True, stop=True)
                            return ins
                        p.op("tensor", f, reads=rds, writes=wrs)

                    def Ep(k, ptidx=ptidx):
                        for u in (2 * k, 2 * k + 1):
                            sbk = sbank(u)
                            pi = next_pt()
                            ptidx[u] = pi
                            p.op("scalar", lambda e, sbk=sbk, pi=pi: e.activation(out=pt[pi], in_=PS[sbk][:], func=AF.Exp, scale=SCALE),
                                 reads=[("ps", sbk)], writes=["pt%d" % pi])

                    def PVp(k, ptidx=ptidx, acc_cnt=acc_cnt):
                        mms, rds, wrs = [], [], []
                        for u in (2 * k, 2 * k + 1):
                            kg, comp, c = units[u]
                            kb, kbuf, vbuf = ensure_loaded(kg)
                            pi = ptidx[u]
                            first = (kg == 0 and c == 0)
                            last = (kg == NKG - 1 and c == KGC - 1)
                            ob = 2 + comp * 2
                            mms.append((PS[ob][:], vbuf[:, c, 0:128], pt[pi], first, last))
                            mms.append((PS[ob + 1][:], vbuf[:, c, 128:256], pt[pi], first, last))
                            rds += [("kbV", kb), "pt%d" % pi]
                            wrs += [("ps", ob), ("ps", ob + 1)]

                        def f(e, mms=mms):
                            for (o_, l_, r_, fi, la) in mms:
                                ins = e.matmul(o_, lhsT=l_, rhs=r_, start=fi, stop=la)
                            return ins
                        p.op("tensor", f, reads=rds, writes=wrs)
                        for u in (2 * k, 2 * k + 1):
                            kg, comp, c = units[u]
                            pi = ptidx[u]
                            k_ = acc_cnt[comp]
                            acc_cnt[comp] += 1
                            on_pool = (k_ % 3 == 2)
                            eng = "gpsimd" if on_pool else "vector"
                            acc = atmp[:, 5 + comp, :] if on_pool else sg[comp]
                            akey = ("atmp", 5 + comp) if on_pool else "sg%d" % comp
                            if k_ == 0 or k_ == 2:
                                p.op(eng, lambda e, acc=acc, pi=pi: e.tensor_copy(out=acc, in_=pt[pi]),
                                     reads=["pt%d" % pi], writes=[akey])
                            else:
                                p.op(eng, lambda e, acc=acc, pi=pi: e.tensor_tensor(out=acc, in0=acc, in1=pt[pi], op=ALU.add),
                                     reads=["pt%d" % pi, akey], writes=[akey])

                    NPR = U // 2
                    Sp(0)
                    Sp(1)
                    Ep(0)
                    for k in range(NPR):
                        PVp(k)
                        if k + 2 < NPR:
                            Sp(k + 2)
                        if k + 1 < NPR:
                            Ep(k + 1)
                    r1, r2, ta, o0, o1, sq0, sq1, rn = [atmp[:, q, :] for q in range(8)]
                    K = lambda q: ("atmp", q)
                    mm_group(PS[6][:], [(ones_f[:], sg[0]), (ones_f[:], atmp[:, 5, :])], reads=["ones_f", "sg0", ("atmp", 5)], writes=[("ps", 6)])
                    mm_group(PS[7][:], [(ones_f[:], sg[1]), (ones_f[:], atmp[:, 6, :])], reads=["ones_f", "sg1", ("atmp", 6)], writes=[("ps", 7)])
                    p.op("vector", lambda e: e.reciprocal(out=r1, in_=PS[6][:]), reads=[("ps", 6)], writes=[K(0)])
                    p.op("vector", lambda e: e.reciprocal(out=r2, in_=PS[7][:]), reads=[("ps", 7)], writes=[K(1)])
                    p.op("vector", lambda e: e.tensor_scalar(out=r2, in0=r2, scalar1=neglam, scalar2=None, op0=ALU.mult),
                         reads=[K(1), "neglam"], writes=[K(1)])
                    for ec, oo, sq in ((0, o0, sq0), (1, o1, sq1)):
                        p.op("vector", lambda e, ec=ec: e.tensor_tensor(out=ta, in0=PS[2 + ec][:], in1=r1, op=ALU.mult),
                             reads=[("ps", 2 + ec), K(0)], writes=[K(2)])
                        p.op("vector", lambda e, ec=ec, oo=oo: e.tensor_tensor(out=oo, in0=PS[4 + ec][:], in1=r2, op=ALU.mult),
                             reads=[("ps", 4 + ec), K(1)], writes=[K(3 + ec)])
                        p.op("vector", lambda e, oo=oo: e.tensor_tensor(out=oo, in0=oo, in1=ta, op=ALU.add),
                             reads=[K(2), K(3 + ec)], writes=[K(3 + ec)])
                        p.op("scalar", lambda e, oo=oo, sq=sq: e.activation(out=sq, in_=oo, func=AF.Square),
                             reads=[K(3 + ec)], writes=[K(5 + ec)])
                    mm_group(PS[0][:], [(ones_f[:], sq0), (ones_f[:], sq1)], reads=["ones_f", K(5), K(6)], writes=[("ps", 0)])
                    p.op("vector", lambda e: e.tensor_scalar(out=rn, in0=PS[0][:], scalar1=1.0 / 256.0, scalar2=EPS, op0=ALU.mult, op1=ALU.add),
                         reads=[("ps", 0)], writes=[K(7)])
                    p.op("scalar", lambda e: e.activation(out=rn, in_=rn, func=AF.Sqrt), reads=[K(7)], writes=[K(7)])
                    p.op("vector", lambda e: e.reciprocal(out=rn, in_=rn), reads=[K(7)], writes=[K(7)])
                    for ec, oo in ((0, o0), (1, o1)):
                        p.op("vector", lambda e, ec=ec, oo=oo, h=h: e.scalar_tensor_tensor(
                            out=oT[:, 8 + h * 2 + ec, :], in0=oo, scalar=subg_s[:, ec:ec + 1], in1=rn, op0=ALU.mult, op1=ALU.mult),
                            reads=[K(3 + ec), K(7), "subg_s"], writes=[("oT", 8 + h * 2 + ec)])

            def attn_a(i):
                n0 = i * 4
                lo = max(n0 - 1, 0)
                hi = min(n0 + 4, NB - 1)
                nblk = hi - lo + 1
                kab = kvb[0][:, 0:2 * 768].rearrange("p (g k) -> p g k", k=768)
                vab = kvb[0][:, 1536:1536 + 6 * 256].rearrange("p (b e) -> p b e", e=256)
                kcb = kvb[1][:, 0:2 * 768].rearrange("p (g k) -> p g k", k=768)
                vcb = kvb[1][:, 1536:1536 + 6 * 256].rearrange("p (b e) -> p b e", e=256)
                for m in range(lo, hi + 1):
                    ti, bi = m // 4, m % 4
                    p.dma("sync", lambda e, m=m, ti=ti, bi=bi: e.dma_start(
                        out=kab[:, :, (m - lo) * 128:(m - lo + 1) * 128],
                        in_=kTA_l[ti][0:256, bi * 128:(bi + 1) * 128].rearrange("(g p) k -> p g k", p=128)),
                        reads=kTA_keys(ti), writes=[("kvK", 0), ("kvV", 0)])
                    p.dma("sync", lambda e, m=m, ti=ti, bi=bi: e.dma_start(
                        out=vab[:, m - lo, :], in_=vA_l[ti][bi * 128:(bi + 1) * 128, 0:256]),
                        reads=vA_keys(ti), writes=[("kvK", 0), ("kvV", 0)])
                cands = []
                if i == 0:
                    cands += [(s_, r, "prev") for s_, r in enumerate((0, 1, 2))]
                if i == NT - 1:
                    cands += [(3 + s_, r, "next") for s_, r in enumerate((1, 2, 3))]
                for (slot, r, which) in cands:
                    ti = NT - 1 if which == "prev" else 0
                    col0 = T - 128 if which == "prev" else 0
                    p.dma("sync", lambda e, slot=slot, r=r, col0=col0, ti=ti: e.dma_start(
                        out=kcb[:, :, slot * 128:(slot + 1) * 128],
                        in_=kTA_g[ti][r * 768:r * 768 + 256, col0:col0 + 128].rearrange("(g p) k -> p g k", p=128)),
                        reads=[("kTAg", ti)], writes=[("kvK", 1), ("kvV", 1)])
                    p.dma("sync", lambda e, slot=slot, r=r, col0=col0, ti=ti: e.dma_start(
                        out=vcb[:, slot, :], in_=vA_g[ti][r * T + col0:r * T + col0 + 128, 0:256]),
                        reads=[("vAg", ti)], writes=[("kvK", 1), ("kvV", 1)])
                units = []
                for nb in range(4):
                    n = n0 + nb
                    for g in range(2):
                        chunks = []
                        own = lambda m: (kab[:, g, (m - lo) * 128:(m - lo + 1) * 128], vab[:, m - lo, g * 128:(g + 1) * 128], 0)
                        cnd = lambda slot: (kcb[:, g, slot * 128:(slot + 1) * 128], vcb[:, slot, g * 128:(g + 1) * 128], 1)
                        if n == 0:
                            for s_ in range(3):
                                chunks.append(cnd(s_) + (2 + s_,))
                        else:
                            chunks.append(own(n - 1) + (0,))
                        chunks.append(own(n) + (None,))
                        if n == NB - 1:
                            for s_ in range(3):
                                chunks.append(cnd(3 + s_) + (5 + s_,))
                        else:
                            chunks.append(own(n + 1) + (1,))
                        qv = qT[:, g * 4:(g + 1) * 4, nb * 128:(nb + 1) * 128]
                        idx = nb * 2 + g
                        for ci, (kap, vap, which, mi) in enumerate(chunks):
                            units.append(dict(kap=kap, vap=vap, which=which, mi=mi, qv=qv, ob=2 + idx % 2, sb=4 + idx % 2,
                                              first=(ci == 0), last=(ci == len(chunks) - 1), nb=nb, g=g, idx=idx))
                U = len(units)
                ptidx = {}

                def S_(u):
                    d = units[u]
                    sbk = u % 2
                    mm_group(PS[sbk][:], [(d["kap"], d["qv"])], reads=[("kvK", d["which"]), ("kvV", d["which"]), "qT"], writes=[("ps", sbk)])

                def E_(u):
                    d = units[u]
                    sbk = u % 2
                    pi = next_pt()
                    ptidx[u] = pi
                    p.op("scalar", lambda e, sbk=sbk, pi=pi: e.activation(out=pt[pi], in_=PS[sbk][:], func=AF.Exp, scale=SCALE),
                         reads=[("ps", sbk)], writes=["pt%d" % pi])
                    if d["mi"] is not None:
                        mi = d["mi"]
                        ptv = pt[pi].rearrange("p (r q) -> p r q", q=128)
                        mv = amask[:, mi:mi + 1, :].broadcast_to([128, 4, 128])
                        p.op("vector", lambda e, ptv=ptv, mv=mv: e.tensor_tensor(out=ptv, in0=ptv, in1=mv, op=ALU.mult),
                             reads=["pt%d" % pi, "amask"], writes=["pt%d" % pi])

                def PV_(u):
                    d = units[u]
                    pi = ptidx[u]
                    ob, sb_, g, nb = d["ob"], d["sb"], d["g"], d["nb"]

                    def f(e, vap=d["vap"], pi=pi, ob=ob, sb_=sb_, first=d["first"], last=d["last"]):
                        e.matmul(PS[ob][:], lhsT=vap, rhs=pt[pi], start=first, stop=last)
                        return e.matmul(PS[sb_][:], lhsT=ones_b[:], rhs=pt[pi], start=first, stop=last)
                    p.op("tensor", f, reads=[("kvK", d["which"]), ("kvV", d["which"]), "pt%d" % pi, "ones_b"], writes=[("ps", ob), ("ps", sb_)])
                    if d["last"]:
                        den = atmp[:, d["idx"] % 2, :]
                        dk = ("atmp", d["idx"] % 2)
                        p.op("vector", lambda e, den=den, sb_=sb_, g=g: e.tensor_tensor(
                            out=den, in0=PS[sb_][:], in1=esink[:, g * 4:(g + 1) * 4, :].rearrange("p r q -> p (r q)"), op=ALU.add),
                            reads=[("ps", sb_), "esink"], writes=[dk])
                        p.op("vector", lambda e, den=den: e.reciprocal(out=den, in_=den), reads=[dk], writes=[dk])
                        dst = oT[:, g * 4:(g + 1) * 4, nb * 128:(nb + 1) * 128]
                        p.op("vector", lambda e, dst=dst, ob=ob, den=den: e.tensor_tensor(
                            out=dst, in0=PS[ob][:].rearrange("p (r q) -> p r q", q=128), in1=den.rearrange("p (r q) -> p r q", q=128), op=ALU.mult),
                            reads=[("ps", ob), dk], writes=[("oT", g * 4 + r_) for r_ in range(4)])

                S_(0)
                S_(1)
                E_(0)
                for u in range(U):
                    PV_(u)
                    if u + 2 < U:
                        S_(u + 2)
                    if u + 1 < U:
                        E_(u + 1)

            def gates_merge(i):
                p.alias(["qT"], [("gt", q) for q in range(8)])
                oT_keys = [("oT", c) for c in range(16)]
                for j2 in range(8):
                    ba = load_wbuf(wv(win_d)[:, :, 4608 + j2 * 256:4608 + (j2 + 1) * 256], uid=("in", 4608 + j2 * 256))
                    bb = load_wbuf(wv(win_d)[:, :, 6656 + j2 * 256:6656 + (j2 + 1) * 256], uid=("in", 6656 + j2 * 256))
                    bp = load_wbuf(wv(wpa_d)[:, :, j2 * 256:(j2 + 1) * 256], 0, 8, uid=("pa", j2))
                    load_wbuf(wv(wpb_d)[:, :, j2 * 256:(j2 + 1) * 256], 8, 16, buf=bp, uid=("pb", j2))
                    for jj in range(2):
                        j = j2 * 2 + jj
                        bs = (0, 1, 2, 3) if j % 2 == 0 else (4, 5, 6, 7)
                        cs = slice(jj * 128, (jj + 1) * 128)
                        mm_group(PS[bs[0]][:], [(wbuf[ba][:, k, cs], hT[:, k, :]) for k in range(KC)],
                                 reads=[("wb", ba, 0), ("wb", ba, 8), "hT"], writes=[("ps", bs[0])])
                        mm_group(PS[bs[1]][:], [(wbuf[bb][:, k, cs], hT[:, k, :]) for k in range(KC)],
                                 reads=[("wb", bb, 0), ("wb", bb, 8), "hT"], writes=[("ps", bs[1])])
                        mm_group(PS[bs[2]][:], [(wbuf[bp][:, k, cs], oT[:, k, :]) for k in range(8)],
                                 reads=[("wb", bp, 0)] + oT_keys, writes=[("ps", bs[2])])
                        mm_group(PS[bs[3]][:], [(wbuf[bp][:, 8 + k, cs], oT[:, 8 + k, :]) for k in range(8)],
                                 reads=[("wb", bp, 8)] + oT_keys, writes=[("ps", bs[3])])
                        q0 = (j % 2) * 4
                        ga, gb_, ta, tb = [gtmp[:, q0 + q, :] for q in range(4)]
                        GK = lambda q: ("gt", q0 + q)
                        p.op("scalar", lambda e, ga=ga, b0=bs[0], j=j: e.activation(out=ga, in_=PS[b0][:], func=AF.Sigmoid, bias=gbias[:, j:j + 1]),
                             reads=[("ps", bs[0]), "gbias"], writes=[GK(0)])
                        p.op("scalar", lambda e, gb_=gb_, b1=bs[1], j=j: e.activation(out=gb_, in_=PS[b1][:], func=AF.Sigmoid, bias=gbias[:, 16 + j:17 + j]),
                             reads=[("ps", bs[1]), "gbias"], writes=[GK(1)])
                        p.op("vector", lambda e, ga=ga, ta=ta, b2=bs[2]: e.tensor_tensor(out=ta, in0=PS[b2][:], in1=ga, op=ALU.mult),
                             reads=[("ps", bs[2]), GK(0)], writes=[GK(2)])
                        p.op("vector", lambda e, gb_=gb_, tb=tb, b3=bs[3]: e.tensor_tensor(out=tb, in0=PS[b3][:], in1=gb_, op=ALU.mult),
                             reads=[("ps", bs[3]), GK(1)], writes=[GK(3)])
                        p.op("vector", lambda e, ta=ta, tb=tb, j=j: e.tensor_tensor(out=mT[:, j, :], in0=ta, in1=tb, op=ALU.add),
                             reads=[GK(2), GK(3)], writes=[("mTc", j)])

            for i in range(NT):
                c0 = i * T
                load_x(x1sp_d, i, "x1sp")
                norm_to_hT(gmixpre_d)
                p.alias(["A0", "A1"], ["qT"] + [("oT", c) for c in range(16)])
                p.alias(["kv0", "kv1"], [("kbK", j) for j in range(4)] + [("kbV", j) for j in range(4)])
                p.alias(["mT"] + [("mTc", j) for j in range(16)], [("atmp", q) for q in range(8)])
                p.dma("sync", lambda e, c0=c0: e.dma_start(out=qT, in_=qsp_d[:, c0:c0 + T].rearrange("(c p) t -> p c t", p=128)),
                      reads=[("qsp", i, c) for c in range(26)], writes=["qT"])
                attn_b(i)
                p.alias([("kbK", j) for j in range(4)] + [("kbV", j) for j in range(4)], [("kvK", 0), ("kvV", 0), ("kvK", 1), ("kvV", 1)])
                _ck(6)
                attn_a(i)
                _ck(7)
                p.alias([("atmp", q) for q in range(8)], [("mTc", j) for j in range(16)])
                gates_merge(i)
                _ck(8)
                p.alias(["qT"] + [("gt", q) for q in range(8)] + [("oT", c) for c in range(16)], ["A0", "A1"])
                p.alias([("kvK", 0), ("kvV", 0), ("kvK", 1), ("kvV", 1)] + [("kbK", j) for j in range(4)] + [("kbV", j) for j in range(4)], ["kv0", "kv1"])
                down_proj(mT, lambda k: ("mTc", k), wout_d, KC, fb_mix, FB_MIX_KEYS, "wout")
                post_norm_res(fb_mix, FB_MIX_KEYS, gmixpost_d, 1.0)
                p.alias([("mTc", j) for j in range(16)], ["mT"])
                ffn(1)
                store_x(y_d, i, "y")

        try:
            run_all()
        except _Stop:
            pass
        p.wait_all("sync", list(p.last_w.keys()))
        p.emit()
    return nc


def _host_consts(TPC, rank):
    pos = (rank * TPC + np.arange(TPC)).astype(np.float32)
    inv = (np.float32(10000.0) ** (-np.arange(0, HD, 2, dtype=np.float32) / np.float32(HD))).astype(np.float32)
    ang = (pos[None, :] * inv[:, None]).astype(np.float32)
    c = np.cos(ang.astype(np.float64)).astype(np.float32)
    s = np.sin(ang.astype(np.float64)).astype(np.float32)
    ropeC = np.concatenate([c, c], 0)
    ropeS = np.concatenate([-s, s], 0)
    j = np.arange(128)[:, None]
    i = np.arange(128)[None, :]
    tri_prev = (j >= i).astype(np.float32)
    tri_next = (j <= i).astype(np.float32)
    am = np.zeros((128, 8, 128), np.float32)
    am[:, 0] = tri_prev
    am[:, 1] = tri_next
    for s_, r in enumerate((0, 1, 2)):
        if r == rank - 1:
            am[:, 2 + s_] = tri_prev
    for s_, r in enumerate((1, 2, 3)):
        if r == rank + 1:
            am[:, 5 + s_] = tri_next
    return ropeC, ropeS, am.astype(ml_dtypes.bfloat16), np.eye(128, dtype=np.float32).astype(ml_dtypes.bfloat16)


def make_in_maps(inputs, TPC):
    x = np.asarray(inputs["x"], np.float32)
    xf = x.reshape(-1, D)
    f = lambda k: np.ascontiguousarray(np.asarray(inputs[k], np.float32)[0])
    shared = {k: f(k) for k in ("ffn1_w_gate", "ffn1_w_up", "ffn1_w_down", "ffn2_w_gate", "ffn2_w_up", "ffn2_w_down",
                                 "w_in", "w_proj_a", "w_proj_b", "w_out")}
    for k in ("ffn1_pre_g", "ffn1_post_g", "ffn2_pre_g", "ffn2_post_g", "mix_pre_g", "mix_post_g"):
        shared[k] = np.ascontiguousarray(np.asarray(inputs[k], np.float32).reshape(1, D))
    shared["gate_biasT"] = np.ascontiguousarray(f("gate_bias").reshape(32, 128).T)
    shared["sink_bc"] = np.ascontiguousarray(np.broadcast_to(f("sink_logit").reshape(1, 8), (128, 8)))
    shared["lamv"] = np.ascontiguousarray(np.stack([f("lambda_q1"), f("lambda_q2"), f("lambda_k1"), f("lambda_k2")], 1))
    shared["sublnT"] = np.ascontiguousarray(f("subln_g").reshape(2, 128).T)
    in_maps = []
    for c in range(NCORES):
        rank = c % NR
        ropeC, ropeS, am, ident = _host_consts(TPC, rank)
        m = dict(shared)
        m["x"] = np.ascontiguousarray(xf[c * TPC:(c + 1) * TPC])
        m["ropeC"], m["ropeS"], m["amask"], m["ident"] = ropeC, ropeS, am, ident
        in_maps.append(m)
    return in_maps


_NC_CACHE = {}


def kernel(**inputs):
    x = np.asarray(inputs["x"])
    B, S_, _ = x.shape
    TPC = (B * S_) // NCORES
    if TPC not in _NC_CACHE:
        _NC_CACHE[TPC] = build_nc(TPC)
    nc = _NC_CACHE[TPC]
    in_maps = make_in_maps(inputs, TPC)
    res = run_bass_kernel_spmd(nc, in_maps, core_ids=list(range(NCORES)))
    y = np.concatenate([np.asarray(r["y"], np.float32) for r in res.results], 0)
    return y.reshape(B, S_, D)
```

```python
import math
from contextlib import ExitStack

import numpy as np
import ml_dtypes
import concourse.bass as bass
import concourse.mybir as mybir
from concourse.bass_utils import run_bass_kernel_spmd

F32 = mybir.dt.float32
BF16 = mybir.dt.bfloat16
AF = mybir.ActivationFunctionType
ALU = mybir.AluOpType

NCORES = 8
NR = 4
D = 2048
DFF = 5632
HD = 128
WIN_COLS = 8704
EPS = 1e-6
LAM_INIT = 0.8 - 0.6 * math.exp(-0.3 * 0)
T = 512
KC = D // 128
JF = DFF // 128
SCALE = HD ** -0.5

ENGINES = ("tensor", "vector", "scalar", "gpsimd", "sync")
N_DMA_SEMS = 12


class _Op:
    __slots__ = ("fn", "waits", "semkey", "incval")

    def __init__(self, fn, waits, semkey, incval):
        self.fn = fn
        self.waits = waits
        self.semkey = semkey
        self.incval = incval


class Prog:
    def __init__(self, nc):
        self.nc = nc
        self.ops = {e: [] for e in ENGINES}
        self.cnt = {e: 0 for e in ENGINES}
        self.dma_rr = {"sync": 0, "gpsimd": 0}
        self.dma_cnt = {}
        self.last_w = {}
        self.readers = {}
        self.known = {e: {} for e in ENGINES}
        self.semkeys = [e for e in ENGINES if e != "sync"] + ["cc"]
        for q in ("sync", "gpsimd"):
            for i in range(N_DMA_SEMS):
                self.semkeys.append(("dma", q, i))
                self.dma_cnt[("dma", q, i)] = 0

    def _deps(self, eng, reads, writes):
        need = {}

        def add(tok):
            if tok is None:
                return
            k, v = tok
            if eng == "tensor" and k == "tensor":
                return
            if need.get(k, 0) < v:
                need[k] = v

        for r in reads:
            add(self.last_w.get(r))
        for w in writes:
            add(self.last_w.get(w))
            for t in self.readers.get(w, ()):
                add(t)
        return need

    def _commit(self, tok, reads, writes):
        for w in writes:
            self.last_w[w] = tok
            self.readers[w] = []
        for r in reads:
            self.readers.setdefault(r, []).append(tok)

    def _filter(self, eng, need):
        kn = self.known[eng]
        out = []
        for k, v in need.items():
            if kn.get(k, 0) < v:
                kn[k] = v
                out.append((k, v))
        return out

    def op(self, eng, fn, reads=(), writes=()):
        reads = tuple(reads)
        writes = tuple(writes)
        waits = self._filter(eng, self._deps(eng, reads, writes))
        self.cnt[eng] += 1
        tok = (eng, self.cnt[eng])
        self.ops[eng].append(_Op(fn, waits, eng, 1))
        self._commit(tok, reads, writes)
        return tok

    def dma(self, q, fn, reads=(), writes=()):
        reads = tuple(reads)
        writes = tuple(writes)
        need = self._deps(q, reads, writes)
        i = self.dma_rr[q]
        self.dma_rr[q] = (i + 1) % N_DMA_SEMS
        sk = ("dma", q, i)
        prev = self.dma_cnt[sk]
        if prev and need.get(sk, 0) < 16 * prev:
            need[sk] = 16 * prev
        waits = self._filter(q, need)
        self.dma_cnt[sk] = prev + 1
        tok = (sk, 16 * (prev + 1))
        self.ops[q].append(_Op(fn, waits, sk, 16))
        self._commit(tok, reads, writes)
        return tok

    def cc(self, fn, reads, writes, total):
        reads = tuple(reads)
        writes = tuple(writes)
        waits = self._filter("gpsimd", self._deps("gpsimd", reads, writes))
        self.ops["gpsimd"].append(_Op(fn, waits, "cc", 1))
        self._commit(("cc", total), reads, writes)

    def alias(self, src_keys, dst_keys):
        toks = []
        for s in src_keys:
            if self.last_w.get(s) is not None:
                toks.append(self.last_w[s])
            toks.extend(self.readers.get(s, ()))
        for d in dst_keys:
            self.readers.setdefault(d, []).extend(toks)

    def wait_all(self, eng, keys):
        waits = self._filter(eng, self._deps(eng, keys, keys))
        self.ops[eng].append(_Op(None, waits, None, 0))

    def emit(self):
        nc = self.nc
        with ExitStack() as es:
            sems = {}
            for k in self.semkeys:
                nm = k if isinstance(k, str) else "d_%s_%d" % (k[1], k[2])
                sems[k] = es.enter_context(nc.semaphore("s_" + nm))
            block = es.enter_context(nc.Block())

            def run(eng_name):
                def body(e):
                    for o in self.ops[eng_name]:
                        for (k, v) in o.waits:
                            e.wait_ge(sems[k], v)
                        if o.fn is not None:
                            o.fn(e).then_inc(sems[o.semkey], o.incval)
                return body

            block.tensor(run("tensor"))
            block.vector(run("vector"))
            block.scalar(run("scalar"))
            block.gpsimd(run("gpsimd"))
            block.sync(run("sync"))


class _Stop(Exception):
    pass


import os as _os
_KSTOP = int(_os.environ.get("KSTOP", "99"))


def _ck(n):
    if _KSTOP <= n:
        raise _Stop()


def build_nc(TPC):
    NT = TPC // T
    NB = TPC // 128
    S = NR * TPC
    KG = T
    NKG = NR * NT
    KGC = KG // 128

    nc = bass.Bass("TRN2", target_bir_lowering=False)

    def din(name, shape, dt=F32):
        return nc.dram_tensor(name, list(shape), dt, kind="ExternalInput").ap()

    x_d = din("x", [TPC, D])
    wg_d = [din("ffn1_w_gate", [D, DFF]), din("ffn2_w_gate", [D, DFF])]
    wu_d = [din("ffn1_w_up", [D, DFF]), din("ffn2_w_up", [D, DFF])]
    wd_d = [din("ffn1_w_down", [DFF, D]), din("ffn2_w_down", [DFF, D])]
    gpre_d = [din("ffn1_pre_g", [1, D]), din("ffn2_pre_g", [1, D])]
    gpost_d = [din("ffn1_post_g", [1, D]), din("ffn2_post_g", [1, D])]
    gmixpre_d = din("mix_pre_g", [1, D])
    gmixpost_d = din("mix_post_g", [1, D])
    win_d = din("w_in", [D, WIN_COLS])
    wpa_d = din("w_proj_a", [1024, D])
    wpb_d = din("w_proj_b", [1024, D])
    wout_d = din("w_out", [D, D])
    gbias_d = din("gate_biasT", [128, 32])
    sink_d = din("sink_bc", [128, 8])
    lamv_d = din("lamv", [128, 4])
    subg_d = din("sublnT", [128, 2])
    ropeC_d = din("ropeC", [128, TPC])
    ropeS_d = din("ropeS", [128, TPC])
    amask_d = din("amask", [128, 8, 128], BF16)
    ident_d = din("ident", [128, 128], BF16)
    y_d = nc.dram_tensor("y", [TPC, D], F32, kind="ExternalOutput").ap()

    qsp_d = nc.dram_tensor("q_sp", [16 * 128, TPC], BF16).ap()
    x1sp_d = nc.dram_tensor("x1_sp", [TPC, D], F32).ap()
    def dint(name, shape):
        return nc.dram_tensor(name, list(shape), BF16).ap()
    kTA_l = [dint("kTA_l%d" % i, [768, T]) for i in range(NT)]
    kTB_l = [dint("kTB_l%d" % i, [512, T]) for i in range(NT)]
    vA_l = [dint("vA_l%d" % i, [T, 768]) for i in range(NT)]
    vB_l = [dint("vB_l%d" % i, [T, 512]) for i in range(NT)]
    kTA_g = [dint("kTA_g%d" % i, [NR * 768, T]) for i in range(NT)]
    kTB_g = [dint("kTB_g%d" % i, [NR * 512, T]) for i in range(NT)]
    vA_g = [dint("vA_g%d" % i, [NR * T, 768]) for i in range(NT)]
    vB_g = [dint("vB_g%d" % i, [NR * T, 512]) for i in range(NT)]
    p = Prog(nc)
    es = ExitStack()
    with es:
        def sb(name, shape, dt):
            return es.enter_context(nc.sbuf_tensor(name, list(shape), dt))

        x_sb = sb("x_sb", [128, 4, D], F32)
        regA = sb("regA", [128, 24576], BF16)
        regB = sb("regB", [128, 16384], BF16)
        wbuf = [sb("wbuf%d" % i, [128, 16, 256], BF16) for i in range(4)]
        wdbuf = [sb("wdbuf%d" % i, [128, 4, 512], BF16) for i in range(2)]
        gbuf = sb("gbuf", [128, D], F32)
        xsb = [sb("xsb%d" % i, [128, D], BF16) for i in range(2)]
        sgj = sb("sgj", [128, 1024], F32)
        ptr = sb("ptr", [128, 2048], BF16)
        ropeC = sb("ropeC_sb", [128, T], F32)
        ropeS = sb("ropeS_sb", [128, T], F32)
        esink = sb("esink", [128, 8, 128], F32)
        amask = sb("amask_sb", [128, 8, 128], BF16)
        ident = sb("ident_sb", [128, 128], BF16)
        ones_b = sb("ones_b", [128, 128], BF16)
        ones_f = sb("ones_f", [128, 128], F32)
        stage = [sb("stage%d" % i, [128, T], BF16) for i in range(2)]
        vstage = sb("vstage", [128, 4, 256], BF16)
        gbias = sb("gbias", [128, 32], F32)
        subg = sb("subg", [128, 2], F32)
        lamv = sb("lamv_sb", [128, 4], F32)
        small = sb("small", [128, 64], F32)
        PS = [es.enter_context(nc.psum_tensor("ps%d" % i, [128, 512], F32)) for i in range(8)]

        aT = regA[:, 0:JF * T].rearrange("p (j t) -> p j t", t=T)
        qT = regA[:, 0:8192].rearrange("p (c t) -> p c t", t=T)
        oT = regA[:, 8192:16384].rearrange("p (c t) -> p c t", t=T)
        kvb = [regA[:, 16384 + i * 4096:16384 + (i + 1) * 4096] for i in range(2)]
        fb_mix = regA[:, 0:16384].bitcast(F32).rearrange("p (t d) -> p t d", d=D)
        gtmp = regA[:, 0:8192].bitcast(F32).rearrange("p (i t) -> p i t", t=T)
        hT = regB[:, 0:8192].rearrange("p (k t) -> p k t", t=T)
        mT = regB[:, 8192:16384].rearrange("p (k t) -> p k t", t=T)
        fb_ffn = regB[:].bitcast(F32).rearrange("p (t d) -> p t d", d=D)
        atmp = regB[:, 8192:16384].bitcast(F32).rearrange("p (i t) -> p i t", t=T)
        sg = [sgj[:, 0:512], sgj[:, 512:1024]]
        junk = sgj[:].bitcast(BF16)
        pt = [ptr[:, i * 512:(i + 1) * 512] for i in range(4)]
        rtmp = [ptr[:, 0:1024].bitcast(F32), ptr[:, 1024:2048].bitcast(F32)]

        AT_KEYS = ["A0", "A1", "kv0", "kv1"]
        ss = small[:, 0:4]
        ms = small[:, 4:8]
        sd = small[:, 8:12]
        rstd = small[:, 12:16]
        prod = small[:, 16:18]
        elam = small[:, 18:20]
        neglam = small[:, 20:21]
        esk = small[:, 24:32]
        subg_s = small[:, 32:34]

        wv = lambda w: w.rearrange("(k p) n -> p k n", p=128)

        st = {"wb": 0, "wd": 0, "xs": 0, "sg": 0, "stg": 0, "pt": 0, "kv": 0, "psT": 0}

        wbs = nc.dram_tensor("wbs", [160, 128, 4096], BF16).ap()
        wds = nc.dram_tensor("wds", [112, 128, 2048], BF16).ap()
        scr = {"wb": {}, "wd": {}}

        def load_wbuf(src_ap, k0=0, k1=16, buf=None, uid=None):
            if buf is None:
                buf = st["wb"]
                st["wb"] = (buf + 1) % 4
            keys = [("wb", buf, 0)] if k1 <= 8 else ([("wb", buf, 8)] if k0 >= 8 else [("wb", buf, 0), ("wb", buf, 8)])
            nk = k1 - k0
            dstv = wbuf[buf][:, k0:k1, :]
            if uid in scr["wb"]:
                sc = wbs[scr["wb"][uid], :, 0:nk * 256].rearrange("p (k c) -> p k c", c=256)
                p.dma("gpsimd", lambda e: e.dma_start(out=dstv, in_=sc), reads=[("wbs", uid)], writes=keys)
            else:
                p.dma("gpsimd", lambda e: e.dma_start(out=dstv, in_=src_ap), writes=keys)
                idx = len(scr["wb"])
                scr["wb"][uid] = idx
                sc = wbs[idx, :, 0:nk * 256].rearrange("p (k c) -> p k c", c=256)
                p.dma("sync", lambda e: e.dma_start(out=sc, in_=dstv), reads=keys, writes=[("wbs", uid)])
            return buf

        def load_wd(src_ap, uid=None):
            buf = st["wd"]
            st["wd"] = (buf + 1) % 2
            if uid in scr["wd"]:
                sc = wds[scr["wd"][uid]].rearrange("p (k c) -> p k c", c=512)
                p.dma("gpsimd", lambda e: e.dma_start(out=wdbuf[buf][:], in_=sc), reads=[("wds", uid)], writes=[("wd", buf)])
            else:
                p.dma("gpsimd", lambda e: e.dma_start(out=wdbuf[buf][:], in_=src_ap), writes=[("wd", buf)])
                idx = len(scr["wd"])
                scr["wd"][uid] = idx
                sc = wds[idx].rearrange("p (k c) -> p k c", c=512)
                p.dma("sync", lambda e: e.dma_start(out=sc, in_=wdbuf[buf][:]), reads=[("wd", buf)], writes=[("wds", uid)])
            return buf

        def mm_group(out_ap, pairs, reads, writes):
            n = len(pairs)

            def f(e):
                for i, (l, r) in enumerate(pairs):
                    ins = e.matmul(out_ap, lhsT=l, rhs=r, start=(i == 0), stop=(i == n - 1))
                return ins
            p.op("tensor", f, reads=reads, writes=writes)

        for dst, src, key in ((amask[:], amask_d, "amask"), (ident[:], ident_d, "ident"), (gbias[:], gbias_d, "gbias"),
                              (subg[:], subg_d, "subg"), (lamv[:], lamv_d, "lamv"), (esk, sink_d, "esk")):
            p.dma("sync", lambda e, dst=dst, src=src: e.dma_start(out=dst, in_=src), writes=[key])
        p.op("vector", lambda e: e.memset(ones_b[:], 1.0), writes=["ones_b"])
        p.op("vector", lambda e: e.memset(ones_f[:], 1.0), writes=["ones_f"])
        p.op("scalar", lambda e: e.activation(out=esk, in_=esk, func=AF.Exp), reads=["esk"], writes=["esk"])
        p.op("vector", lambda e: e.tensor_copy(out=esink[:], in_=esk.unsqueeze(2).broadcast_to([128, 8, 128])),
             reads=["esk"], writes=["esink"])
        p.op("vector", lambda e: e.tensor_tensor(out=prod, in0=lamv[:, 0:2], in1=lamv[:, 2:4], op=ALU.mult),
             reads=["lamv"], writes=["prod"])
        mm_group(PS[0][:, 0:2], [(ones_f[:], prod)], reads=["ones_f", "prod"], writes=[("ps", 0)])
        p.op("scalar", lambda e: e.activation(out=elam, in_=PS[0][:, 0:2], func=AF.Exp),
             reads=[("ps", 0)], writes=["elam"])
        p.op("vector", lambda e: e.tensor_tensor(out=neglam, in0=elam[:, 1:2], in1=elam[:, 0:1], op=ALU.subtract),
             reads=["elam"], writes=["neglam"])
        p.op("vector", lambda e: e.tensor_scalar(out=neglam, in0=neglam, scalar1=-LAM_INIT, scalar2=None, op0=ALU.add),
             reads=["neglam"], writes=["neglam"])
        p.op("vector", lambda e: e.tensor_scalar(out=subg_s, in0=subg[:], scalar1=1.0 - LAM_INIT, scalar2=None, op0=ALU.mult),
             reads=["subg"], writes=["subg_s"])

        def load_gain(g_d):
            p.dma("sync", lambda e: e.dma_start(out=gbuf[:], in_=g_d.broadcast_to([128, D])), writes=["gbuf"])

        def rows_rstd(srcs, read_keys):
            for t in range(4):
                p.op("scalar", lambda e, t=t: e.activation(out=junk, in_=srcs[t], func=AF.Square, accum_out=ss[:, t:t + 1]),
                     reads=[read_keys[t]], writes=[("ss", t), "sg0", "sg1"])
            p.op("vector", lambda e: e.tensor_scalar(out=ms, in0=ss, scalar1=1.0 / D, scalar2=EPS, op0=ALU.mult, op1=ALU.add),
                 reads=[("ss", t) for t in range(4)], writes=["ms"])
            p.op("scalar", lambda e: e.activation(out=sd, in_=ms, func=AF.Sqrt), reads=["ms"], writes=["sd"])
            p.op("vector", lambda e: e.reciprocal(out=rstd, in_=sd), reads=["sd"], writes=["rstd"])

        def norm_to_hT(g_d):
            load_gain(g_d)
            rows_rstd([x_sb[:, t, :] for t in range(4)], [("x", t) for t in range(4)])
            for t in range(4):
                xi = st["xs"]
                st["xs"] = 1 - xi
                p.op("vector", lambda e, t=t, xi=xi: e.scalar_tensor_tensor(
                    out=xsb[xi][:], in0=x_sb[:, t, :], scalar=rstd[:, t:t + 1], in1=gbuf[:], op0=ALU.mult, op1=ALU.mult),
                    reads=[("x", t), "rstd", "gbuf"], writes=[("xsb", xi)])
                for half in range(2):
                    b = 4 + st["psT"]
                    st["psT"] = (st["psT"] + 1) % 4
                    psv = PS[b][:].bitcast(BF16)

                    def tr(e, xi=xi, half=half, psv=psv):
                        for kk in range(8):
                            k = half * 8 + kk
                            ins = e.transpose(psv[:, kk * 128:(kk + 1) * 128], xsb[xi][:, k * 128:(k + 1) * 128], ident[:])
                        return ins
                    p.op("tensor", tr, reads=[("xsb", xi), "ident"], writes=[("ps", b)])
                    src = psv.rearrange("p (k t) -> p k t", t=128)
                    dst = hT[:, half * 8:(half + 1) * 8, t * 128:(t + 1) * 128]
                    p.op("scalar", lambda e, src=src, dst=dst: e.activation(out=dst, in_=src, func=AF.Copy),
                         reads=[("ps", b)], writes=["hT"])

        def ffn_stage1(wg, wu, wname, hook=None):
            for j2 in range(JF // 2):
                if j2 == 2 and hook is not None:
                    hook()
                bg = load_wbuf(wv(wg)[:, :, j2 * 256:(j2 + 1) * 256], uid=("g", wname, j2))
                bu = load_wbuf(wv(wu)[:, :, j2 * 256:(j2 + 1) * 256], uid=("u", wname, j2))
                for jj in range(2):
                    j = j2 * 2 + jj
                    pg, pu = (0, 1) if j % 2 == 0 else (2, 3)
                    mm_group(PS[pg][:], [(wbuf[bg][:, k, jj * 128:(jj + 1) * 128], hT[:, k, :]) for k in range(KC)],
                             reads=[("wb", bg, 0), ("wb", bg, 8), "hT"], writes=[("ps", pg)])
                    mm_group(PS[pu][:], [(wbuf[bu][:, k, jj * 128:(jj + 1) * 128], hT[:, k, :]) for k in range(KC)],
                             reads=[("wb", bu, 0), ("wb", bu, 8), "hT"], writes=[("ps", pu)])
                    si = st["sg"]
                    st["sg"] = 1 - si
                    p.op("scalar", lambda e, pg=pg, si=si: e.activation(out=sg[si], in_=PS[pg][:], func=AF.Silu),
                         reads=[("ps", pg)], writes=["sg%d" % si])
                    p.op("vector", lambda e, pu=pu, si=si, j=j: e.tensor_tensor(out=aT[:, j, :], in0=sg[si], in1=PS[pu][:], op=ALU.mult),
                         reads=[("ps", pu), "sg%d" % si], writes=[("aT", j)])

        def down_proj(src, src_key, w_d, nk, fb, fb_keys, wname):
            for n in range(4):
                banks = (4, 5, 6, 7) if n % 2 == 0 else (0, 1, 2, 3)
                ngr = nk // 4
                for kg in range(ngr):
                    b = load_wd(wv(w_d)[:, kg * 4:(kg + 1) * 4, n * 512:(n + 1) * 512], uid=(wname, n, kg))

                    def f(e, kg=kg, b=b, banks=banks, ngr=ngr):
                        for t in range(4):
                            for k in range(4):
                                ins = e.matmul(PS[banks[t]][:], lhsT=src[:, kg * 4 + k, t * 128:(t + 1) * 128], rhs=wdbuf[b][:, k, :],
                                               start=(kg == 0 and k == 0), stop=(kg == ngr - 1 and k == 3))
                        return ins
                    p.op("tensor", f, reads=[("wd", b)] + [src_key(kg * 4 + k) for k in range(4)],
                         writes=[("ps", bk) for bk in banks])
                for t in range(4):
                    dst = fb[:, t, n * 512:(n + 1) * 512]
                    if t % 2 == 0:
                        p.op("vector", lambda e, dst=dst, bk=banks[t]: e.tensor_copy(out=dst, in_=PS[bk][:]),
                             reads=[("ps", banks[t])], writes=[fb_keys[t]])
                    else:
                        p.op("scalar", lambda e, dst=dst, bk=banks[t]: e.activation(out=dst, in_=PS[bk][:], func=AF.Copy),
                             reads=[("ps", banks[t])], writes=[fb_keys[t]])

        def post_norm_res(fb, fb_keys, g_d, factor):
            load_gain(g_d)
            rows_rstd([fb[:, t, :] for t in range(4)], fb_keys)
            for t in range(4):
                p.op("vector", lambda e, t=t: e.scalar_tensor_tensor(
                    out=fb[:, t, :], in0=fb[:, t, :], scalar=rstd[:, t:t + 1], in1=gbuf[:], op0=ALU.mult, op1=ALU.mult),
                    reads=[fb_keys[t], "rstd", "gbuf"], writes=[fb_keys[t]])
                p.op("vector", lambda e, t=t: e.scalar_tensor_tensor(
                    out=x_sb[:, t, :], in0=fb[:, t, :], scalar=float(factor), in1=x_sb[:, t, :], op0=ALU.mult, op1=ALU.add),
                    reads=[fb_keys[t], ("x", t)], writes=[("x", t)])

        FB_FFN_KEYS = ["hT", "hT", "mT", "mT"]
        FB_MIX_KEYS = ["A0", "A0", "A1", "A1"]

        def ffn(l, hook=None):
            norm_to_hT(gpre_d[l])
            p.alias(["A0", "A1", "kv0", "kv1"], [("aT", j) for j in range(JF)])
            ffn_stage1(wg_d[l], wu_d[l], l, hook)
            down_proj(aT, lambda j: ("aT", j), wd_d[l], JF, fb_ffn, FB_FFN_KEYS, ("d", l))
            p.alias([("aT", j) for j in range(JF)], ["A0", "A1", "kv0", "kv1"])
            post_norm_res(fb_ffn, FB_FFN_KEYS, gpost_d[l], 0.5)

        def load_x(src_d, i, rkey=None):
            for t in range(4):
                r0 = i * T + t * 128
                p.dma("sync", lambda e, t=t, r0=r0: e.dma_start(out=x_sb[:, t, :], in_=src_d[r0:r0 + 128, :]),
                      reads=[(rkey, i, t)] if rkey else [], writes=[("x", t)])

        def store_x(dst_d, i, key):
            for t in range(4):
                r0 = i * T + t * 128
                p.dma("sync", lambda e, t=t, r0=r0: e.dma_start(out=dst_d[r0:r0 + 128, :], in_=x_sb[:, t, :]),
                      reads=[("x", t)], writes=[(key, i, t)])

        def qkv(i):
            c0 = i * T
            p.dma("sync", lambda e: e.dma_start(out=ropeC[:], in_=ropeC_d[:, c0:c0 + T]), writes=["ropeC"])
            p.dma("sync", lambda e: e.dma_start(out=ropeS[:], in_=ropeS_d[:, c0:c0 + T]), writes=["ropeS"])
            fm = []
            for c in range(8):
                fm.append((c * 128, qsp_d[c * 128:(c + 1) * 128, c0:c0 + T], ("qsp", i)))
            for g in range(2):
                fm.append((1024 + g * 128, kTA_l[i][g * 128:(g + 1) * 128, :], ("kTA", i, g)))
            for c in range(8):
                fm.append((1536 + c * 128, qsp_d[(8 + c) * 128:(9 + c) * 128, c0:c0 + T], ("qsp", i)))
            for c in range(8):
                if c < 4:
                    fm.append((2560 + c * 128, kTA_l[i][(2 + c) * 128:(3 + c) * 128, :], ("kTA", i, 2 + c)))
                else:
                    fm.append((2560 + c * 128, kTB_l[i][(c - 4) * 128:(c - 3) * 128, :], ("kTB", i, c - 4)))
            for pr in range(len(fm) // 2):
                col0 = fm[2 * pr][0]
                b = load_wbuf(wv(win_d)[:, :, col0:col0 + 256], uid=("in", col0))
                for jj in range(2):
                    _, dst_d, dkey = fm[2 * pr + jj]
                    bk = (2 * pr + jj) % 4
                    mm_group(PS[bk][:], [(wbuf[b][:, k, jj * 128:(jj + 1) * 128], hT[:, k, :]) for k in range(KC)],
                             reads=[("wb", b, 0), ("wb", b, 8), "hT"], writes=[("ps", bk)])
                    p.op("vector", lambda e, bk=bk: e.tensor_tensor(out=rtmp[0], in0=PS[bk][:], in1=ropeC[:], op=ALU.mult),
                         reads=[("ps", bk), "ropeC"], writes=["pt0", "pt1"])
                    p.op("vector", lambda e, bk=bk: e.tensor_tensor(out=rtmp[1][0:64, :], in0=PS[bk][64:128, :], in1=ropeS[0:64, :], op=ALU.mult),
                         reads=[("ps", bk), "ropeS"], writes=["pt2"])
                    p.op("vector", lambda e, bk=bk: e.tensor_tensor(out=rtmp[1][64:128, :], in0=PS[bk][0:64, :], in1=ropeS[64:128, :], op=ALU.mult),
                         reads=[("ps", bk), "ropeS"], writes=["pt3"])
                    si = st["stg"]
                    st["stg"] = 1 - si
                    p.op("vector", lambda e, si=si: e.tensor_tensor(out=stage[si][:], in0=rtmp[0], in1=rtmp[1], op=ALU.add),
                         reads=["pt0", "pt1", "pt2", "pt3"], writes=[("stage", si)])
                    p.dma("sync", lambda e, si=si, dst_d=dst_d: e.dma_start(out=dst_d, in_=stage[si][:]),
                          reads=[("stage", si)], writes=[dkey if dkey[0] != "qsp" else ("qsp", i, 2 * pr + jj)])
            vs = [(1280, vA_l[i], 0, "vA"), (3584, vA_l[i], 256, "vA"), (3840, vA_l[i], 512, "vA"),
                  (4096, vB_l[i], 0, "vB"), (4352, vB_l[i], 256, "vB")]
            for (col0, vdst, vc0, vkey) in vs:
                b = load_wbuf(wv(win_d)[:, :, col0:col0 + 256], uid=("in", col0))
                for t in range(4):
                    bk = 4 + t
                    mm_group(PS[bk][:, 0:256],
                             [(hT[:, k, t * 128:(t + 1) * 128], wbuf[b][:, k, :]) for k in range(KC)],
                             reads=[("wb", b, 0), ("wb", b, 8), "hT"], writes=[("ps", bk)])
                    p.op("scalar", lambda e, bk=bk, t=t: e.activation(out=vstage[:, t, :], in_=PS[bk][:, 0:256], func=AF.Copy),
                         reads=[("ps", bk)], writes=["vstage"])
                dst = vdst[:, vc0:vc0 + 256].rearrange("(t p) c -> p t c", p=128)
                p.dma("sync", lambda e, dst=dst: e.dma_start(out=dst, in_=vstage[:]), reads=["vstage"], writes=[(vkey, i, vc0)])

        def run_all():
            groups = [list(range(g * NR, (g + 1) * NR)) for g in range(NCORES // NR)]
            kTA_keys = lambda i: [("kTA", i, c) for c in range(6)]
            kTB_keys = lambda i: [("kTB", i, c) for c in range(4)]
            vA_keys = lambda i: [("vA", i, c) for c in (0, 256, 512)]
            vB_keys = lambda i: [("vB", i, c) for c in (0, 256)]

            def exchange(i):
                for (src, dst, rk, wk) in ((kTA_l[i], kTA_g[i], kTA_keys(i), ("kTAg", i)), (kTB_l[i], kTB_g[i], kTB_keys(i), ("kTBg", i)),
                                           (vA_l[i], vA_g[i], vA_keys(i), ("vAg", i)), (vB_l[i], vB_g[i], vB_keys(i), ("vBg", i))):
                    p.cc(lambda e, src=src, dst=dst: e.collective_compute(
                        "AllGather", ALU.bypass, replica_groups=groups, ins=[src], outs=[dst]), rk, [wk], 4 * NT)

            _ck(1)
            for i in range(NT):
                load_x(x_d, i)
                norm_to_hT(gpre_d[0]) if _KSTOP == 2 else None
                _ck(2)
                ffn(0, hook=(lambda i=i: exchange(i - 1)) if i > 0 else None)
                _ck(3)
                store_x(x1sp_d, i, "x1sp")
                norm_to_hT(gmixpre_d)
                qkv(i)
            _ck(4)

            exchange(NT - 1)
            _ck(5)

            def next_pt():
                i = st["pt"]
                st["pt"] = (i + 1) % 4
                return i

            def attn_b(i):
                for h in range(4):
                    units = [(kg, comp, c) for kg in range(NKG) for comp in range(2) for c in range(KGC)]
                    U = len(units)
                    loaded = {}
                    ptidx = {}
                    acc_cnt = [0, 0]

                    def ensure_loaded(kg, h=h, loaded=loaded):
                        if kg in loaded:
                            return loaded[kg]
                        r, ti = kg // NT, kg % NT
                        kb = st["kv"]
                        st["kv"] = (kb + 1) % 4
                        base = kvb[kb // 2][:, (kb % 2) * 2048:(kb % 2 + 1) * 2048]
                        kbuf = base[:, 0:2 * KG].rearrange("p (c k) -> p c k", k=KG)
                        vbuf = base[:, 2 * KG:2 * KG + KGC * 256].rearrange("p (c e) -> p c e", e=256)
                        if h < 2:
                            row0 = r * 768 + (2 + h * 2) * 128
                            ksrc = kTA_g[ti][row0:row0 + 256, :].rearrange("(c p) k -> p c k", p=128)
                            vsrc = vA_g[ti][r * T:(r + 1) * T, 256 + h * 256:512 + h * 256].rearrange("(c p) e -> p c e", p=128)
                            kkey, vkey = ("kTAg", ti), ("vAg", ti)
                        else:
                            row0 = r * 512 + (h - 2) * 256
                            ksrc = kTB_g[ti][row0:row0 + 256, :].rearrange("(c p) k -> p c k", p=128)
                            vsrc = vB_g[ti][r * T:(r + 1) * T, (h - 2) * 256:(h - 1) * 256].rearrange("(c p) e -> p c e", p=128)
                            kkey, vkey = ("kTBg", ti), ("vBg", ti)
                        p.dma("sync", lambda e, kbuf=kbuf, ksrc=ksrc: e.dma_start(out=kbuf, in_=ksrc),
                              reads=[kkey], writes=[("kbK", kb)])
                        p.dma("sync", lambda e, vbuf=vbuf, vsrc=vsrc: e.dma_start(out=vbuf, in_=vsrc),
                              reads=[vkey], writes=[("kbV", kb)])
                        loaded[kg] = (kb, kbuf, vbuf)
                        return loaded[kg]

                    SB = ((0, 1), (6, 7))

                    def sbank(u):
                        return SB[(u // 2) % 2][u % 2]

                    def Sp(k, h=h):
                        prs, rds, wrs = [], ["qT"], []
                        for u in (2 * k, 2 * k + 1):
                            kg, comp, c = units[u]
                            kb, kbuf, vbuf = ensure_loaded(kg)
                            prs.append((PS[sbank(u)][:], kbuf[:, comp, c * 128:(c + 1) * 128], qT[:, 8 + h * 2 + comp, :]))
                            rds.append(("kbK", kb))
                            wrs.append(("ps", sbank(u)))

                        def f(e, prs=prs):
                            for (o_, l_, r_) in prs:
                                ins = e.matmul(o_, lhsT=l_, rhs=r_, start=True, stop=True)
                            return ins
                        p.op("tensor", f, reads=rds, writes=wrs)

                    def Ep(k, ptidx=ptidx):
                        for u in (2 * k, 2 * k + 1):
                            sbk = sbank(u)
                            pi = next_pt()
                            ptidx[u] = pi
                            p.op("scalar", lambda e, sbk=sbk, pi=pi: e.activation(out=pt[pi], in_=PS[sbk][:], func=AF.Exp, scale=SCALE),
                                 reads=[("ps", sbk)], writes=["pt%d" % pi])

                    def PVp(k, ptidx=ptidx, acc_cnt=acc_cnt):
                        mms, rds, wrs = [], [], []
                        for u in (2 * k, 2 * k + 1):
                            kg, comp, c = units[u]
                            kb, kbuf, vbuf = ensure_loaded(kg)
                            pi = ptidx[u]
                            first = (kg == 0 and c == 0)
                            last = (kg == NKG - 1 and c == KGC - 1)
                            ob = 2 + comp * 2
                            mms.append((PS[ob][:], vbuf[:, c, 0:128], pt[pi], first, last))
                            mms.append((PS[ob + 1][:], vbuf[:, c, 128:256], pt[pi], first, last))
                            rds += [("kbV", kb), "pt%d" % pi]
                            wrs += [("ps", ob), ("ps", ob + 1)]

                        def f(e, mms=mms):
                            for (o_, l_, r_, fi, la) in mms:
                                ins = e.matmul(o_, lhsT=l_, rhs=r_, start=fi, stop=la)
                            return ins
                        p.op("tensor", f, reads=rds, writes=wrs)
                        for u in (2 * k, 2 * k + 1):
                            kg, comp, c = units[u]
                            pi = ptidx[u]
                            k_ = acc_cnt[comp]
                            acc_cnt[comp] += 1
                            on_pool = (k_ % 3 == 2)
                            eng = "gpsimd" if on_pool else "vector"
                            acc = atmp[:, 5 + comp, :] if on_pool else sg[comp]
                            akey = ("atmp", 5 + comp) if on_pool else "sg%d" % comp
                            if k_ == 0 or k_ == 2:
                                p.op(eng, lambda e, acc=acc, pi=pi: e.tensor_copy(out=acc, in_=pt[pi]),
                                     reads=["pt%d" % pi], writes=[akey])
                            else:
                                p.op(eng, lambda e, acc=acc, pi=pi: e.tensor_tensor(out=acc, in0=acc, in1=pt[pi], op=ALU.add),
                                     reads=["pt%d" % pi, akey], writes=[akey])

                    NPR = U // 2
                    Sp(0)
                    Sp(1)
                    Ep(0)
                    for k in range(NPR):
                        PVp(k)
                        if k + 2 < NPR:
                            Sp(k + 2)
                        if k + 1 < NPR:
                            Ep(k + 1)
                    r1, r2, ta, o0, o1, sq0, sq1, rn = [atmp[:, q, :] for q in range(8)]
                    K = lambda q: ("atmp", q)
                    mm_group(PS[6][:], [(ones_f[:], sg[0]), (ones_f[:], atmp[:, 5, :])], reads=["ones_f", "sg0", ("atmp", 5)], writes=[("ps", 6)])
                    mm_group(PS[7][:], [(ones_f[:], sg[1]), (ones_f[:], atmp[:, 6, :])], reads=["ones_f", "sg1", ("atmp", 6)], writes=[("ps", 7)])
                    p.op("vector", lambda e: e.reciprocal(out=r1, in_=PS[6][:]), reads=[("ps", 6)], writes=[K(0)])
                    p.op("vector", lambda e: e.reciprocal(out=r2, in_=PS[7][:]), reads=[("ps", 7)], writes=[K(1)])
                    p.op("vector", lambda e: e.tensor_scalar(out=r2, in0=r2, scalar1=neglam, scalar2=None, op0=ALU.mult),
                         reads=[K(1), "neglam"], writes=[K(1)])
                    for ec, oo, sq in ((0, o0, sq0), (1, o1, sq1)):
                        p.op("vector", lambda e, ec=ec: e.tensor_tensor(out=ta, in0=PS[2 + ec][:], in1=r1, op=ALU.mult),
                             reads=[("ps", 2 + ec), K(0)], writes=[K(2)])
                        p.op("vector", lambda e, ec=ec, oo=oo: e.tensor_tensor(out=oo, in0=PS[4 + ec][:], in1=r2, op=ALU.mult),
                             reads=[("ps", 4 + ec), K(1)], writes=[K(3 + ec)])
                        p.op("vector", lambda e, oo=oo: e.tensor_tensor(out=oo, in0=oo, in1=ta, op=ALU.add),
                             reads=[K(2), K(3 + ec)], writes=[K(3 + ec)])
                        p.op("scalar", lambda e, oo=oo, sq=sq: e.activation(out=sq, in_=oo, func=AF.Square),
                             reads=[K(3 + ec)], writes=[K(5 + ec)])
                    mm_group(PS[0][:], [(ones_f[:], sq0), (ones_f[:], sq1)], reads=["ones_f", K(5), K(6)], writes=[("ps", 0)])
                    p.op("vector", lambda e: e.tensor_scalar(out=rn, in0=PS[0][:], scalar1=1.0 / 256.0, scalar2=EPS, op0=ALU.mult, op1=ALU.add),
                         reads=[("ps", 0)], writes=[K(7)])
                    p.op("scalar", lambda e: e.activation(out=rn, in_=rn, func=AF.Sqrt), reads=[K(7)], writes=[K(7)])
                    p.op("vector", lambda e: e.reciprocal(out=rn, in_=rn), reads=[K(7)], writes=[K(7)])
                    for ec, oo in ((0, o0), (1, o1)):
                        p.op("vector", lambda e, ec=ec, oo=oo, h=h: e.scalar_tensor_tensor(
                            out=oT[:, 8 + h * 2 + ec, :], in0=oo, scalar=subg_s[:, ec:ec + 1], in1=rn, op0=ALU.mult, op1=ALU.mult),
                            reads=[K(3 + ec), K(7), "subg_s"], writes=[("oT", 8 + h * 2 + ec)])

            def attn_a(i):
                n0 = i * 4
                lo = max(n0 - 1, 0)
                hi = min(n0 + 4, NB - 1)
                nblk = hi - lo + 1
                kab = kvb[0][:, 0:2 * 768].rearrange("p (g k) -> p g k", k=768)
                vab = kvb[0][:, 1536:1536 + 6 * 256].rearrange("p (b e) -> p b e", e=256)
                kcb = kvb[1][:, 0:2 * 768].rearrange("p (g k) -> p g k", k=768)
                vcb = kvb[1][:, 1536:1536 + 6 * 256].rearrange("p (b e) -> p b e", e=256)
                for m in range(lo, hi + 1):
                    ti, bi = m // 4, m % 4
                    p.dma("sync", lambda e, m=m, ti=ti, bi=bi: e.dma_start(
                        out=kab[:, :, (m - lo) * 128:(m - lo + 1) * 128],
                        in_=kTA_l[ti][0:256, bi * 128:(bi + 1) * 128].rearrange("(g p) k -> p g k", p=128)),
                        reads=kTA_keys(ti), writes=[("kvK", 0), ("kvV", 0)])
                    p.dma("sync", lambda e, m=m, ti=ti, bi=bi: e.dma_start(
                        out=vab[:, m - lo, :], in_=vA_l[ti][bi * 128:(bi + 1) * 128, 0:256]),
                        reads=vA_keys(ti), writes=[("kvK", 0), ("kvV", 0)])
                cands = []
                if i == 0:
                    cands += [(s_, r, "prev") for s_, r in enumerate((0, 1, 2))]
                if i == NT - 1:
                    cands += [(3 + s_, r, "next") for s_, r in enumerate((1, 2, 3))]
                for (slot, r, which) in cands:
                    ti = NT - 1 if which == "prev" else 0
                    col0 = T - 128 if which == "prev" else 0
                    p.dma("sync", lambda e, slot=slot, r=r, col0=col0, ti=ti: e.dma_start(
                        out=kcb[:, :, slot * 128:(slot + 1) * 128],
                        in_=kTA_g[ti][r * 768:r * 768 + 256, col0:col0 + 128].rearrange("(g p) k -> p g k", p=128)),
                        reads=[("kTAg", ti)], writes=[("kvK", 1), ("kvV", 1)])
                    p.dma("sync", lambda e, slot=slot, r=r, col0=col0, ti=ti: e.dma_start(
                        out=vcb[:, slot, :], in_=vA_g[ti][r * T + col0:r * T + col0 + 128, 0:256]),
                        reads=[("vAg", ti)], writes=[("kvK", 1), ("kvV", 1)])
                units = []
                for nb in range(4):
                    n = n0 + nb
                    for g in range(2):
                        chunks = []
                        own = lambda m: (kab[:, g, (m - lo) * 128:(m - lo + 1) * 128], vab[:, m - lo, g * 128:(g + 1) * 128], 0)
                        cnd = lambda slot: (kcb[:, g, slot * 128:(slot + 1) * 128], vcb[:, slot, g * 128:(g + 1) * 128], 1)
                        if n == 0:
                            for s_ in range(3):
                                chunks.append(cnd(s_) + (2 + s_,))
                        else:
                            chunks.append(own(n - 1) + (0,))
                        chunks.append(own(n) + (None,))
                        if n == NB - 1:
                            for s_ in range(3):
                                chunks.append(cnd(3 + s_) + (5 + s_,))
                        else:
                            chunks.append(own(n + 1) + (1,))
                        qv = qT[:, g * 4:(g + 1) * 4, nb * 128:(nb + 1) * 128]
                        idx = nb * 2 + g
                        for ci, (kap, vap, which, mi) in enumerate(chunks):
                            units.append(dict(kap=kap, vap=vap, which=which, mi=mi, qv=qv, ob=2 + idx % 2, sb=4 + idx % 2,
                                              first=(ci == 0), last=(ci == len(chunks) - 1), nb=nb, g=g, idx=idx))
                U = len(units)
                ptidx = {}

                def S_(u):
                    d = units[u]
                    sbk = u % 2
                    mm_group(PS[sbk][:], [(d["kap"], d["qv"])], reads=[("kvK", d["which"]), ("kvV", d["which"]), "qT"], writes=[("ps", sbk)])

                def E_(u):
                    d = units[u]
                    sbk = u % 2
                    pi = next_pt()
                    ptidx[u] = pi
                    p.op("scalar", lambda e, sbk=sbk, pi=pi: e.activation(out=pt[pi], in_=PS[sbk][:], func=AF.Exp, scale=SCALE),
                         reads=[("ps", sbk)], writes=["pt%d" % pi])
                    if d["mi"] is not None:
                        mi = d["mi"]
                        ptv = pt[pi].rearrange("p (r q) -> p r q", q=128)
                        mv = amask[:, mi:mi + 1, :].broadcast_to([128, 4, 128])
                        p.op("vector", lambda e, ptv=ptv, mv=mv: e.tensor_tensor(out=ptv, in0=ptv, in1=mv, op=ALU.mult),
                             reads=["pt%d" % pi, "amask"], writes=["pt%d" % pi])

                def PV_(u):
                    d = units[u]
                    pi = ptidx[u]
                    ob, sb_, g, nb = d["ob"], d["sb"], d["g"], d["nb"]

                    def f(e, vap=d["vap"], pi=pi, ob=ob, sb_=sb_, first=d["first"], last=d["last"]):
                        e.matmul(PS[ob][:], lhsT=vap, rhs=pt[pi], start=first, stop=last)
                        return e.matmul(PS[sb_][:], lhsT=ones_b[:], rhs=pt[pi], start=first, stop=last)
                    p.op("tensor", f, reads=[("kvK", d["which"]), ("kvV", d["which"]), "pt%d" % pi, "ones_b"], writes=[("ps", ob), ("ps", sb_)])
                    if d["last"]:
                        den = atmp[:, d["idx"] % 2, :]
                        dk = ("atmp", d["idx"] % 2)
                        p.op("vector", lambda e, den=den, sb_=sb_, g=g: e.tensor_tensor(
                            out=den, in0=PS[sb_][:], in1=esink[:, g * 4:(g + 1) * 4, :].rearrange("p r q -> p (r q)"), op=ALU.add),
                            reads=[("ps", sb_), "esink"], writes=[dk])
                        p.op("vector", lambda e, den=den: e.reciprocal(out=den, in_=den), reads=[dk], writes=[dk])
                        dst = oT[:, g * 4:(g + 1) * 4, nb * 128:(nb + 1) * 128]
                        p.op("vector", lambda e, dst=dst, ob=ob, den=den: e.tensor_tensor(
                            out=dst, in0=PS[ob][:].rearrange("p (r q) -> p r q", q=128), in1=den.rearrange("p (r q) -> p r q", q=128), op=ALU.mult),
                            reads=[("ps", ob), dk], writes=[("oT", g * 4 + r_) for r_ in range(4)])

                S_(0)
                S_(1)
                E_(0)
                for u in range(U):
                    PV_(u)
                    if u + 2 < U:
                        S_(u + 2)
                    if u + 1 < U:
                        E_(u + 1)

            def gates_merge(i):
                p.alias(["qT"], [("gt", q) for q in range(8)])
                oT_keys = [("oT", c) for c in range(16)]
                for j2 in range(8):
                    ba = load_wbuf(wv(win_d)[:, :, 4608 + j2 * 256:4608 + (j2 + 1) * 256], uid=("in", 4608 + j2 * 256))
                    bb = load_wbuf(wv(win_d)[:, :, 6656 + j2 * 256:6656 + (j2 + 1) * 256], uid=("in", 6656 + j2 * 256))
                    bp = load_wbuf(wv(wpa_d)[:, :, j2 * 256:(j2 + 1) * 256], 0, 8, uid=("pa", j2))
                    load_wbuf(wv(wpb_d)[:, :, j2 * 256:(j2 + 1) * 256], 8, 16, buf=bp, uid=("pb", j2))
                    for jj in range(2):
                        j = j2 * 2 + jj
                        bs = (0, 1, 2, 3) if j % 2 == 0 else (4, 5, 6, 7)
                        cs = slice(jj * 128, (jj + 1) * 128)
                        mm_group(PS[bs[0]][:], [(wbuf[ba][:, k, cs], hT[:, k, :]) for k in range(KC)],
                                 reads=[("wb", ba, 0), ("wb", ba, 8), "hT"], writes=[("ps", bs[0])])
                        mm_group(PS[bs[1]][:], [(wbuf[bb][:, k, cs], hT[:, k, :]) for k in range(KC)],
                                 reads=[("wb", bb, 0), ("wb", bb, 8), "hT"], writes=[("ps", bs[1])])
                        mm_group(PS[bs[2]][:], [(wbuf[bp][:, k, cs], oT[:, k, :]) for k in range(8)],
                                 reads=[("wb", bp, 0)] + oT_keys, writes=[("ps", bs[2])])
                        mm_group(PS[bs[3]][:], [(wbuf[bp][:, 8 + k, cs], oT[:, 8 + k, :]) for k in range(8)],
                                 reads=[("wb", bp, 8)] + oT_keys, writes=[("ps", bs[3])])
                        q0 = (j % 2) * 4
                        ga, gb_, ta, tb = [gtmp[:, q0 + q, :] for q in range(4)]
                        GK = lambda q: ("gt", q0 + q)
                        p.op("scalar", lambda e, ga=ga, b0=bs[0], j=j: e.activation(out=ga, in_=PS[b0][:], func=AF.Sigmoid, bias=gbias[:, j:j + 1]),
                             reads=[("ps", bs[0]), "gbias"], writes=[GK(0)])
                        p.op("scalar", lambda e, gb_=gb_, b1=bs[1], j=j: e.activation(out=gb_, in_=PS[b1][:], func=AF.Sigmoid, bias=gbias[:, 16 + j:17 + j]),
                             reads=[("ps", bs[1]), "gbias"], writes=[GK(1)])
                        p.op("vector", lambda e, ga=ga, ta=ta, b2=bs[2]: e.tensor_tensor(out=ta, in0=PS[b2][:], in1=ga, op=ALU.mult),
                             reads=[("ps", bs[2]), GK(0)], writes=[GK(2)])
                        p.op("vector", lambda e, gb_=gb_, tb=tb, b3=bs[3]: e.tensor_tensor(out=tb, in0=PS[b3][:], in1=gb_, op=ALU.mult),
                             reads=[("ps", bs[3]), GK(1)], writes=[GK(3)])
                        p.op("vector", lambda e, ta=ta, tb=tb, j=j: e.tensor_tensor(out=mT[:, j, :], in0=ta, in1=tb, op=ALU.add),
                             reads=[GK(2), GK(3)], writes=[("mTc", j)])

            for i in range(NT):
                c0 = i * T
                load_x(x1sp_d, i, "x1sp")
                norm_to_hT(gmixpre_d)
                p.alias(["A0", "A1"], ["qT"] + [("oT", c) for c in range(16)])
                p.alias(["kv0", "kv1"], [("kbK", j) for j in range(4)] + [("kbV", j) for j in range(4)])
                p.alias(["mT"] + [("mTc", j) for j in range(16)], [("atmp", q) for q in range(8)])
                p.dma("sync", lambda e, c0=c0: e.dma_start(out=qT, in_=qsp_d[:, c0:c0 + T].rearrange("(c p) t -> p c t", p=128)),
                      reads=[("qsp", i, c) for c in range(26)], writes=["qT"])
                attn_b(i)
                p.alias([("kbK", j) for j in range(4)] + [("kbV", j) for j in range(4)], [("kvK", 0), ("kvV", 0), ("kvK", 1), ("kvV", 1)])
                _ck(6)
                attn_a(i)
                _ck(7)
                p.alias([("atmp", q) for q in range(8)], [("mTc", j) for j in range(16)])
                gates_merge(i)
                _ck(8)
                p.alias(["qT"] + [("gt", q) for q in range(8)] + [("oT", c) for c in range(16)], ["A0", "A1"])
                p.alias([("kvK", 0), ("kvV", 0), ("kvK", 1), ("kvV", 1)] + [("kbK", j) for j in range(4)] + [("kbV", j) for j in range(4)], ["kv0", "kv1"])
                down_proj(mT, lambda k: ("mTc", k), wout_d, KC, fb_mix, FB_MIX_KEYS, "wout")
                post_norm_res(fb_mix, FB_MIX_KEYS, gmixpost_d, 1.0)
                p.alias([("mTc", j) for j in range(16)], ["mT"])
                ffn(1)
                store_x(y_d, i, "y")

        try:
            run_all()
        except _Stop:
            pass
        p.wait_all("sync", list(p.last_w.keys()))
        p.emit()
    return nc


def _host_consts(TPC, rank):
    pos = (rank * TPC + np.arange(TPC)).astype(np.float32)
    inv = (np.float32(10000.0) ** (-np.arange(0, HD, 2, dtype=np.float32) / np.float32(HD))).astype(np.float32)
    ang = (pos[None, :] * inv[:, None]).astype(np.float32)
    c = np.cos(ang.astype(np.float64)).astype(np.float32)
    s = np.sin(ang.astype(np.float64)).astype(np.float32)
    ropeC = np.concatenate([c, c], 0)
    ropeS = np.concatenate([-s, s], 0)
    j = np.arange(128)[:, None]
    i = np.arange(128)[None, :]
    tri_prev = (j >= i).astype(np.float32)
    tri_next = (j <= i).astype(np.float32)
    am = np.zeros((128, 8, 128), np.float32)
    am[:, 0] = tri_prev
    am[:, 1] = tri_next
    for s_, r in enumerate((0, 1, 2)):
        if r == rank - 1:
            am[:, 2 + s_] = tri_prev
    for s_, r in enumerate((1, 2, 3)):
        if r == rank + 1:
            am[:, 5 + s_] = tri_next
    return ropeC, ropeS, am.astype(ml_dtypes.bfloat16), np.eye(128, dtype=np.float32).astype(ml_dtypes.bfloat16)


def make_in_maps(inputs, TPC):
    x = np.asarray(inputs["x"], np.float32)
    xf = x.reshape(-1, D)
    f = lambda k: np.ascontiguousarray(np.asarray(inputs[k], np.float32)[0])
    shared = {k: f(k) for k in ("ffn1_w_gate", "ffn1_w_up", "ffn1_w_down", "ffn2_w_gate", "ffn2_w_up", "ffn2_w_down",
                                 "w_in", "w_proj_a", "w_proj_b", "w_out")}
    for k in ("ffn1_pre_g", "ffn1_post_g", "ffn2_pre_g", "ffn2_post_g", "mix_pre_g", "mix_post_g"):
        shared[k] = np.ascontiguousarray(np.asarray(inputs[k], np.float32).reshape(1, D))
    shared["gate_biasT"] = np.ascontiguousarray(f("gate_bias").reshape(32, 128).T)
    shared["sink_bc"] = np.ascontiguousarray(np.broadcast_to(f("sink_logit").reshape(1, 8), (128, 8)))
    shared["lamv"] = np.ascontiguousarray(np.stack([f("lambda_q1"), f("lambda_q2"), f("lambda_k1"), f("lambda_k2")], 1))
    shared["sublnT"] = np.ascontiguousarray(f("subln_g").reshape(2, 128).T)
    in_maps = []
    for c in range(NCORES):
        rank = c % NR
        ropeC, ropeS, am, ident = _host_consts(TPC, rank)
        m = dict(shared)
        m["x"] = np.ascontiguousarray(xf[c * TPC:(c + 1) * TPC])
        m["ropeC"], m["ropeS"], m["amask"], m["ident"] = ropeC, ropeS, am, ident
        in_maps.append(m)
    return in_maps


_NC_CACHE = {}


def kernel(**inputs):
    x = np.asarray(inputs["x"])
    B, S_, _ = x.shape
    TPC = (B * S_) // NCORES
    if TPC not in _NC_CACHE:
        _NC_CACHE[TPC] = build_nc(TPC)
    nc = _NC_CACHE[TPC]
    in_maps = make_in_maps(inputs, TPC)
    res = run_bass_kernel_spmd(nc, in_maps, core_ids=list(range(NCORES)))
    y = np.concatenate([np.asarray(r["y"], np.float32) for r in res.results], 0)
    return y.reshape(B, S_, D)
```

```python
import math
from contextlib import ExitStack

import numpy as np
import ml_dtypes
import concourse.bass as bass
import concourse.mybir as mybir
from concourse.bass_utils import run_bass_kernel_spmd

F32 = mybir.dt.float32
BF16 = mybir.dt.bfloat16
AF = mybir.ActivationFunctionType
ALU = mybir.AluOpType

NCORES = 8
NR = 4
D = 2048
DFF = 5632
HD = 128
WIN_COLS = 8704
EPS = 1e-6
LAM_INIT = 0.8 - 0.6 * math.exp(-0.3 * 0)
T = 512
KC = D // 128
JF = DFF // 128
SCALE = HD ** -0.5

ENGINES = ("tensor", "vector", "scalar", "gpsimd", "sync")
N_DMA_SEMS = 12


class _Op:
    __slots__ = ("fn", "waits", "semkey", "incval")

    def __init__(self, fn, waits, semkey, incval):
        self.fn = fn
        self.waits = waits
        self.semkey = semkey
        self.incval = incval


class Prog:
    def __init__(self, nc):
        self.nc = nc
        self.ops = {e: [] for e in ENGINES}
        self.cnt = {e: 0 for e in ENGINES}
        self.dma_rr = {"sync": 0, "gpsimd": 0}
        self.dma_cnt = {}
        self.last_w = {}
        self.readers = {}
        self.known = {e: {} for e in ENGINES}
        self.semkeys = [e for e in ENGINES if e != "sync"] + ["cc"]
        for q in ("sync", "gpsimd"):
            for i in range(N_DMA_SEMS):
                self.semkeys.append(("dma", q, i))
                self.dma_cnt[("dma", q, i)] = 0

    def _deps(self, eng, reads, writes):
        need = {}

        def add(tok):
            if tok is None:
                return
            k, v = tok
            if eng == "tensor" and k == "tensor":
                return
            if need.get(k, 0) < v:
                need[k] = v

        for r in reads:
            add(self.last_w.get(r))
        for w in writes:
            add(self.last_w.get(w))
            for t in self.readers.get(w, ()):
                add(t)
        return need

    def _commit(self, tok, reads, writes):
        for w in writes:
            self.last_w[w] = tok
            self.readers[w] = []
        for r in reads:
            self.readers.setdefault(r, []).append(tok)

    def _filter(self, eng, need):
        kn = self.known[eng]
        out = []
        for k, v in need.items():
            if kn.get(k, 0) < v:
                kn[k] = v
                out.append((k, v))
        return out

    def op(self, eng, fn, reads=(), writes=()):
        reads = tuple(reads)
        writes = tuple(writes)
        waits = self._filter(eng, self._deps(eng, reads, writes))
        self.cnt[eng] += 1
        tok = (eng, self.cnt[eng])
        self.ops[eng].append(_Op(fn, waits, eng, 1))
        self._commit(tok, reads, writes)
        return tok

    def dma(self, q, fn, reads=(), writes=()):
        reads = tuple(reads)
        writes = tuple(writes)
        need = self._deps(q, reads, writes)
        i = self.dma_rr[q]
        self.dma_rr[q] = (i + 1) % N_DMA_SEMS
        sk = ("dma", q, i)
        prev = self.dma_cnt[sk]
        if prev and need.get(sk, 0) < 16 * prev:
            need[sk] = 16 * prev
        waits = self._filter(q, need)
        self.dma_cnt[sk] = prev + 1
        tok = (sk, 16 * (prev + 1))
        self.ops[q].append(_Op(fn, waits, sk, 16))
        self._commit(tok, reads, writes)
        return tok

    def cc(self, fn, reads, writes, total):
        reads = tuple(reads)
        writes = tuple(writes)
        waits = self._filter("gpsimd", self._deps("gpsimd", reads, writes))
        self.ops["gpsimd"].append(_Op(fn, waits, "cc", 1))
        self._commit(("cc", total), reads, writes)

    def alias(self, src_keys, dst_keys):
        toks = []
        for s in src_keys:
            if self.last_w.get(s) is not None:
                toks.append(self.last_w[s])
            toks.extend(self.readers.get(s, ()))
        for d in dst_keys:
            self.readers.setdefault(d, []).extend(toks)

    def wait_all(self, eng, keys):
        waits = self._filter(eng, self._deps(eng, keys, keys))
        self.ops[eng].append(_Op(None, waits, None, 0))

    def emit(self):
        nc = self.nc
        with ExitStack() as es:
            sems = {}
            for k in self.semkeys:
                nm = k if isinstance(k, str) else "d_%s_%d" % (k[1], k[2])
                sems[k] = es.enter_context(nc.semaphore("s_" + nm))
            block = es.enter_context(nc.Block())

            def run(eng_name):
                def body(e):
                    for o in self.ops[eng_name]:
                        for (k, v) in o.waits:
                            e.wait_ge(sems[k], v)
                        if o.fn is not None:
                            o.fn(e).then_inc(sems[o.semkey], o.incval)
                return body

            block.tensor(run("tensor"))
            block.vector(run("vector"))
            block.scalar(run("scalar"))
            block.gpsimd(run("gpsimd"))
            block.sync(run("sync"))


class _Stop(Exception):
    pass


_KSTOP = 99


def _ck(n):
    if _KSTOP <= n:
        raise _Stop()


def build_nc(TPC):
    NT = TPC // T
    NB = TPC // 128
    S = NR * TPC
    KG = T
    NKG = NR * NT
    KGC = KG // 128

    nc = bass.Bass("TRN2", target_bir_lowering=False)

    def din(name, shape, dt=F32):
        return nc.dram_tensor(name, list(shape), dt, kind="ExternalInput").ap()

    x_d = din("x", [TPC, D])
    wg_d = [din("ffn1_w_gate", [D, DFF]), din("ffn2_w_gate", [D, DFF])]
    wu_d = [din("ffn1_w_up", [D, DFF]), din("ffn2_w_up", [D, DFF])]
    wd_d = [din("ffn1_w_down", [DFF, D]), din("ffn2_w_down", [DFF, D])]
    gpre_d = [din("ffn1_pre_g", [1, D]), din("ffn2_pre_g", [1, D])]
    gpost_d = [din("ffn1_post_g", [1, D]), din("ffn2_post_g", [1, D])]
    gmixpre_d = din("mix_pre_g", [1, D])
    gmixpost_d = din("mix_post_g", [1, D])
    win_d = din("w_in", [D, WIN_COLS])
    wpa_d = din("w_proj_a", [1024, D])
    wpb_d = din("w_proj_b", [1024, D])
    wout_d = din("w_out", [D, D])
    gbias_d = din("gate_biasT", [128, 32])
    sink_d = din("sink_bc", [128, 8])
    lamv_d = din("lamv", [128, 4])
    subg_d = din("sublnT", [128, 2])
    ropeC_d = din("ropeC", [128, TPC])
    ropeS_d = din("ropeS", [128, TPC])
    amask_d = din("amask", [128, 8, 128], BF16)
    ident_d = din("ident", [128, 128], BF16)
    y_d = nc.dram_tensor("y", [TPC, D], F32, kind="ExternalOutput").ap()

    qsp_d = nc.dram_tensor("q_sp", [16 * 128, TPC], BF16).ap()
    x1sp_d = nc.dram_tensor("x1_sp", [TPC, D], F32).ap()
    def dint(name, shape):
        return nc.dram_tensor(name, list(shape), BF16).ap()
    kTA_l = [dint("kTA_l%d" % i, [768, T]) for i in range(NT)]
    kTB_l = [dint("kTB_l%d" % i, [512, T]) for i in range(NT)]
    vA_l = [dint("vA_l%d" % i, [T, 768]) for i in range(NT)]
    vB_l = [dint("vB_l%d" % i, [T, 512]) for i in range(NT)]
    kTA_g = [dint("kTA_g%d" % i, [NR * 768, T]) for i in range(NT)]
    kTB_g = [dint("kTB_g%d" % i, [NR * 512, T]) for i in range(NT)]
    vA_g = [dint("vA_g%d" % i, [NR * T, 768]) for i in range(NT)]
    vB_g = [dint("vB_g%d" % i, [NR * T, 512]) for i in range(NT)]
    p = Prog(nc)
    es = ExitStack()
    with es:
        def sb(name, shape, dt):
            return es.enter_context(nc.sbuf_tensor(name, list(shape), dt))

        x_sb = sb("x_sb", [128, 4, D], F32)
        regA = sb("regA", [128, 24576], BF16)
        regB = sb("regB", [128, 16384], BF16)
        wbuf = [sb("wbuf%d" % i, [128, 16, 256], BF16) for i in range(5)]
        wdbuf = [sb("wdbuf%d" % i, [128, 4, 512], BF16) for i in range(2)]
        gbuf = sb("gbuf", [128, D], F32)
        xsb = [sb("xsb%d" % i, [128, D], BF16) for i in range(2)]
        sgj = sb("sgj", [128, 1024], F32)
        ptr = sb("ptr", [128, 2048], BF16)
        ropeC = sb("ropeC_sb", [128, T], F32)
        ropeS = sb("ropeS_sb", [128, T], F32)
        esink = sb("esink", [128, 8, 128], F32)
        amask = sb("amask_sb", [128, 8, 128], BF16)
        ident = sb("ident_sb", [128, 128], BF16)
        ones_b = sb("ones_b", [128, 128], BF16)
        ones_f = sb("ones_f", [128, 128], F32)
        stage = [sb("stage%d" % i, [128, T], BF16) for i in range(2)]
        vstage = sb("vstage", [128, 4, 256], BF16)
        gbias = sb("gbias", [128, 32], F32)
        subg = sb("subg", [128, 2], F32)
        lamv = sb("lamv_sb", [128, 4], F32)
        small = sb("small", [128, 64], F32)
        PS = [es.enter_context(nc.psum_tensor("ps%d" % i, [128, 512], F32)) for i in range(8)]

        aT = regA[:, 0:JF * T].rearrange("p (j t) -> p j t", t=T)
        qT = regA[:, 0:8192].rearrange("p (c t) -> p c t", t=T)
        oT = regA[:, 8192:16384].rearrange("p (c t) -> p c t", t=T)
        kvb = [regA[:, 16384 + i * 4096:16384 + (i + 1) * 4096] for i in range(2)]
        fb_mix = regA[:, 0:16384].bitcast(F32).rearrange("p (t d) -> p t d", d=D)
        gtmp = regA[:, 0:8192].bitcast(F32).rearrange("p (i t) -> p i t", t=T)
        hT = regB[:, 0:8192].rearrange("p (k t) -> p k t", t=T)
        mT = regB[:, 8192:16384].rearrange("p (k t) -> p k t", t=T)
        fb_ffn = regB[:].bitcast(F32).rearrange("p (t d) -> p t d", d=D)
        atmp = regB[:, 8192:16384].bitcast(F32).rearrange("p (i t) -> p i t", t=T)
        sg = [sgj[:, 0:512], sgj[:, 512:1024]]
        junk = sgj[:].bitcast(BF16)
        pt = [ptr[:, i * 512:(i + 1) * 512] for i in range(4)]
        rtmp = [ptr[:, 0:1024].bitcast(F32), ptr[:, 1024:2048].bitcast(F32)]

        AT_KEYS = ["A0", "A1", "kv0", "kv1"]
        ss = small[:, 0:4]
        ms = small[:, 4:8]
        sd = small[:, 8:12]
        rstd = small[:, 12:16]
        prod = small[:, 16:18]
        elam = small[:, 18:20]
        neglam = small[:, 20:21]
        esk = small[:, 24:32]
        subg_s = small[:, 32:34]

        wv = lambda w: w.rearrange("(k p) n -> p k n", p=128)

        st = {"wb": 0, "wd": 0, "xs": 0, "sg": 0, "stg": 0, "pt": 0, "kv": 0, "psT": 0}

        wbs = nc.dram_tensor("wbs", [160, 128, 4096], BF16).ap()
        wds = nc.dram_tensor("wds", [112, 128, 2048], BF16).ap()
        scr = {"wb": {}, "wd": {}}

        def load_wbuf(src_ap, k0=0, k1=16, buf=None, uid=None):
            if buf is None:
                buf = st["wb"]
                st["wb"] = (buf + 1) % 5
            keys = [("wb", buf, 0)] if k1 <= 8 else ([("wb", buf, 8)] if k0 >= 8 else [("wb", buf, 0), ("wb", buf, 8)])
            nk = k1 - k0
            dstv = wbuf[buf][:, k0:k1, :]
            if uid in scr["wb"]:
                sc = wbs[scr["wb"][uid], :, 0:nk * 256].rearrange("p (k c) -> p k c", c=256)
                p.dma("gpsimd", lambda e: e.dma_start(out=dstv, in_=sc), reads=[("wbs", uid)], writes=keys)
            else:
                p.dma("gpsimd", lambda e: e.dma_start(out=dstv, in_=src_ap), writes=keys)
                idx = len(scr["wb"])
                scr["wb"][uid] = idx
                sc = wbs[idx, :, 0:nk * 256].rearrange("p (k c) -> p k c", c=256)
                p.dma("sync", lambda e: e.dma_start(out=sc, in_=dstv), reads=keys, writes=[("wbs", uid)])
            return buf

        def load_wd(src_ap, uid=None):
            buf = st["wd"]
            st["wd"] = (buf + 1) % 2
            if uid in scr["wd"]:
                sc = wds[scr["wd"][uid]].rearrange("p (k c) -> p k c", c=512)
                p.dma("gpsimd", lambda e: e.dma_start(out=wdbuf[buf][:], in_=sc), reads=[("wds", uid)], writes=[("wd", buf)])
            else:
                p.dma("gpsimd", lambda e: e.dma_start(out=wdbuf[buf][:], in_=src_ap), writes=[("wd", buf)])
                idx = len(scr["wd"])
                scr["wd"][uid] = idx
                sc = wds[idx].rearrange("p (k c) -> p k c", c=512)
                p.dma("sync", lambda e: e.dma_start(out=sc, in_=wdbuf[buf][:]), reads=[("wd", buf)], writes=[("wds", uid)])
            return buf

        def mm_group(out_ap, pairs, reads, writes):
            n = len(pairs)

            def f(e):
                for i, (l, r) in enumerate(pairs):
                    ins = e.matmul(out_ap, lhsT=l, rhs=r, start=(i == 0), stop=(i == n - 1))
                return ins
            p.op("tensor", f, reads=reads, writes=writes)

        for dst, src, key in ((amask[:], amask_d, "amask"), (ident[:], ident_d, "ident"), (gbias[:], gbias_d, "gbias"),
                              (subg[:], subg_d, "subg"), (lamv[:], lamv_d, "lamv"), (esk, sink_d, "esk")):
            p.dma("sync", lambda e, dst=dst, src=src: e.dma_start(out=dst, in_=src), writes=[key])
        p.op("vector", lambda e: e.memset(ones_b[:], 1.0), writes=["ones_b"])
        p.op("vector", lambda e: e.memset(ones_f[:], 1.0), writes=["ones_f"])
        p.op("scalar", lambda e: e.activation(out=esk, in_=esk, func=AF.Exp), reads=["esk"], writes=["esk"])
        p.op("vector", lambda e: e.tensor_copy(out=esink[:], in_=esk.unsqueeze(2).broadcast_to([128, 8, 128])),
             reads=["esk"], writes=["esink"])
        p.op("vector", lambda e: e.tensor_tensor(out=prod, in0=lamv[:, 0:2], in1=lamv[:, 2:4], op=ALU.mult),
             reads=["lamv"], writes=["prod"])
        mm_group(PS[0][:, 0:2], [(ones_f[:], prod)], reads=["ones_f", "prod"], writes=[("ps", 0)])
        p.op("scalar", lambda e: e.activation(out=elam, in_=PS[0][:, 0:2], func=AF.Exp),
             reads=[("ps", 0)], writes=["elam"])
        p.op("vector", lambda e: e.tensor_tensor(out=neglam, in0=elam[:, 1:2], in1=elam[:, 0:1], op=ALU.subtract),
             reads=["elam"], writes=["neglam"])
        p.op("vector", lambda e: e.tensor_scalar(out=neglam, in0=neglam, scalar1=-LAM_INIT, scalar2=None, op0=ALU.add),
             reads=["neglam"], writes=["neglam"])
        p.op("vector", lambda e: e.tensor_scalar(out=subg_s, in0=subg[:], scalar1=1.0 - LAM_INIT, scalar2=None, op0=ALU.mult),
             reads=["subg"], writes=["subg_s"])

        def load_gain(g_d):
            p.dma("sync", lambda e: e.dma_start(out=gbuf[:], in_=g_d.broadcast_to([128, D])), writes=["gbuf"])

        def rows_rstd(srcs, read_keys):
            for t in range(4):
                p.op("scalar", lambda e, t=t: e.activation(out=junk, in_=srcs[t], func=AF.Square, accum_out=ss[:, t:t + 1]),
                     reads=[read_keys[t]], writes=[("ss", t), "sg0", "sg1"])
            p.op("vector", lambda e: e.tensor_scalar(out=ms, in0=ss, scalar1=1.0 / D, scalar2=EPS, op0=ALU.mult, op1=ALU.add),
                 reads=[("ss", t) for t in range(4)], writes=["ms"])
            p.op("scalar", lambda e: e.activation(out=sd, in_=ms, func=AF.Sqrt), reads=["ms"], writes=["sd"])
            p.op("vector", lambda e: e.reciprocal(out=rstd, in_=sd), reads=["sd"], writes=["rstd"])

        def norm_to_hT(g_d):
            load_gain(g_d)
            rows_rstd([x_sb[:, t, :] for t in range(4)], [("x", t) for t in range(4)])
            for t in range(4):
                xi = st["xs"]
                st["xs"] = 1 - xi
                p.op("vector", lambda e, t=t, xi=xi: e.scalar_tensor_tensor(
                    out=xsb[xi][:], in0=x_sb[:, t, :], scalar=rstd[:, t:t + 1], in1=gbuf[:], op0=ALU.mult, op1=ALU.mult),
                    reads=[("x", t), "rstd", "gbuf"], writes=[("xsb", xi)])
                for half in range(2):
                    b = 4 + st["psT"]
                    st["psT"] = (st["psT"] + 1) % 4
                    psv = PS[b][:].bitcast(BF16)

                    def tr(e, xi=xi, half=half, psv=psv):
                        for kk in range(8):
                            k = half * 8 + kk
                            ins = e.transpose(psv[:, kk * 128:(kk + 1) * 128], xsb[xi][:, k * 128:(k + 1) * 128], ident[:])
                        return ins
                    p.op("tensor", tr, reads=[("xsb", xi), "ident"], writes=[("ps", b)])
                    src = psv.rearrange("p (k t) -> p k t", t=128)
                    dst = hT[:, half * 8:(half + 1) * 8, t * 128:(t + 1) * 128]
                    p.op("scalar", lambda e, src=src, dst=dst: e.activation(out=dst, in_=src, func=AF.Copy),
                         reads=[("ps", b)], writes=["hT"])

        def ffn_stage1(wg, wu, wname, hook=None):
            for j2 in range(JF // 2):
                if j2 == 2 and hook is not None:
                    hook()
                bg = load_wbuf(wv(wg)[:, :, j2 * 256:(j2 + 1) * 256], uid=("g", wname, j2))
                bu = load_wbuf(wv(wu)[:, :, j2 * 256:(j2 + 1) * 256], uid=("u", wname, j2))
                for jj in range(2):
                    j = j2 * 2 + jj
                    pg, pu = (0, 1) if j % 2 == 0 else (2, 3)
                    mm_group(PS[pg][:], [(wbuf[bg][:, k, jj * 128:(jj + 1) * 128], hT[:, k, :]) for k in range(KC)],
                             reads=[("wb", bg, 0), ("wb", bg, 8), "hT"], writes=[("ps", pg)])
                    mm_group(PS[pu][:], [(wbuf[bu][:, k, jj * 128:(jj + 1) * 128], hT[:, k, :]) for k in range(KC)],
                             reads=[("wb", bu, 0), ("wb", bu, 8), "hT"], writes=[("ps", pu)])
                    si = st["sg"]
                    st["sg"] = 1 - si
                    p.op("scalar", lambda e, pg=pg, si=si: e.activation(out=sg[si], in_=PS[pg][:], func=AF.Silu),
                         reads=[("ps", pg)], writes=["sg%d" % si])
                    p.op("vector", lambda e, pu=pu, si=si, j=j: e.tensor_tensor(out=aT[:, j, :], in0=sg[si], in1=PS[pu][:], op=ALU.mult),
                         reads=[("ps", pu), "sg%d" % si], writes=[("aT", j)])

        def down_proj(src, src_key, w_d, nk, fb, fb_keys, wname):
            for n in range(4):
                banks = (4, 5, 6, 7) if n % 2 == 0 else (0, 1, 2, 3)
                ngr = nk // 4
                for kg in range(ngr):
                    b = load_wd(wv(w_d)[:, kg * 4:(kg + 1) * 4, n * 512:(n + 1) * 512], uid=(wname, n, kg))

                    def f(e, kg=kg, b=b, banks=banks, ngr=ngr):
                        for t in range(4):
                            for k in range(4):
                                ins = e.matmul(PS[banks[t]][:], lhsT=src[:, kg * 4 + k, t * 128:(t + 1) * 128], rhs=wdbuf[b][:, k, :],
                                               start=(kg == 0 and k == 0), stop=(kg == ngr - 1 and k == 3))
                        return ins
                    p.op("tensor", f, reads=[("wd", b)] + [src_key(kg * 4 + k) for k in range(4)],
                         writes=[("ps", bk) for bk in banks])
                for t in range(4):
                    dst = fb[:, t, n * 512:(n + 1) * 512]
                    if t % 2 == 0:
                        p.op("vector", lambda e, dst=dst, bk=banks[t]: e.tensor_copy(out=dst, in_=PS[bk][:]),
                             reads=[("ps", banks[t])], writes=[fb_keys[t]])
                    else:
                        p.op("scalar", lambda e, dst=dst, bk=banks[t]: e.activation(out=dst, in_=PS[bk][:], func=AF.Copy),
                             reads=[("ps", banks[t])], writes=[fb_keys[t]])

        def post_norm_res(fb, fb_keys, g_d, factor):
            load_gain(g_d)
            rows_rstd([fb[:, t, :] for t in range(4)], fb_keys)
            for t in range(4):
                p.op("vector", lambda e, t=t: e.scalar_tensor_tensor(
                    out=fb[:, t, :], in0=fb[:, t, :], scalar=rstd[:, t:t + 1], in1=gbuf[:], op0=ALU.mult, op1=ALU.mult),
                    reads=[fb_keys[t], "rstd", "gbuf"], writes=[fb_keys[t]])
                p.op("vector", lambda e, t=t: e.scalar_tensor_tensor(
                    out=x_sb[:, t, :], in0=fb[:, t, :], scalar=float(factor), in1=x_sb[:, t, :], op0=ALU.mult, op1=ALU.add),
                    reads=[fb_keys[t], ("x", t)], writes=[("x", t)])

        FB_FFN_KEYS = ["hT", "hT", "mT", "mT"]
        FB_MIX_KEYS = ["A0", "A0", "A1", "A1"]

        def ffn(l, hook=None):
            norm_to_hT(gpre_d[l])
            p.alias(["A0", "A1", "kv0", "kv1"], [("aT", j) for j in range(JF)])
            ffn_stage1(wg_d[l], wu_d[l], l, hook)
            down_proj(aT, lambda j: ("aT", j), wd_d[l], JF, fb_ffn, FB_FFN_KEYS, ("d", l))
            p.alias([("aT", j) for j in range(JF)], ["A0", "A1", "kv0", "kv1"])
            post_norm_res(fb_ffn, FB_FFN_KEYS, gpost_d[l], 0.5)

        def load_x(src_d, i, rkey=None):
            for t in range(4):
                r0 = i * T + t * 128
                p.dma("sync", lambda e, t=t, r0=r0: e.dma_start(out=x_sb[:, t, :], in_=src_d[r0:r0 + 128, :]),
                      reads=[(rkey, i, t)] if rkey else [], writes=[("x", t)])

        def store_x(dst_d, i, key):
            for t in range(4):
                r0 = i * T + t * 128
                p.dma("sync", lambda e, t=t, r0=r0: e.dma_start(out=dst_d[r0:r0 + 128, :], in_=x_sb[:, t, :]),
                      reads=[("x", t)], writes=[(key, i, t)])

        def qkv(i):
            c0 = i * T
            p.dma("sync", lambda e: e.dma_start(out=ropeC[:], in_=ropeC_d[:, c0:c0 + T]), writes=["ropeC"])
            p.dma("sync", lambda e: e.dma_start(out=ropeS[:], in_=ropeS_d[:, c0:c0 + T]), writes=["ropeS"])
            fm = []
            for c in range(8):
                fm.append((c * 128, qsp_d[c * 128:(c + 1) * 128, c0:c0 + T], ("qsp", i)))
            for g in range(2):
                fm.append((1024 + g * 128, kTA_l[i][g * 128:(g + 1) * 128, :], ("kTA", i, g)))
            for c in range(8):
                fm.append((1536 + c * 128, qsp_d[(8 + c) * 128:(9 + c) * 128, c0:c0 + T], ("qsp", i)))
            for c in range(8):
                if c < 4:
                    fm.append((2560 + c * 128, kTA_l[i][(2 + c) * 128:(3 + c) * 128, :], ("kTA", i, 2 + c)))
                else:
                    fm.append((2560 + c * 128, kTB_l[i][(c - 4) * 128:(c - 3) * 128, :], ("kTB", i, c - 4)))
            for pr in range(len(fm) // 2):
                col0 = fm[2 * pr][0]
                b = load_wbuf(wv(win_d)[:, :, col0:col0 + 256], uid=("in", col0))
                for jj in range(2):
                    _, dst_d, dkey = fm[2 * pr + jj]
                    bk = (2 * pr + jj) % 4
                    mm_group(PS[bk][:], [(wbuf[b][:, k, jj * 128:(jj + 1) * 128], hT[:, k, :]) for k in range(KC)],
                             reads=[("wb", b, 0), ("wb", b, 8), "hT"], writes=[("ps", bk)])
                    p.op("vector", lambda e, bk=bk: e.tensor_tensor(out=rtmp[0], in0=PS[bk][:], in1=ropeC[:], op=ALU.mult),
                         reads=[("ps", bk), "ropeC"], writes=["pt0", "pt1"])
                    p.op("vector", lambda e, bk=bk: e.tensor_tensor(out=rtmp[1][0:64, :], in0=PS[bk][64:128, :], in1=ropeS[0:64, :], op=ALU.mult),
                         reads=[("ps", bk), "ropeS"], writes=["pt2"])
                    p.op("vector", lambda e, bk=bk: e.tensor_tensor(out=rtmp[1][64:128, :], in0=PS[bk][0:64, :], in1=ropeS[64:128, :], op=ALU.mult),
                         reads=[("ps", bk), "ropeS"], writes=["pt3"])
                    si = st["stg"]
                    st["stg"] = 1 - si
                    p.op("vector", lambda e, si=si: e.tensor_tensor(out=stage[si][:], in0=rtmp[0], in1=rtmp[1], op=ALU.add),
                         reads=["pt0", "pt1", "pt2", "pt3"], writes=[("stage", si)])
                    p.dma("sync", lambda e, si=si, dst_d=dst_d: e.dma_start(out=dst_d, in_=stage[si][:]),
                          reads=[("stage", si)], writes=[dkey if dkey[0] != "qsp" else ("qsp", i, 2 * pr + jj)])
            vs = [(1280, vA_l[i], 0, "vA"), (3584, vA_l[i], 256, "vA"), (3840, vA_l[i], 512, "vA"),
                  (4096, vB_l[i], 0, "vB"), (4352, vB_l[i], 256, "vB")]
            for (col0, vdst, vc0, vkey) in vs:
                b = load_wbuf(wv(win_d)[:, :, col0:col0 + 256], uid=("in", col0))
                for t in range(4):
                    bk = 4 + t
                    mm_group(PS[bk][:, 0:256],
                             [(hT[:, k, t * 128:(t + 1) * 128], wbuf[b][:, k, :]) for k in range(KC)],
                             reads=[("wb", b, 0), ("wb", b, 8), "hT"], writes=[("ps", bk)])
                    p.op("scalar", lambda e, bk=bk, t=t: e.activation(out=vstage[:, t, :], in_=PS[bk][:, 0:256], func=AF.Copy),
                         reads=[("ps", bk)], writes=["vstage"])
                dst = vdst[:, vc0:vc0 + 256].rearrange("(t p) c -> p t c", p=128)
                p.dma("sync", lambda e, dst=dst: e.dma_start(out=dst, in_=vstage[:]), reads=["vstage"], writes=[(vkey, i, vc0)])

        def run_all():
            groups = [list(range(g * NR, (g + 1) * NR)) for g in range(NCORES // NR)]
            kTA_keys = lambda i: [("kTA", i, c) for c in range(6)]
            kTB_keys = lambda i: [("kTB", i, c) for c in range(4)]
            vA_keys = lambda i: [("vA", i, c) for c in (0, 256, 512)]
            vB_keys = lambda i: [("vB", i, c) for c in (0, 256)]

            def exchange(i):
                for (src, dst, rk, wk) in ((kTA_l[i], kTA_g[i], kTA_keys(i), ("kTAg", i)), (kTB_l[i], kTB_g[i], kTB_keys(i), ("kTBg", i)),
                                           (vA_l[i], vA_g[i], vA_keys(i), ("vAg", i)), (vB_l[i], vB_g[i], vB_keys(i), ("vBg", i))):
                    p.cc(lambda e, src=src, dst=dst: e.collective_compute(
                        "AllGather", ALU.bypass, replica_groups=groups, ins=[src], outs=[dst]), rk, [wk], 4 * NT)

            _ck(1)
            for i in range(NT):
                load_x(x_d, i)
                norm_to_hT(gpre_d[0]) if _KSTOP == 2 else None
                _ck(2)
                ffn(0, hook=(lambda i=i: exchange(i - 1)) if i > 0 else None)
                _ck(3)
                store_x(x1sp_d, i, "x1sp")
                norm_to_hT(gmixpre_d)
                qkv(i)
            _ck(4)

            exchange(NT - 1)
            _ck(5)

            def next_pt():
                i = st["pt"]
                st["pt"] = (i + 1) % 4
                return i

            def attn_b(i):
                for h in range(4):
                    units = [(kg, comp, c) for kg in range(NKG) for comp in range(2) for c in range(KGC)]
                    U = len(units)
                    loaded = {}
                    ptidx = {}
                    acc_cnt = [0, 0]

                    def ensure_loaded(kg, h=h, loaded=loaded):
                        if kg in loaded:
                            return loaded[kg]
                        r, ti = kg // NT, kg % NT
                        kb = st["kv"]
                        st["kv"] = (kb + 1) % 4
                        base = kvb[kb // 2][:, (kb % 2) * 2048:(kb % 2 + 1) * 2048]
                        kbuf = base[:, 0:2 * KG].rearrange("p (c k) -> p c k", k=KG)
                        vbuf = base[:, 2 * KG:2 * KG + KGC * 256].rearrange("p (c e) -> p c e", e=256)
                        if h < 2:
                            row0 = r * 768 + (2 + h * 2) * 128
                            ksrc = kTA_g[ti][row0:row0 + 256, :].rearrange("(c p) k -> p c k", p=128)
                            vsrc = vA_g[ti][r * T:(r + 1) * T, 256 + h * 256:512 + h * 256].rearrange("(c p) e -> p c e", p=128)
                            kkey, vkey = ("kTAg", ti), ("vAg", ti)
                        else:
                            row0 = r * 512 + (h - 2) * 256
                            ksrc = kTB_g[ti][row0:row0 + 256, :].rearrange("(c p) k -> p c k", p=128)
                            vsrc = vB_g[ti][r * T:(r + 1) * T, (h - 2) * 256:(h - 1) * 256].rearrange("(c p) e -> p c e", p=128)
                            kkey, vkey = ("kTBg", ti), ("vBg", ti)
                        p.dma("sync", lambda e, kbuf=kbuf, ksrc=ksrc: e.dma_start(out=kbuf, in_=ksrc),
                              reads=[kkey], writes=[("kbK", kb)])
                        p.dma("sync", lambda e, vbuf=vbuf, vsrc=vsrc: e.dma_start(out=vbuf, in_=vsrc),
                              reads=[vkey], writes=[("kbV", kb)])
                        loaded[kg] = (kb, kbuf, vbuf)
                        return loaded[kg]

                    SB = ((0, 1), (6, 7))

                    def sbank(u):
                        return SB[(u // 2) % 2][u % 2]

                    def Sp(k, h=h):
                        prs, rds, wrs = [], ["qT"], []
                        for u in (2 * k, 2 * k + 1):
                            kg, comp, c = units[u]
                            kb, kbuf, vbuf = ensure_loaded(kg)
                            prs.append((PS[sbank(u)][:], kbuf[:, comp, c * 128:(c + 1) * 128], qT[:, 8 + h * 2 + comp, :]))
                            rds.append(("kbK", kb))
                            wrs.append(("ps", sbank(u)))

                        def f(e, prs=prs):
                            for (o_, l_, r_) in prs:
                                ins = e.matmul(o_, lhsT=l_, rhs=r_, start=True, stop=True)
                            return ins
                        p.op("tensor", f, reads=rds, writes=wrs)

                    def Ep(k, ptidx=ptidx):
                        for u in (2 * k, 2 * k + 1):
                            sbk = sbank(u)
                            pi = next_pt()
                            ptidx[u] = pi
                            p.op("scalar", lambda e, sbk=sbk, pi=pi: e.activation(out=pt[pi], in_=PS[sbk][:], func=AF.Exp, scale=SCALE),
                                 reads=[("ps", sbk)], writes=["pt%d" % pi])

                    def PVp(k, ptidx=ptidx, acc_cnt=acc_cnt):
                        mms, rds, wrs = [], [], []
                        for u in (2 * k, 2 * k + 1):
                            kg, comp, c = units[u]
                            kb, kbuf, vbuf = ensure_loaded(kg)
                            pi = ptidx[u]
                            first = (kg == 0 and c == 0)
                            last = (kg == NKG - 1 and c == KGC - 1)
                            ob = 2 + comp * 2
                            mms.append((PS[ob][:], vbuf[:, c, 0:128], pt[pi], first, last))
                            mms.append((PS[ob + 1][:], vbuf[:, c, 128:256], pt[pi], first, last))
                            rds += [("kbV", kb), "pt%d" % pi]
                            wrs += [("ps", ob), ("ps", ob + 1)]

                        def f(e, mms=mms):
                            for (o_, l_, r_, fi, la) in mms:
                                ins = e.matmul(o_, lhsT=l_, rhs=r_, start=fi, stop=la)
                            return ins
                        p.op("tensor", f, reads=rds, writes=wrs)
                        for u in (2 * k, 2 * k + 1):
                            kg, comp, c = units[u]
                            pi = ptidx[u]
                            k_ = acc_cnt[comp]
                            acc_cnt[comp] += 1
                            on_pool = (k_ % 3 == 2)
                            eng = "gpsimd" if on_pool else "vector"
                            acc = atmp[:, 5 + comp, :] if on_pool else sg[comp]
                            akey = ("atmp", 5 + comp) if on_pool else "sg%d" % comp
                            if k_ == 0 or k_ == 2:
                                p.op(eng, lambda e, acc=acc, pi=pi: e.tensor_copy(out=acc, in_=pt[pi]),
                                     reads=["pt%d" % pi], writes=[akey])
                            else:
                                p.op(eng, lambda e, acc=acc, pi=pi: e.tensor_tensor(out=acc, in0=acc, in1=pt[pi], op=ALU.add),
                                     reads=["pt%d" % pi, akey], writes=[akey])

                    NPR = U // 2
                    Sp(0)
                    Sp(1)
                    Ep(0)
                    for k in range(NPR):
                        PVp(k)
                        if k + 2 < NPR:
                            Sp(k + 2)
                        if k + 1 < NPR:
                            Ep(k + 1)
                    r1, r2, ta, o0, o1, sq0, sq1, rn = [atmp[:, q, :] for q in range(8)]
                    K = lambda q: ("atmp", q)
                    mm_group(PS[6][:], [(ones_f[:], sg[0]), (ones_f[:], atmp[:, 5, :])], reads=["ones_f", "sg0", ("atmp", 5)], writes=[("ps", 6)])
                    mm_group(PS[7][:], [(ones_f[:], sg[1]), (ones_f[:], atmp[:, 6, :])], reads=["ones_f", "sg1", ("atmp", 6)], writes=[("ps", 7)])
                    p.op("vector", lambda e: e.reciprocal(out=r1, in_=PS[6][:]), reads=[("ps", 6)], writes=[K(0)])
                    p.op("vector", lambda e: e.reciprocal(out=r2, in_=PS[7][:]), reads=[("ps", 7)], writes=[K(1)])
                    p.op("vector", lambda e: e.tensor_scalar(out=r2, in0=r2, scalar1=neglam, scalar2=None, op0=ALU.mult),
                         reads=[K(1), "neglam"], writes=[K(1)])
                    for ec, oo, sq in ((0, o0, sq0), (1, o1, sq1)):
                        p.op("vector", lambda e, ec=ec: e.tensor_tensor(out=ta, in0=PS[2 + ec][:], in1=r1, op=ALU.mult),
                             reads=[("ps", 2 + ec), K(0)], writes=[K(2)])
                        p.op("vector", lambda e, ec=ec, oo=oo: e.tensor_tensor(out=oo, in0=PS[4 + ec][:], in1=r2, op=ALU.mult),
                             reads=[("ps", 4 + ec), K(1)], writes=[K(3 + ec)])
                        p.op("vector", lambda e, oo=oo: e.tensor_tensor(out=oo, in0=oo, in1=ta, op=ALU.add),
                             reads=[K(2), K(3 + ec)], writes=[K(3 + ec)])
                        p.op("scalar", lambda e, oo=oo, sq=sq: e.activation(out=sq, in_=oo, func=AF.Square),
                             reads=[K(3 + ec)], writes=[K(5 + ec)])
                    mm_group(PS[0][:], [(ones_f[:], sq0), (ones_f[:], sq1)], reads=["ones_f", K(5), K(6)], writes=[("ps", 0)])
                    p.op("vector", lambda e: e.tensor_scalar(out=rn, in0=PS[0][:], scalar1=1.0 / 256.0, scalar2=EPS, op0=ALU.mult, op1=ALU.add),
                         reads=[("ps", 0)], writes=[K(7)])
                    p.op("scalar", lambda e: e.activation(out=rn, in_=rn, func=AF.Sqrt), reads=[K(7)], writes=[K(7)])
                    p.op("vector", lambda e: e.reciprocal(out=rn, in_=rn), reads=[K(7)], writes=[K(7)])
                    for ec, oo in ((0, o0), (1, o1)):
                        p.op("vector", lambda e, ec=ec, oo=oo, h=h: e.scalar_tensor_tensor(
                            out=oT[:, 8 + h * 2 + ec, :], in0=oo, scalar=subg_s[:, ec:ec + 1], in1=rn, op0=ALU.mult, op1=ALU.mult),
                            reads=[K(3 + ec), K(7), "subg_s"], writes=[("oT", 8 + h * 2 + ec)])

            def attn_a(i):
                n0 = i * 4
                lo = max(n0 - 1, 0)
                hi = min(n0 + 4, NB - 1)
                nblk = hi - lo + 1
                kab = kvb[0][:, 0:2 * 768].rearrange("p (g k) -> p g k", k=768)
                vab = kvb[0][:, 1536:1536 + 6 * 256].rearrange("p (b e) -> p b e", e=256)
                kcb = kvb[1][:, 0:2 * 768].rearrange("p (g k) -> p g k", k=768)
                vcb = kvb[1][:, 1536:1536 + 6 * 256].rearrange("p (b e) -> p b e", e=256)
                for m in range(lo, hi + 1):
                    ti, bi = m // 4, m % 4
                    p.dma("sync", lambda e, m=m, ti=ti, bi=bi: e.dma_start(
                        out=kab[:, :, (m - lo) * 128:(m - lo + 1) * 128],
                        in_=kTA_l[ti][0:256, bi * 128:(bi + 1) * 128].rearrange("(g p) k -> p g k", p=128)),
                        reads=kTA_keys(ti), writes=[("kvK", 0), ("kvV", 0)])
                    p.dma("sync", lambda e, m=m, ti=ti, bi=bi: e.dma_start(
                        out=vab[:, m - lo, :], in_=vA_l[ti][bi * 128:(bi + 1) * 128, 0:256]),
                        reads=vA_keys(ti), writes=[("kvK", 0), ("kvV", 0)])
                cands = []
                if i == 0:
                    cands += [(s_, r, "prev") for s_, r in enumerate((0, 1, 2))]
                if i == NT - 1:
                    cands += [(3 + s_, r, "next") for s_, r in enumerate((1, 2, 3))]
                for (slot, r, which) in cands:
                    ti = NT - 1 if which == "prev" else 0
                    col0 = T - 128 if which == "prev" else 0
                    p.dma("sync", lambda e, slot=slot, r=r, col0=col0, ti=ti: e.dma_start(
                        out=kcb[:, :, slot * 128:(slot + 1) * 128],
                        in_=kTA_g[ti][r * 768:r * 768 + 256, col0:col0 + 128].rearrange("(g p) k -> p g k", p=128)),
                        reads=[("kTAg", ti)], writes=[("kvK", 1), ("kvV", 1)])
                    p.dma("sync", lambda e, slot=slot, r=r, col0=col0, ti=ti: e.dma_start(
                        out=vcb[:, slot, :], in_=vA_g[ti][r * T + col0:r * T + col0 + 128, 0:256]),
                        reads=[("vAg", ti)], writes=[("kvK", 1), ("kvV", 1)])
                units = []
                for nb in range(4):
                    n = n0 + nb
                    for g in range(2):
                        chunks = []
                        own = lambda m: (kab[:, g, (m - lo) * 128:(m - lo + 1) * 128], vab[:, m - lo, g * 128:(g + 1) * 128], 0)
                        cnd = lambda slot: (kcb[:, g, slot * 128:(slot + 1) * 128], vcb[:, slot, g * 128:(g + 1) * 128], 1)
                        if n == 0:
                            for s_ in range(3):
                                chunks.append(cnd(s_) + (2 + s_,))
                        else:
                            chunks.append(own(n - 1) + (0,))
                        chunks.append(own(n) + (None,))
                        if n == NB - 1:
                            for s_ in range(3):
                                chunks.append(cnd(3 + s_) + (5 + s_,))
                        else:
                            chunks.append(own(n + 1) + (1,))
                        qv = qT[:, g * 4:(g + 1) * 4, nb * 128:(nb + 1) * 128]
                        idx = nb * 2 + g
                        for ci, (kap, vap, which, mi) in enumerate(chunks):
                            units.append(dict(kap=kap, vap=vap, which=which, mi=mi, qv=qv, ob=2 + idx % 2, sb=4 + idx % 2,
                                              first=(ci == 0), last=(ci == len(chunks) - 1), nb=nb, g=g, idx=idx))
                U = len(units)
                ptidx = {}

                def S_(u):
                    d = units[u]
                    sbk = u % 2
                    mm_group(PS[sbk][:], [(d["kap"], d["qv"])], reads=[("kvK", d["which"]), ("kvV", d["which"]), "qT"], writes=[("ps", sbk)])

                def E_(u):
                    d = units[u]
                    sbk = u % 2
                    pi = next_pt()
                    ptidx[u] = pi
                    p.op("scalar", lambda e, sbk=sbk, pi=pi: e.activation(out=pt[pi], in_=PS[sbk][:], func=AF.Exp, scale=SCALE),
                         reads=[("ps", sbk)], writes=["pt%d" % pi])
                    if d["mi"] is not None:
                        mi = d["mi"]
                        ptv = pt[pi].rearrange("p (r q) -> p r q", q=128)
                        mv = amask[:, mi:mi + 1, :].broadcast_to([128, 4, 128])
                        p.op("vector", lambda e, ptv=ptv, mv=mv: e.tensor_tensor(out=ptv, in0=ptv, in1=mv, op=ALU.mult),
                             reads=["pt%d" % pi, "amask"], writes=["pt%d" % pi])

                def PV_(u):
                    d = units[u]
                    pi = ptidx[u]
                    ob, sb_, g, nb = d["ob"], d["sb"], d["g"], d["nb"]

                    def f(e, vap=d["vap"], pi=pi, ob=ob, sb_=sb_, first=d["first"], last=d["last"]):
                        e.matmul(PS[ob][:], lhsT=vap, rhs=pt[pi], start=first, stop=last)
                        return e.matmul(PS[sb_][:], lhsT=ones_b[:], rhs=pt[pi], start=first, stop=last)
                    p.op("tensor", f, reads=[("kvK", d["which"]), ("kvV", d["which"]), "pt%d" % pi, "ones_b"], writes=[("ps", ob), ("ps", sb_)])
                    if d["last"]:
                        den = atmp[:, d["idx"] % 2, :]
                        dk = ("atmp", d["idx"] % 2)
                        p.op("vector", lambda e, den=den, sb_=sb_, g=g: e.tensor_tensor(
                            out=den, in0=PS[sb_][:], in1=esink[:, g * 4:(g + 1) * 4, :].rearrange("p r q -> p (r q)"), op=ALU.add),
                            reads=[("ps", sb_), "esink"], writes=[dk])
                        p.op("vector", lambda e, den=den: e.reciprocal(out=den, in_=den), reads=[dk], writes=[dk])
                        dst = oT[:, g * 4:(g + 1) * 4, nb * 128:(nb + 1) * 128]
                        p.op("vector", lambda e, dst=dst, ob=ob, den=den: e.tensor_tensor(
                            out=dst, in0=PS[ob][:].rearrange("p (r q) -> p r q", q=128), in1=den.rearrange("p (r q) -> p r q", q=128), op=ALU.mult),
                            reads=[("ps", ob), dk], writes=[("oT", g * 4 + r_) for r_ in range(4)])

                S_(0)
                S_(1)
                E_(0)
                for u in range(U):
                    PV_(u)
                    if u + 2 < U:
                        S_(u + 2)
                    if u + 1 < U:
                        E_(u + 1)

            def gates_merge(i):
                p.alias(["qT"], [("gt", q) for q in range(8)])
                oT_keys = [("oT", c) for c in range(16)]
                for j2 in range(8):
                    ba = load_wbuf(wv(win_d)[:, :, 4608 + j2 * 256:4608 + (j2 + 1) * 256], uid=("in", 4608 + j2 * 256))
                    bb = load_wbuf(wv(win_d)[:, :, 6656 + j2 * 256:6656 + (j2 + 1) * 256], uid=("in", 6656 + j2 * 256))
                    bp = load_wbuf(wv(wpa_d)[:, :, j2 * 256:(j2 + 1) * 256], 0, 8, uid=("pa", j2))
                    load_wbuf(wv(wpb_d)[:, :, j2 * 256:(j2 + 1) * 256], 8, 16, buf=bp, uid=("pb", j2))
                    for jj in range(2):
                        j = j2 * 2 + jj
                        bs = (0, 1, 2, 3) if j % 2 == 0 else (4, 5, 6, 7)
                        cs = slice(jj * 128, (jj + 1) * 128)
                        mm_group(PS[bs[0]][:], [(wbuf[ba][:, k, cs], hT[:, k, :]) for k in range(KC)],
                                 reads=[("wb", ba, 0), ("wb", ba, 8), "hT"], writes=[("ps", bs[0])])
                        mm_group(PS[bs[1]][:], [(wbuf[bb][:, k, cs], hT[:, k, :]) for k in range(KC)],
                                 reads=[("wb", bb, 0), ("wb", bb, 8), "hT"], writes=[("ps", bs[1])])
                        mm_group(PS[bs[2]][:], [(wbuf[bp][:, k, cs], oT[:, k, :]) for k in range(8)],
                                 reads=[("wb", bp, 0)] + oT_keys, writes=[("ps", bs[2])])
                        mm_group(PS[bs[3]][:], [(wbuf[bp][:, 8 + k, cs], oT[:, 8 + k, :]) for k in range(8)],
                                 reads=[("wb", bp, 8)] + oT_keys, writes=[("ps", bs[3])])
                        q0 = (j % 2) * 4
                        ga, gb_, ta, tb = [gtmp[:, q0 + q, :] for q in range(4)]
                        GK = lambda q: ("gt", q0 + q)
                        p.op("scalar", lambda e, ga=ga, b0=bs[0], j=j: e.activation(out=ga, in_=PS[b0][:], func=AF.Sigmoid, bias=gbias[:, j:j + 1]),
                             reads=[("ps", bs[0]), "gbias"], writes=[GK(0)])
                        p.op("scalar", lambda e, gb_=gb_, b1=bs[1], j=j: e.activation(out=gb_, in_=PS[b1][:], func=AF.Sigmoid, bias=gbias[:, 16 + j:17 + j]),
                             reads=[("ps", bs[1]), "gbias"], writes=[GK(1)])
                        p.op("vector", lambda e, ga=ga, ta=ta, b2=bs[2]: e.tensor_tensor(out=ta, in0=PS[b2][:], in1=ga, op=ALU.mult),
                             reads=[("ps", bs[2]), GK(0)], writes=[GK(2)])
                        p.op("vector", lambda e, gb_=gb_, tb=tb, b3=bs[3]: e.tensor_tensor(out=tb, in0=PS[b3][:], in1=gb_, op=ALU.mult),
                             reads=[("ps", bs[3]), GK(1)], writes=[GK(3)])
                        p.op("vector", lambda e, ta=ta, tb=tb, j=j: e.tensor_tensor(out=mT[:, j, :], in0=ta, in1=tb, op=ALU.add),
                             reads=[GK(2), GK(3)], writes=[("mTc", j)])

            for i in range(NT):
                c0 = i * T
                load_x(x1sp_d, i, "x1sp")
                norm_to_hT(gmixpre_d)
                p.alias(["A0", "A1"], ["qT"] + [("oT", c) for c in range(16)])
                p.alias(["kv0", "kv1"], [("kbK", j) for j in range(4)] + [("kbV", j) for j in range(4)])
                p.alias(["mT"] + [("mTc", j) for j in range(16)], [("atmp", q) for q in range(8)])
                p.dma("sync", lambda e, c0=c0: e.dma_start(out=qT, in_=qsp_d[:, c0:c0 + T].rearrange("(c p) t -> p c t", p=128)),
                      reads=[("qsp", i, c) for c in range(26)], writes=["qT"])
                attn_b(i)
                p.alias([("kbK", j) for j in range(4)] + [("kbV", j) for j in range(4)], [("kvK", 0), ("kvV", 0), ("kvK", 1), ("kvV", 1)])
                _ck(6)
                attn_a(i)
                _ck(7)
                p.alias([("atmp", q) for q in range(8)], [("mTc", j) for j in range(16)])
                gates_merge(i)
                _ck(8)
                p.alias(["qT"] + [("gt", q) for q in range(8)] + [("oT", c) for c in range(16)], ["A0", "A1"])
                p.alias([("kvK", 0), ("kvV", 0), ("kvK", 1), ("kvV", 1)] + [("kbK", j) for j in range(4)] + [("kbV", j) for j in range(4)], ["kv0", "kv1"])
                down_proj(mT, lambda k: ("mTc", k), wout_d, KC, fb_mix, FB_MIX_KEYS, "wout")
                post_norm_res(fb_mix, FB_MIX_KEYS, gmixpost_d, 1.0)
                p.alias([("mTc", j) for j in range(16)], ["mT"])
                ffn(1)
                store_x(y_d, i, "y")

        try:
            run_all()
        except _Stop:
            pass
        p.wait_all("sync", list(p.last_w.keys()))
        p.emit()
    return nc


def _host_consts(TPC, rank):
    pos = (rank * TPC + np.arange(TPC)).astype(np.float32)
    inv = (np.float32(10000.0) ** (-np.arange(0, HD, 2, dtype=np.float32) / np.float32(HD))).astype(np.float32)
    ang = (pos[None, :] * inv[:, None]).astype(np.float32)
    c = np.cos(ang.astype(np.float64)).astype(np.float32)
    s = np.sin(ang.astype(np.float64)).astype(np.float32)
    ropeC = np.concatenate([c, c], 0)
    ropeS = np.concatenate([-s, s], 0)
    j = np.arange(128)[:, None]
    i = np.arange(128)[None, :]
    tri_prev = (j >= i).astype(np.float32)
    tri_next = (j <= i).astype(np.float32)
    am = np.zeros((128, 8, 128), np.float32)
    am[:, 0] = tri_prev
    am[:, 1] = tri_next
    for s_, r in enumerate((0, 1, 2)):
        if r == rank - 1:
            am[:, 2 + s_] = tri_prev
    for s_, r in enumerate((1, 2, 3)):
        if r == rank + 1:
            am[:, 5 + s_] = tri_next
    return ropeC, ropeS, am.astype(ml_dtypes.bfloat16), np.eye(128, dtype=np.float32).astype(ml_dtypes.bfloat16)


def make_in_maps(inputs, TPC):
    x = np.asarray(inputs["x"], np.float32)
    xf = x.reshape(-1, D)
    f = lambda k: np.ascontiguousarray(np.asarray(inputs[k], np.float32)[0])
    shared = {k: f(k) for k in ("ffn1_w_gate", "ffn1_w_up", "ffn1_w_down", "ffn2_w_gate", "ffn2_w_up", "ffn2_w_down",
                                 "w_in", "w_proj_a", "w_proj_b", "w_out")}
    for k in ("ffn1_pre_g", "ffn1_post_g", "ffn2_pre_g", "ffn2_post_g", "mix_pre_g", "mix_post_g"):
        shared[k] = np.ascontiguousarray(np.asarray(inputs[k], np.float32).reshape(1, D))
    shared["gate_biasT"] = np.ascontiguousarray(f("gate_bias").reshape(32, 128).T)
    shared["sink_bc"] = np.ascontiguousarray(np.broadcast_to(f("sink_logit").reshape(1, 8), (128, 8)))
    shared["lamv"] = np.ascontiguousarray(np.stack([f("lambda_q1"), f("lambda_q2"), f("lambda_k1"), f("lambda_k2")], 1))
    shared["sublnT"] = np.ascontiguousarray(f("subln_g").reshape(2, 128).T)
    in_maps = []
    for c in range(NCORES):
        rank = c % NR
        ropeC, ropeS, am, ident = _host_consts(TPC, rank)
        m = dict(shared)
        m["x"] = np.ascontiguousarray(xf[c * TPC:(c + 1) * TPC])
        m["ropeC"], m["ropeS"], m["amask"], m["ident"] = ropeC, ropeS, am, ident
        in_maps.append(m)
    return in_maps


_NC_CACHE = {}


def kernel(**inputs):
    x = np.asarray(inputs["x"])
    B, S_, _ = x.shape
    TPC = (B * S_) // NCORES
    if TPC not in _NC_CACHE:
        _NC_CACHE[TPC] = build_nc(TPC)
    nc = _NC_CACHE[TPC]
    in_maps = make_in_maps(inputs, TPC)
    res = run_bass_kernel_spmd(nc, in_maps, core_ids=list(range(NCORES)))
    y = np.concatenate([np.asarray(r["y"], np.float32) for r in res.results], 0)
    return y.reshape(B, S_, D)
```
